# Optimizing a Trainium2 kernel written in Bass

```python
import jax, jax.numpy as jnp
from jax import lax
import numpy as np

D_MODEL = 2048
BATCH = 1
SEQ = 8192
DEPTH = 1
DEC_BATCH = 1
DEC_SEQ = 16384
PAST_LEN = 128

GRID_W = 64
HEAD_DIM = 128
NA_HEADS = 8
NA_WIDTH = NA_HEADS * HEAD_DIM
NA_KH_MAX = 8
NA_KW = 16
MLA_HEADS = 8
Q_LORA_RANK = 512
KV_LORA_RANK = 512
QK_NOPE_DIM = 128
QK_ROPE_DIM = 64
V_HEAD_DIM = 128
MLA_QK_DIM = QK_NOPE_DIM + QK_ROPE_DIM
MLA_WIDTH = MLA_HEADS * V_HEAD_DIM
MIX_WIDTH = NA_WIDTH + MLA_WIDTH
IN_PROJ_WIDTH = 3 * NA_WIDTH + Q_LORA_RANK + KV_LORA_RANK + QK_ROPE_DIM
D_FF = -(-8 * D_MODEL // (3 * 256)) * 256
ROPE_THETA = 10000.0
RMS_EPS = 1e-6
Q_BLOCK = 128

kernel_name = "hybrid_natten_mla_encoder"


def _rmsnorm(x, g):
    xf = x.astype(jnp.float32)
    y = xf * lax.rsqrt(jnp.mean(xf * xf, axis=-1, keepdims=True) + RMS_EPS)
    return (y * g.astype(jnp.float32)).astype(x.dtype)


def _rotate_half(x):
    x1, x2 = jnp.split(x, 2, axis=-1)
    return jnp.concatenate([-x2, x1], axis=-1)


def _rope(x, seq_len):
    r = x.shape[-1]
    pos = jnp.arange(seq_len, dtype=jnp.float32)
    inv_freq = ROPE_THETA ** (-jnp.arange(0, r, 2, dtype=jnp.float32) / r)
    ang = pos[:, None] * inv_freq[None, :]
    ang = jnp.concatenate([ang, ang], axis=-1)[None, :, None, :]
    cos = jnp.cos(ang).astype(x.dtype)
    sin = jnp.sin(ang).astype(x.dtype)
    return x * cos + _rotate_half(x) * sin


def _neighbourhood_attention(q, k, v, rpb):
    B, S, H, dh = q.shape
    rows = S // GRID_W
    kh = min(NA_KH_MAX, rows)
    q = q.reshape(B, rows, GRID_W, H, dh)
    k = k.reshape(B, rows, GRID_W, H, dh)
    v = v.reshape(B, rows, GRID_W, H, dh)
    cols = jnp.arange(GRID_W)
    col_start = jnp.clip(cols - NA_KW // 2, 0, GRID_W - NA_KW)
    col_idx = col_start[:, None] + jnp.arange(NA_KW)[None, :]
    dc = col_idx - cols[:, None] + (NA_KW - 1)
    scale = HEAD_DIM ** -0.5

    def row_step(args):
        r, q_row = args
        rs = jnp.clip(r - kh // 2, 0, rows - kh)
        k_blk = lax.dynamic_slice_in_dim(k, rs, kh, axis=1)
        v_blk = lax.dynamic_slice_in_dim(v, rs, kh, axis=1)
        k_g = k_blk[:, :, col_idx]
        v_g = v_blk[:, :, col_idx]
        dr = rs + jnp.arange(kh) - r + (NA_KH_MAX - 1)
        bias = rpb[:, dr[None, :, None], dc[:, None, :]]
        s = jnp.einsum('bqhd,bkqwhd->bhqkw', q_row, k_g,
                       preferred_element_type=jnp.float32) * scale
        s = s + bias.astype(jnp.float32)[None]
        p = jax.nn.softmax(s.reshape(B, H, GRID_W, kh * NA_KW), axis=-1)
        p = p.reshape(B, H, GRID_W, kh, NA_KW).astype(v.dtype)
        return jnp.einsum('bhqkw,bkqwhd->bqhd', p, v_g)

    out = lax.map(row_step, (jnp.arange(rows), jnp.moveaxis(q, 1, 0)))
    return jnp.moveaxis(out, 0, 1).reshape(B, S, H * dh)


def _mla(q_c, kv_c, k_pe, g_q, w_uq, g_kv, w_ukv):
    B, S, _ = q_c.shape
    q = (_rmsnorm(q_c, g_q) @ w_uq).reshape(B, S, MLA_HEADS, MLA_QK_DIM)
    kv = (_rmsnorm(kv_c, g_kv) @ w_ukv).reshape(B, S, MLA_HEADS, QK_NOPE_DIM + V_HEAD_DIM)
    q_nope, q_pe = q[..., :QK_NOPE_DIM], q[..., QK_NOPE_DIM:]
    k_nope, v = kv[..., :QK_NOPE_DIM], kv[..., QK_NOPE_DIM:]
    q_pe = _rope(q_pe, S)
    k_pe = _rope(k_pe[:, :, None, :], S)
    q = jnp.concatenate([q_nope, q_pe], axis=-1)
    k = jnp.concatenate([k_nope, jnp.broadcast_to(k_pe, (B, S, MLA_HEADS, QK_ROPE_DIM))], axis=-1)
    scale = MLA_QK_DIM ** -0.5
    nblk = S // Q_BLOCK
    qb = jnp.moveaxis(q.reshape(B, nblk, Q_BLOCK, MLA_HEADS, MLA_QK_DIM), 1, 0)

    def blk(qi):
        s = jnp.einsum('bqhd,bkhd->bhqk', qi, k, preferred_element_type=jnp.float32) * scale
        p = jax.nn.softmax(s, axis=-1).astype(v.dtype)
        return jnp.einsum('bhqk,bkhd->bqhd', p, v)

    out = lax.map(blk, qb)
    return jnp.moveaxis(out, 0, 1).reshape(B, S, MLA_WIDTH)


def _encoder(x, c, w_ada, b_ada, g_attn, w_in, rpb, g_q, w_uq, g_kv, w_ukv,
             g_out_na, g_out_mla, w_o, g_ffn, w_gate, w_up, w_down, g_final):
    for l in range(DEPTH):
        mod = jax.nn.silu(c) @ w_ada[l] + b_ada[l]
        sh1, sc1, gt1, sh2, sc2, gt2 = [m[:, None, :] for m in jnp.split(mod, 6, axis=-1)]
        h = _rmsnorm(x, g_attn[l]) * (1 + sc1) + sh1
        proj = h @ w_in[l]
        B, S, _ = proj.shape
        o = 0
        q_na = proj[..., 0:NA_WIDTH]
        k_na = proj[..., NA_WIDTH:2 * NA_WIDTH]
        v_na = proj[..., 2 * NA_WIDTH:3 * NA_WIDTH]
        o = 3 * NA_WIDTH
        q_c = proj[..., o:o + Q_LORA_RANK]
        o += Q_LORA_RANK
        kv_c = proj[..., o:o + KV_LORA_RANK]
        o += KV_LORA_RANK
        k_pe = proj[..., o:o + QK_ROPE_DIM]
        hs = (B, S, NA_HEADS, HEAD_DIM)
        o_na = _neighbourhood_attention(q_na.reshape(hs), k_na.reshape(hs), v_na.reshape(hs), rpb[l])
        o_mla = _mla(q_c, kv_c, k_pe, g_q[l], w_uq[l], g_kv[l], w_ukv[l])
        merged = jnp.concatenate([_rmsnorm(o_na, g_out_na[l]), _rmsnorm(o_mla, g_out_mla[l])], axis=-1)
        x = x + gt1 * (merged @ w_o[l])
        h = _rmsnorm(x, g_ffn[l]) * (1 + sc2) + sh2
        f = (jax.nn.silu(h @ w_gate[l]) * (h @ w_up[l])) @ w_down[l]
        x = x + gt2 * f
    return _rmsnorm(x, g_final)


def setup_inputs(seed: int = 0) -> dict:
    key = jax.random.key(seed)
    ks = jax.random.split(key, 24)
    f32 = jnp.float32

    def nrm(k, shape, s):
        return jax.random.normal(k, shape, f32) * s

    def gain(k, shape):
        return 1.0 + 0.02 * jax.random.normal(k, shape, f32)

    L, D = DEPTH, D_MODEL
    return {
        "x_prompt": nrm(ks[0], (BATCH, SEQ, D), 1.0),
        "x_sample": nrm(ks[1], (DEC_BATCH, DEC_SEQ, D), 1.0),
        "c_prompt": nrm(ks[2], (BATCH, D), 1.0),
        "c_sample": nrm(ks[3], (DEC_BATCH, D), 1.0),
        "w_ada": nrm(ks[4], (L, D, 6 * D), 0.5 * D ** -0.5),
        "b_ada": nrm(ks[5], (L, 6 * D), 0.01),
        "g_attn": gain(ks[6], (L, D)),
        "w_in": nrm(ks[7], (L, D, IN_PROJ_WIDTH), D ** -0.5),
        "rpb": nrm(ks[8], (L, NA_HEADS, 2 * NA_KH_MAX - 1, 2 * NA_KW - 1), 0.1),
        "g_q": gain(ks[9], (L, Q_LORA_RANK)),
        "w_uq": nrm(ks[10], (L, Q_LORA_RANK, MLA_HEADS * MLA_QK_DIM), Q_LORA_RANK ** -0.5),
        "g_kv": gain(ks[11], (L, KV_LORA_RANK)),
        "w_ukv": nrm(ks[12], (L, KV_LORA_RANK, MLA_HEADS * (QK_NOPE_DIM + V_HEAD_DIM)), KV_LORA_RANK ** -0.5),
        "g_out_na": gain(ks[13], (L, NA_WIDTH)),
        "g_out_mla": gain(ks[14], (L, MLA_WIDTH)),
        "w_o": nrm(ks[15], (L, MIX_WIDTH, D), MIX_WIDTH ** -0.5),
        "g_ffn": gain(ks[16], (L, D)),
        "w_gate": nrm(ks[17], (L, D, D_FF), D ** -0.5),
        "w_up": nrm(ks[18], (L, D, D_FF), D ** -0.5),
        "w_down": nrm(ks[19], (L, D_FF, D), D_FF ** -0.5),
        "g_final": gain(ks[20], (D,)),
    }


def reference(x_prompt, x_sample, c_prompt, c_sample, w_ada, b_ada, g_attn, w_in, rpb,
              g_q, w_uq, g_kv, w_ukv, g_out_na, g_out_mla, w_o, g_ffn, w_gate, w_up,
              w_down, g_final):
    y_prompt = _encoder(x_prompt, c_prompt, w_ada, b_ada, g_attn, w_in, rpb, g_q, w_uq,
                        g_kv, w_ukv, g_out_na, g_out_mla, w_o, g_ffn, w_gate, w_up,
                        w_down, g_final)
    y_sample = _encoder(x_sample, c_sample, w_ada, b_ada, g_attn, w_in, rpb, g_q, w_uq,
                        g_kv, w_ukv, g_out_na, g_out_mla, w_o, g_ffn, w_gate, w_up,
                        w_down, g_final)
    return (y_prompt, y_sample)
```

```python
import numpy as np
import concourse.bass as bass
import concourse.mybir as mybir
from concourse.bass_utils import run_bass_kernel_spmd

F32 = mybir.dt.float32
BF16 = mybir.dt.bfloat16
AF = mybir.ActivationFunctionType
ALU = mybir.AluOpType

NCORES = 8
D = 2048
KC = 16
SP_, SS_ = 8192, 16384
OWNP, OWNS = 1024, 2048
HALO = 512
NAP, NAS = OWNP + 2 * HALO, OWNS + 2 * HALO
NBP, NBS = SP_ // 512, SS_ // 512
DFF = 5632
NFF = DFF // 128
EPS = 1e-6
NEG = -30000.0
SCK = 1024
NKV = 5
import os
SKIP = int(os.environ.get('KSKIP', '0'))
KPART = int(os.environ.get('KPART', '9'))

V_BADA, V_GATTN, V_GFFN, V_GFIN, V_GQ, V_GKV, V_GNA, V_GMLA, V_CP, V_CS, NV = 0, 96, 112, 128, 144, 148, 152, 160, 168, 184, 200


class Eng:
    def __init__(self, name, h, sem, compute=True):
        self.name, self.h, self.sem, self.cnt, self.seen, self.compute = name, h, sem, 0, {}, compute


class DSem:
    def __init__(self, sem):
        self.sem, self.cnt = sem, 0


class Buf:
    def __init__(self, name, space, lo, hi):
        self.name, self.space, self.lo, self.hi = name, space, lo, hi
        self.w = None
        self.r = {}
        self.ov = [self]


class Tile:
    def __init__(self, t, buf):
        self.t, self.buf = t, buf


class Sched:
    def __init__(self, nc):
        self.nc = nc
        self.bufs = []
        self.dsems = []
        mk = lambda n, h, c=True: Eng(n, h, nc.alloc_semaphore("sem_" + n), c)
        self.PE = mk("pe", nc.tensor)
        self.ACT = mk("act", nc.scalar)
        self.DVE = mk("dve", nc.vector)
        self.POOL = mk("pool", nc.gpsimd)
        self.SP = mk("sp", nc.sync, False)
        self.engs = [self.PE, self.ACT, self.DVE, self.POOL, self.SP]
        self.nsem = 5

    def dsem(self, name):
        d = DSem(self.nc.alloc_semaphore("ds_" + name))
        self.dsems.append(d)
        self.nsem += 1
        return d

    def _reg(self, b):
        for o in self.bufs:
            if o.space == b.space and o.lo < b.hi and b.lo < o.hi:
                o.ov.append(b)
                b.ov.append(o)
        self.bufs.append(b)
        return b

    def tile(self, name, shape, dtype, off):
        esz = 2 if dtype == BF16 else 4
        n = 1
        for s in shape[1:]:
            n *= s
        t = self.nc.alloc_sbuf_tensor_at(name, list(shape), dtype, offset=int(off))
        return Tile(t, self._reg(Buf(name, "sb", int(off), int(off) + n * esz)))

    def dram(self, name, shape, dtype, kind):
        t = self.nc.dram_tensor(name, list(shape), dtype, kind=kind).ap()
        return Tile(t, self._reg(Buf(name, "dram_" + name, 0, 1)))

    def _wait(self, eng, ev):
        sem, val, _ = ev
        k = id(sem)
        if eng.seen.get(k, 0) >= val:
            return
        eng.seen[k] = val
        eng.h.wait_ge(sem, val)

    def _deps(self, eng, reads, writes, is_dma):
        for t in reads:
            for o in t.buf.ov:
                if o.w is not None:
                    if o.w[2] is eng and not is_dma and eng is self.PE:
                        continue
                    self._wait(eng, o.w)
                if o.space == "ps":
                    for e in o.r.values():
                        if e[2] is not eng:
                            self._wait(eng, e)
        for t in writes:
            for o in t.buf.ov:
                if o.w is not None and (is_dma or o.w[2] is not eng):
                    self._wait(eng, o.w)
                for e in o.r.values():
                    if is_dma or e[2] is not eng:
                        self._wait(eng, e)

    def op(self, eng, fn, reads=(), writes=(), inc=True):
        self._deps(eng, reads, writes, False)
        ins = fn()
        if inc:
            eng.cnt += 1
            ins.then_inc(eng.sem, 1)
            ev = (eng.sem, eng.cnt, eng)
        else:
            ev = (eng.sem, eng.cnt + 1, eng)
        for t in reads:
            t.buf.r[eng.name] = ev
        for t in writes:
            t.buf.w = ev
            t.buf.r = {}
        return ev

    def dma(self, q, ds, out, in_, reads=(), writes=()):
        self._deps(q, reads, writes, True)
        ds.cnt += 16
        q.h.dma_start(out=out, in_=in_).then_inc(ds.sem, 16)
        ev = (ds.sem, ds.cnt, None)
        for t in reads:
            t.buf.r[("d", id(ds))] = ev
        for t in writes:
            t.buf.w = ev
            t.buf.r = {}
        return ev

    def barrier(self):
        for e in self.engs:
            for o in self.engs:
                if o is not e and o.cnt > 0:
                    self._wait(e, (o.sem, o.cnt, o))
            for d in self.dsems:
                if d.cnt > 0:
                    self._wait(e, (d.sem, d.cnt, None))


def _SL(w):
    K, N = w.shape
    return np.ascontiguousarray(w.reshape(K // 128, 128, N // 128, 128).transpose(2, 1, 0, 3))


def _ML(w):
    K, N = w.shape
    return np.ascontiguousarray(w.reshape(K // 128, 128, N).transpose(1, 0, 2))


def _fm(v):
    return np.ascontiguousarray(v.reshape(-1, 128).T)


def _rope_tables(pos):
    inv = (np.float32(10000.0) ** (-(np.arange(0, 64, 2, dtype=np.float32)) / np.float32(64))).astype(np.float32)
    ang = pos.astype(np.float32)[:, None] * inv[None, :]
    ang = np.concatenate([ang, ang], axis=-1).astype(np.float32)
    c = np.cos(ang).astype(np.float32).T
    s = np.sin(ang).astype(np.float32).T
    s = np.concatenate([-s[:32], s[32:]], axis=0)
    return np.concatenate([c, c], 0), np.concatenate([s, s], 0)


def _na_bias_variant(rpb, r, rows):
    i = np.arange(14)[:, None, None, None]
    kc = np.arange(64)[None, :, None, None]
    qq = np.arange(2)[None, None, :, None]
    qc = np.arange(64)[None, None, None, :]
    kr = r - 6 + i
    qr = r + qq
    rs = np.clip(qr - 4, 0, rows - 8)
    cs = np.clip(qc - 8, 0, 64 - 16)
    valid = (kr >= 0) & (kr < rows) & (kr >= rs) & (kr < rs + 8) & (kc >= cs) & (kc < cs + 16)
    dr = np.clip(kr - qr + 7, 0, 14)
    dc = np.clip(kc - qc + 15, 0, 30)
    dr, dc, valid = np.broadcast_arrays(dr, dc, valid)
    g = rpb[:, dr, dc]
    g = np.where(valid[None], g, np.float32(NEG)).astype(np.float32)
    g = g.reshape(8, 7, 128, 128)
    return np.ascontiguousarray(g.transpose(0, 2, 1, 3).reshape(8, 128, 896))


def _slot_of(i, n):
    return 1 if i == 0 else 2 if i == 1 else 3 if i == n - 2 else 4 if i == n - 1 else 0


def _prep(inp, cores=None):
    f32 = np.float32
    w_in = np.asarray(inp["w_in"][0], f32)
    w_uq = np.asarray(inp["w_uq"][0], f32)
    w_ukv = np.asarray(inp["w_ukv"][0], f32)
    perm = (np.arange(64) + 32) % 64
    kpe = w_in[:, 4096:4160]
    WA = np.concatenate([w_in[:, 3584:4096], kpe, kpe, kpe[:, perm], kpe[:, perm]], axis=1)
    uk = w_ukv.reshape(512, 8, 256)[:, :, :128].reshape(512, 1024)
    uv = w_ukv.reshape(512, 8, 256)[:, :, 128:].reshape(512, 1024)
    uq = w_uq.reshape(512, 8, 192)
    uqn = uq[:, :, :128].reshape(512, 1024)
    uqr = uq[:, :, 128:].reshape(512, 512)
    uqp = uq[:, :, 128:][:, :, perm].reshape(512, 512)
    WQ = np.concatenate([w_in[:, 0:1024], w_in[:, 3072:3584]], axis=1)
    WUQ = np.concatenate([uqn, uqr, uqp], axis=1)
    wg = _SL(np.asarray(inp["w_gate"][0], f32))
    wu = _SL(np.asarray(inp["w_up"][0], f32))
    WGU = np.ascontiguousarray(np.stack([wg, wu], axis=2))
    vecs = np.zeros((128, NV), f32)
    vecs[:, V_BADA:V_BADA + 96] = _fm(np.asarray(inp["b_ada"][0], f32))
    vecs[:, V_GATTN:V_GATTN + 16] = _fm(np.asarray(inp["g_attn"][0], f32))
    vecs[:, V_GFFN:V_GFFN + 16] = _fm(np.asarray(inp["g_ffn"][0], f32))
    vecs[:, V_GFIN:V_GFIN + 16] = _fm(np.asarray(inp["g_final"], f32))
    vecs[:, V_GQ:V_GQ + 4] = _fm(np.asarray(inp["g_q"][0], f32))
    vecs[:, V_GKV:V_GKV + 4] = _fm(np.asarray(inp["g_kv"][0], f32))
    vecs[:, V_GNA:V_GNA + 8] = _fm(np.asarray(inp["g_out_na"][0], f32))
    vecs[:, V_GMLA:V_GMLA + 8] = _fm(np.asarray(inp["g_out_mla"][0], f32))
    vecs[:, V_CP:V_CP + 16] = _fm(np.asarray(inp["c_prompt"][0], f32))
    vecs[:, V_CS:V_CS + 16] = _fm(np.asarray(inp["c_sample"][0], f32))
    shared = {
        "vecs": vecs,
        "w_ada": np.ascontiguousarray(np.asarray(inp["w_ada"][0], f32)),
        "ident": np.eye(128, dtype=f32),
        "WA": _SL(WA), "WUK": _SL(uk), "WUV": _ML(uv),
        "WKNA": _SL(w_in[:, 1024:2048]), "WVNA": _ML(w_in[:, 2048:3072]),
        "WQ": _SL(WQ), "WUQ": _SL(WUQ), "WO": _SL(np.asarray(inp["w_o"][0], f32)),
        "WGU": WGU, "WD": _SL(np.asarray(inp["w_down"][0], f32)),
    }
    rpb = np.asarray(inp["rpb"][0], f32)
    xp = np.asarray(inp["x_prompt"][0], f32)
    xs = np.asarray(inp["x_sample"][0], f32)
    variants = {}

    def variant(r, rows):
        key = ("t", r) if r < 4 else ("b", rows - r) if r >= rows - 4 else ("i",)
        if key not in variants:
            variants[key] = _na_bias_variant(rpb, r, rows)
        return variants[key]

    maps = []
    for c in (range(NCORES) if cores is None else cores):
        sp0 = (c * OWNP - HALO) % SP_
        ss0 = (c * OWNS - HALO) % SS_
        xall = np.concatenate([np.roll(xp, -sp0, axis=0), np.roll(xs, -ss0, axis=0)], axis=0)
        pos = np.concatenate([(sp0 + np.arange(SP_)) % SP_, (ss0 + np.arange(SS_)) % SS_])
        c2, s2 = _rope_tables(pos)
        tab = np.stack([c2, s2], axis=1).reshape(128, 2, NBP + NBS, 512).transpose(2, 0, 1, 3)
        nab = np.zeros((2, 5, 8, 128, 896), f32)
        for sg, (n, rows, row0) in enumerate(((OWNP // 128, SP_ // 64, c * (OWNP // 64)), (OWNS // 128, SS_ // 64, c * (OWNS // 64)))):
            tiles = {0: 2, 1: 0, 2: 1, 3: n - 2, 4: n - 1}
            for slot, i in tiles.items():
                nab[sg, slot] = variant(row0 + 2 * i, rows)
        m = dict(shared)
        m["xall"] = xall
        m["tabs"] = np.ascontiguousarray(tab)
        m["nab"] = nab
        maps.append(m)
    return maps


def build(stage=99, dbg=False):
    nc = bass.Bass("TRN2", target_bir_lowering=False)
    S = Sched(nc)
    PE, ACT, DVE, POOL, SP = S.PE, S.ACT, S.DVE, S.POOL, S.SP
    T, V, G, A = nc.tensor, nc.vector, nc.gpsimd, nc.scalar
    BASE = 16512
    KB = 1024

    din = lambda n, sh: S.dram(n, sh, F32, "ExternalInput")
    xall = din("xall", [SP_ + SS_, D])
    vecs_d = din("vecs", [128, NV])
    wada_d = din("w_ada", [D, 6 * D])
    ident_d = din("ident", [128, 128])
    tabs_d = din("tabs", [NBP + NBS, 128, 2, 512])
    nab_d = din("nab", [2, 5, 8, 128, 896])
    wshapes = {"WA": [6, 128, 16, 128], "WUK": [8, 128, 4, 128], "WUV": [128, 4, 1024],
               "WKNA": [8, 128, 16, 128], "WVNA": [128, 16, 1024], "WQ": [12, 128, 16, 128],
               "WUQ": [16, 128, 4, 128], "WO": [16, 128, 16, 128], "WGU": [NFF, 128, 2, 16, 128],
               "WD": [16, 128, NFF, 128]}
    w32 = {k: din(k, sh) for k, sh in wshapes.items()}
    wbf = {k: S.dram(k + "_bf", sh, BF16, "Internal") for k, sh in wshapes.items()}
    KT = [S.dram("KT%d" % s, [8, 128, L], BF16, "Internal") for s, L in enumerate((SP_, SS_))]
    KPE = [S.dram("KPE%d" % s, [128, L], BF16, "Internal") for s, L in enumerate((SP_, SS_))]
    VV = [S.dram("VV%d" % s, [8, 128, L // 128, 128], BF16, "Internal") for s, L in enumerate((SP_, SS_))]
    KNA = [S.dram("KNA%d" % s, [8, 128, L], BF16, "Internal") for s, L in enumerate((NAP, NAS))]
    VNA = [S.dram("VNA%d" % s, [L, 1024], BF16, "Internal") for s, L in enumerate((NAP, NAS))]
    yout = [S.dram("y%d" % s, [L, D], F32, "ExternalOutput") for s, L in enumerate((OWNP, OWNS))]
    dbg_out = {}

    def dbg_dump(name, tile_, shape, dtype=F32):
        if not dbg:
            return
        o = S.dram("dbg_" + name, shape, dtype, "ExternalOutput")
        dbg_out[name] = o
        S.dma(SP, S.dsem("dbg_" + name), o.t, tile_.t.ap() if isinstance(tile_, Tile) else tile_[0], reads=[tile_ if isinstance(tile_, Tile) else tile_[1]], writes=[o])

    def cast_weight(k):
        ds = S.dsem("cast_" + k)
        n = 1
        for s_ in wshapes[k]:
            n *= s_
        rows = n // 2048
        src = w32[k].t
        dst = wbf[k].t
        names = "abcde"[:len(wshapes[k])]
        pat = " ".join(names)
        src2 = src.rearrange(f"{pat} -> ({pat})").rearrange("(r c) -> r c", c=2048)
        dst2 = dst.rearrange(f"{pat} -> ({pat})").rearrange("(r c) -> r c", c=2048)
        step = 2048
        for r0 in range(0, rows, step):
            r1 = min(rows, r0 + step)
            S.dma(POOL, ds, dst2[r0:r1, :], src2[r0:r1, :], reads=[w32[k]], writes=[wbf[k]])

    for k in ("WA", "WUK", "WUV", "WKNA", "WVNA", "WQ", "WUQ", "WO", "WGU", "WD"):
        cast_weight(k)

    ps_all = nc.alloc_psum_tensor("ps_all", [128, 8, 512], F32)
    bank = [Tile(ps_all[:, i, :], S._reg(Buf("bank%d" % i, "ps", i, i + 1))) for i in range(8)]

    off = BASE
    ident = S.tile("ident", [128, 128], F32, off); off += 512
    ones = S.tile("ones", [128, 128], BF16, off); off += 256
    vecs = S.tile("vecs", [128, NV], F32, off); off += NV * 4
    mod = S.tile("mod", [128, 96, 2], F32, off); off += 768
    der = S.tile("der", [128, 2, 6, 16], F32, off); off += 768
    epsT = S.tile("epsT", [128, 1], F32, off); off += 32
    sc = S.tile("sc", [128, 16, 2], F32, off); off += 128
    gq2 = S.tile("gq2", [128, 8], F32, off); off += 32
    CONST_END = BASE + 6 * KB
    assert off <= CONST_END
    cds = S.dsem("const")
    S.dma(SP, cds, ident.t[:, :], ident_d.t[:, :], reads=[ident_d], writes=[ident])
    S.dma(SP, cds, vecs.t[:, :], vecs_d.t[:, :], reads=[vecs_d], writes=[vecs])
    S.op(DVE, lambda: V.memset(ones.t[:, :], 1.0), writes=[ones])
    S.op(DVE, lambda: V.memset(epsT.t[:, :], EPS), writes=[epsT])
    for s in range(2):
        c0 = V_CP if s == 0 else V_CS
        S.op(ACT, lambda s=s, c0=c0: A.activation(out=sc.t[:, :, s], in_=vecs.t[:, c0:c0 + 16], func=AF.Silu),
             reads=[vecs], writes=[sc])

    P0 = CONST_END
    wad = [S.tile("wad%d" % i, [128, 16, 512], F32, P0 + i * 32 * KB) for i in range(2)]
    wad_ds = [S.dsem("wad%d" % i) for i in range(2)]
    psmod = bank[7]
    psmod_v = ps_all[:, 7, 0:192].rearrange("p (n s) -> p n s", s=2)
    mod_groups_done = [0]

    def mod_phase(groups):
        for g in groups:
            i = mod_groups_done[0] % 2
            mod_groups_done[0] += 1
            S.dma(SP, wad_ds[i], wad[i].t[:, :, :],
                  wada_d.t[:, g * 512:(g + 1) * 512].rearrange("(kc p) n -> p kc n", p=128),
                  reads=[wada_d], writes=[wad[i]])
            for q in range(4):
                n = g * 4 + q
                for kc in range(16):
                    S.op(PE, lambda i=i, q=q, kc=kc, n=n: T.matmul(ps_all[:, 7, 2 * n:2 * n + 2], wad[i].t[:, kc, q * 128:(q + 1) * 128],
                                                                   sc.t[:, kc, :], start=(kc == 0), stop=(kc == 15)),
                         reads=[wad[i], sc] if kc == 0 else [], writes=[psmod] if kc == 0 else [], inc=(kc == 15))
        for s in range(2):
            for g in groups:
                S.op(DVE, lambda s=s, g=g: V.tensor_tensor(out=mod.t[:, g * 4:(g + 1) * 4, s], in0=psmod_v[:, g * 4:(g + 1) * 4, s],
                                                           in1=vecs.t[:, V_BADA + g * 4:V_BADA + (g + 1) * 4], op=ALU.add),
                     reads=[psmod, vecs], writes=[mod])

    def derive(s, which):
        mv = lambda v: mod.t[:, v * 16:(v + 1) * 16, s]
        if which == 0:
            S.op(DVE, lambda: V.scalar_tensor_tensor(out=der.t[:, s, 0, :], in0=mv(1), scalar=1.0, in1=vecs.t[:, V_GATTN:V_GATTN + 16],
                                                     op0=ALU.add, op1=ALU.mult), reads=[mod, vecs], writes=[der])
            S.op(DVE, lambda: V.tensor_copy(out=der.t[:, s, 1, :], in_=mv(0)), reads=[mod], writes=[der])
        else:
            S.op(DVE, lambda: V.tensor_copy(out=der.t[:, s, 2, :], in_=mv(2)), reads=[mod], writes=[der])
            S.op(DVE, lambda: V.scalar_tensor_tensor(out=der.t[:, s, 3, :], in0=mv(4), scalar=1.0, in1=vecs.t[:, V_GFFN:V_GFFN + 16],
                                                     op0=ALU.add, op1=ALU.mult), reads=[mod, vecs], writes=[der])
            S.op(DVE, lambda: V.tensor_copy(out=der.t[:, s, 4, :], in_=mv(3)), reads=[mod], writes=[der])
            S.op(DVE, lambda: V.tensor_copy(out=der.t[:, s, 5, :], in_=mv(5)), reads=[mod], writes=[der])

    mod_phase(list(range(0, 8)))
    for s in range(2):
        derive(s, 0)
    gs1 = lambda s, kc: der.t[:, s, 0, kc:kc + 1]
    sh1 = lambda s, kc: der.t[:, s, 1, kc:kc + 1]
    gt1 = lambda s, kc: der.t[:, s, 2, kc:kc + 1]
    gs2 = lambda s, kc: der.t[:, s, 3, kc:kc + 1]
    sh2 = lambda s, kc: der.t[:, s, 4, kc:kc + 1]
    gt2 = lambda s, kc: der.t[:, s, 5, kc:kc + 1]
    if stage == 0:
        mod_phase(list(range(8, 24)))
        for s in range(2):
            derive(s, 1)
        dbg_dump("mod", mod, [128, 96, 2])
        dbg_dump("der", der, [128, 2, 6, 16])
        S.barrier()
        return nc, dbg_out

    def mm_group(bk, out_ap, pairs, reads):
        n = len(pairs)
        for i, (l, r) in enumerate(pairs):
            S.op(PE, lambda l=l, r=r, i=i: T.matmul(out_ap, l, r, start=(i == 0), stop=(i == n - 1)),
                 reads=reads if i == 0 else [], writes=[bk] if i == 0 else [], inc=(i == n - 1))

    class XRing:
        def __init__(self, base, nslots, tag):
            self.tiles = [S.tile("xt%s%d" % (tag, i), [128, D], F32, base + i * 8 * KB) for i in range(nslots)]
            self.ds = [S.dsem("xt%s%d" % (tag, i)) for i in range(nslots)]
            self.n = nslots
            self.k = 0

        def load(self, row0):
            i = self.k % self.n
            self.k += 1
            S.dma(SP, self.ds[i], self.tiles[i].t[:, :], xall.t[row0:row0 + 128, :], reads=[xall], writes=[self.tiles[i]])
            return self.tiles[i]

    def token_stats_scale(xt, junk, ssb, k):
        S.op(ACT, lambda: A.activation(out=junk.t[:, :], in_=xt.t[:, :], func=AF.Square, accum_out=ssb.t[:, k:k + 1]),
             reads=[xt], writes=[junk, ssb])
        S.op(ACT, lambda: A.activation(out=ssb.t[:, k:k + 1], in_=ssb.t[:, k:k + 1], func=AF.Sqrt, bias=epsT.t[:, 0:1], scale=1.0 / D),
             reads=[ssb, epsT], writes=[ssb])
        S.op(DVE, lambda: V.reciprocal(out=ssb.t[:, k:k + 1], in_=ssb.t[:, k:k + 1]), reads=[ssb], writes=[ssb])
        S.op(DVE, lambda: V.tensor_scalar(out=xt.t[:, :], in0=xt.t[:, :], scalar1=ssb.t[:, k:k + 1], scalar2=None, op0=ALU.mult),
             reads=[xt, ssb], writes=[xt])

    def transpose_to_h(xts, h, s, pbanks, pk):
        for kc in range(16):
            bk = pbanks[pk[0] % len(pbanks)]
            pk[0] += 1
            for t in range(4):
                S.op(PE, lambda t=t, kc=kc, bk=bk: T.transpose(bk.t[:, t * 128:(t + 1) * 128], xts[t].t[:, kc * 128:(kc + 1) * 128], ident.t[:, :]),
                     reads=[xts[t], ident], writes=[bk] if t == 0 else [], inc=(t == 3))
            if kc % 2 == 0:
                S.op(ACT, lambda kc=kc, bk=bk: A.activation(out=h.t[:, kc, :], in_=bk.t[:, :], func=AF.Identity, scale=gs1(s, kc), bias=sh1(s, kc)),
                     reads=[bk, der], writes=[h])
            else:
                S.op(DVE, lambda kc=kc, bk=bk: V.tensor_scalar(out=h.t[:, kc, :], in0=bk.t[:, :], scalar1=gs1(s, kc), scalar2=sh1(s, kc),
                                                                op0=ALU.mult, op1=ALU.add),
                     reads=[bk, der], writes=[h])

    def evac_copy(i, out_ap, in_ap, reads, writes, scale=None):
        if i % 2 == 0:
            if scale is None:
                S.op(ACT, lambda: A.copy(out=out_ap, in_=in_ap), reads=reads, writes=writes)
            else:
                S.op(ACT, lambda: A.mul(out=out_ap, in_=in_ap, mul=scale) if False else A.activation(out=out_ap, in_=in_ap, func=AF.Copy, scale=scale),
                     reads=reads, writes=writes)
        else:
            if scale is None:
                S.op(DVE, lambda: V.tensor_copy(out=out_ap, in_=in_ap), reads=reads, writes=writes)
            else:
                S.op(DVE, lambda: V.tensor_scalar(out=out_ap, in0=in_ap, scalar1=scale, scalar2=None, op0=ALU.mult), reads=reads, writes=writes)

    def rstd_from_bank(bk, dim, rt):
        S.op(ACT, lambda: A.activation(out=rt.t[:, :], in_=bk.t[:, :], func=AF.Sqrt, bias=epsT.t[:, 0:1], scale=1.0 / dim),
             reads=[bk, epsT], writes=[rt])
        S.op(DVE, lambda: V.reciprocal(out=rt.t[:, :], in_=rt.t[:, :]), reads=[rt], writes=[rt])

    P1 = CONST_END
    o = P1
    xr = XRing(o, 8, "a"); o += 64 * KB
    hA = [S.tile("hA%d" % i, [128, 16, 512], BF16, o + i * 16 * KB) for i in range(2)]; o += 32 * KB
    wa = S.tile("wa", [128, 6, 16, 128], BF16, o); o += 24 * KB
    wuk = S.tile("wuk", [128, 8, 4, 128], BF16, o); o += 8 * KB
    wuv = S.tile("wuv", [128, 4, 1024], BF16, o); o += 8 * KB
    kvc32 = S.tile("kvc32", [128, 4, 512], F32, o); o += 8 * KB
    sqkv = S.tile("sqkv", [128, 4, 512], BF16, o); o += 4 * KB
    kvn = S.tile("kvn", [128, 4, 512], BF16, o); o += 4 * KB
    kT_out = S.tile("kT_out", [128, 8, 512], BF16, o); o += 8 * KB
    v_out = S.tile("v_out", [128, 8, 4, 128], BF16, o); o += 8 * KB
    t1 = S.tile("t1", [128, 512], F32, o); o += 2 * KB
    t2 = S.tile("t2", [128, 512], F32, o); o += 2 * KB
    krd = S.tile("krd", [128, 512], BF16, o); o += 1 * KB
    rk32 = S.tile("rk32", [128, 512], F32, o); o += 2 * KB
    tabA = [S.tile("tabA%d" % i, [128, 2, 512], F32, o + i * 4 * KB) for i in range(3)]; o += 12 * KB
    junk = S.tile("junk", [128, D], BF16, o); o += 4 * KB
    ssb = S.tile("ssb", [128, 8], F32, o); o += 32
    assert o <= BASE + 207 * KB, o
    tab_ds = [S.dsem("tabA%d" % i) for i in range(3)]
    wds = S.dsem("w1a")
    S.dma(SP, wds, wa.t[:, :, :, :], wbf["WA"].t.rearrange("m p k n -> p m k n"), reads=[wbf["WA"]], writes=[wa])
    S.dma(SP, wds, wuk.t[:, :, :, :], wbf["WUK"].t.rearrange("m p k n -> p m k n"), reads=[wbf["WUK"]], writes=[wuk])
    S.dma(SP, wds, wuv.t[:, :, :], wbf["WUV"].t, reads=[wbf["WUV"]], writes=[wuv])
    st_k, st_v, st_pe = S.dsem("st_k"), S.dsem("st_v"), S.dsem("st_pe")

    blocks = [(0, j) for j in range(NBP)] + [(1, j) for j in range(NBS)]
    if stage == 1:
        blocks = blocks[:3]
    pbanks = [bank[0], bank[1]]
    mbanks = [bank[2], bank[3], bank[4], bank[5], bank[6]]
    pk = [0]
    mk = [0]

    def nb():
        b = mbanks[mk[0] % len(mbanks)]
        mk[0] += 1
        return b

    xtiles = {}

    def p1_load(bi):
        s, j = blocks[bi]
        row0 = (0 if s == 0 else SP_) + j * 512
        xtiles[bi] = [xr.load(row0 + t * 128) for t in range(4)]
        S.dma(SP, tab_ds[bi % 3], tabA[bi % 3].t[:, :, :], tabs_d.t[(0 if s == 0 else NBP) + j], reads=[tabs_d], writes=[tabA[bi % 3]])

    def p1_prologue(bi):
        s, j = blocks[bi]
        for t in range(4):
            token_stats_scale(xtiles[bi][t], junk, ssb, (bi * 4 + t) % 8)
        transpose_to_h(xtiles[bi], hA[bi % 2], s, pbanks, pk)

    def p1_main(bi):
        s, j = blocks[bi]
        h = hA[bi % 2]
        tab = tabA[bi % 3]
        for m in range(4):
            bk = nb()
            mm_group(bk, bk.t[:, :], [(wa.t[:, m, kc, :], h.t[:, kc, :]) for kc in range(16)], [wa, h])
            S.op(DVE, lambda m=m, bk=bk: V.tensor_copy(out=kvc32.t[:, m, :], in_=bk.t[:, :]), reads=[bk], writes=[kvc32])
            S.op(ACT, lambda m=m: A.activation(out=sqkv.t[:, m, :], in_=kvc32.t[:, m, :], func=AF.Square), reads=[kvc32], writes=[sqkv])
        if KPART < 2:
            return
        bkA = nb()
        mm_group(bkA, bkA.t[:, :], [(wa.t[:, 4, kc, :], h.t[:, kc, :]) for kc in range(16)], [wa, h])
        bkB = nb()
        mm_group(bkB, bkB.t[:, :], [(wa.t[:, 5, kc, :], h.t[:, kc, :]) for kc in range(16)], [wa, h])
        S.op(DVE, lambda: V.tensor_tensor(out=t1.t[:, :], in0=bkA.t[:, :], in1=tab.t[:, 0, :], op=ALU.mult), reads=[bkA, tab], writes=[t1])
        S.op(DVE, lambda: V.tensor_tensor(out=t2.t[:, :], in0=bkB.t[:, :], in1=tab.t[:, 1, :], op=ALU.mult), reads=[bkB, tab], writes=[t2])
        S.op(POOL, lambda: G.tensor_tensor(out=krd.t[:, :], in0=t1.t[:, :], in1=t2.t[:, :], op=ALU.add), reads=[t1, t2], writes=[krd])
        if KPART < 3:
            return
        bk = nb()
        mm_group(bk, bk.t[:, :], [(ones.t[:, :], sqkv.t[:, m, :]) for m in range(4)], [ones, sqkv])
        rstd_from_bank(bk, 512, rk32)
        for m in range(4):
            S.op(DVE, lambda m=m: V.scalar_tensor_tensor(out=kvn.t[:, m, :], in0=kvc32.t[:, m, :], scalar=vecs.t[:, V_GKV + m:V_GKV + m + 1],
                                                         in1=rk32.t[:, :], op0=ALU.mult, op1=ALU.mult),
                 reads=[kvc32, rk32, vecs], writes=[kvn])
        if KPART < 4:
            return
        for hh in range(8):
            bk = nb()
            mm_group(bk, bk.t[:, :], [(wuk.t[:, hh, m, :], kvn.t[:, m, :]) for m in range(4)], [wuk, kvn])
            evac_copy(hh, kT_out.t[:, hh, :], bk.t[:, :], [bk], [kT_out])
        if KPART < 5:
            return
        for t in range(4):
            for c in range(2):
                bk = nb()
                mm_group(bk, bk.t[:, :], [(kvn.t[:, m, t * 128:(t + 1) * 128], wuv.t[:, m, c * 512:(c + 1) * 512]) for m in range(4)], [wuv, kvn])
                evac_copy(t * 2 + c, v_out.t[:, 4 * c:4 * c + 4, t, :], bk.t[:, :].rearrange("p (h d) -> p h d", d=128), [bk], [v_out])

    def p1_store(bi):
        s, j = blocks[bi]
        S.dma(POOL, st_k, KT[s].t[:, :, j * 512:(j + 1) * 512].rearrange("h d t -> d h t"), kT_out.t[:, :, :], reads=[kT_out], writes=[KT[s]])
        S.dma(POOL, st_pe, KPE[s].t[:, j * 512:(j + 1) * 512], krd.t[:, :], reads=[krd], writes=[KPE[s]])
        S.dma(POOL, st_v, VV[s].t[:, :, 4 * j:4 * j + 4, :].rearrange("h p c d -> p h c d"), v_out.t[:, :, :, :], reads=[v_out], writes=[VV[s]])

    nblk = len(blocks)
    p1_load(0)
    if nblk > 1:
        p1_load(1)
    p1_prologue(0)
    for bi in range(nblk):
        if bi + 2 < nblk:
            p1_load(bi + 2)
        if bi + 1 < nblk:
            p1_prologue(bi + 1)
        if SKIP < 2:
            p1_main(bi)
        if SKIP < 1:
            p1_store(bi)
    if stage == 1:
        S.barrier()
        dbg_dump("h0", hA[0], [128, 16, 512], BF16)
        dbg_dump("KT0", (KT[0].t[:, :, 0:1536], KT[0]), [8, 128, 1536], BF16)
        dbg_dump("KPE0", (KPE[0].t[:, 0:1536], KPE[0]), [128, 1536], BF16)
        dbg_dump("VV0", (VV[0].t[:, :, 0:12, :], VV[0]), [8, 128, 12, 128], BF16)
        S.barrier()
        return nc, dbg_out

    o = P1 + 96 * KB
    wkna = S.tile("wkna", [128, 8, 16, 128], BF16, o); o += 32 * KB
    wvna = S.tile("wvna", [128, 16, 1024], BF16, o); o += 32 * KB
    kna_out = S.tile("kna_out", [128, 8, 512], BF16, o); o += 8 * KB
    vna_out = S.tile("vna_out", [128, 4, 1024], BF16, o); o += 8 * KB
    wds2 = S.dsem("w1b")
    S.dma(SP, wds2, wkna.t[:, :, :, :], wbf["WKNA"].t.rearrange("m p k n -> p m k n"), reads=[wbf["WKNA"]], writes=[wkna])
    S.dma(SP, wds2, wvna.t[:, :, :], wbf["WVNA"].t, reads=[wbf["WVNA"]], writes=[wvna])
    st_kn, st_vn = S.dsem("st_kn"), S.dsem("st_vn")
    nblocks = [(0, j) for j in range(NAP // 512)] + [(1, j) for j in range(NAS // 512)]
    if stage == 2:
        nblocks = nblocks[:5]
    xt2 = {}

    def p1b_load(bi):
        s, j = nblocks[bi]
        row0 = (0 if s == 0 else SP_) + j * 512
        xt2[bi] = [xr.load(row0 + t * 128) for t in range(4)]

    def p1b_prologue(bi):
        s, j = nblocks[bi]
        for t in range(4):
            token_stats_scale(xt2[bi][t], junk, ssb, (bi * 4 + t) % 8)
        transpose_to_h(xt2[bi], hA[bi % 2], s, pbanks, pk)

    def p1b_main(bi):
        s, j = nblocks[bi]
        h = hA[bi % 2]
        for hh in range(8):
            bk = nb()
            mm_group(bk, bk.t[:, :], [(wkna.t[:, hh, kc, :], h.t[:, kc, :]) for kc in range(16)], [wkna, h])
            evac_copy(hh, kna_out.t[:, hh, :], bk.t[:, :], [bk], [kna_out])
        for t in range(4):
            for c in range(2):
                bk = nb()
                mm_group(bk, bk.t[:, :], [(h.t[:, kc, t * 128:(t + 1) * 128], wvna.t[:, kc, c * 512:(c + 1) * 512]) for kc in range(16)], [wvna, h])
                evac_copy(t * 2 + c, vna_out.t[:, t, c * 512:(c + 1) * 512], bk.t[:, :], [bk], [vna_out])
        S.dma(POOL, st_kn, KNA[s].t[:, :, j * 512:(j + 1) * 512].rearrange("h d t -> d h t"), kna_out.t[:, :, :], reads=[kna_out], writes=[KNA[s]])
        S.dma(POOL, st_vn, VNA[s].t[j * 512:(j + 1) * 512, :].rearrange("(t p) n -> p t n", p=128), vna_out.t[:, :, :], reads=[vna_out], writes=[VNA[s]])

    nnb = len(nblocks)
    p1b_load(0)
    p1b_load(1)
    p1b_prologue(0)
    for bi in range(nnb):
        if bi + 2 < nnb:
            p1b_load(bi + 2)
        if bi + 1 < nnb:
            p1b_prologue(bi + 1)
        p1b_main(bi)

    mod_phase(list(range(8, 24)))
    for s in range(2):
        derive(s, 1)
    S.barrier()

    Q = CONST_END
    xT = S.tile("xT", [128, 16, 512], F32, Q)
    h2 = S.tile("h2", [128, 16, 512], BF16, Q + 32 * KB)
    HS = Q + 32 * KB
    sq = S.tile("sq", [128, 16, 512], BF16, Q + 48 * KB)
    rA = S.tile("rA", [128, 512], F32, Q + 64 * KB)
    rB = S.tile("rB", [128, 512], F32, Q + 66 * KB)
    tab2 = S.tile("tab2", [128, 2, 512], F32, Q + 68 * KB)
    tmp = [S.tile("tmp%d" % i, [128, 512], F32, Q + (72 + 2 * i) * KB) for i in range(2)]
    junk2 = S.tile("junk2", [128, D], BF16, Q + 72 * KB)
    ssb2 = S.tile("ssb2", [128, 8], F32, BASE + 6 * KB - 64)
    NWR = 5
    WR = Q + 76 * KB
    wr_pair = [S.tile("wrp%d" % i, [128, 2, 16, 128], BF16, WR + i * 8 * KB) for i in range(NWR)]
    wr_uq = [S.tile("wru%d" % i, [128, 8, 4, 128], BF16, WR + i * 8 * KB) for i in range(NWR)]
    wr_ds = [S.dsem("wr%d" % i) for i in range(NWR)]
    wrk = [0]
    AR = Q + 116 * KB
    assert AR + 85 * KB <= BASE + 207 * KB
    xr2 = XRing(AR, 4, "b")
    Kwin = S.tile("Kwin", [128, 8, 896], BF16, AR)
    Vwin = S.tile("Vwin", [128, 7, 1024], BF16, AR + 14 * KB)
    kv_ds = [S.dsem("kv%d" % i) for i in range(NKV)]
    PER = SCK // 128
    Ksl = [S.tile("Ksl%d" % i, [128, SCK], BF16, AR + i * 6 * KB) for i in range(NKV)]
    Pesl = [S.tile("Pesl%d" % i, [128, SCK], BF16, AR + i * 6 * KB + 2 * KB) for i in range(NKV)]
    Vsl = [S.tile("Vsl%d" % i, [128, PER, 128], BF16, AR + i * 6 * KB + 4 * KB) for i in range(NKV)]
    merged = S.tile("merged", [128, 16, 512], BF16, AR)
    aT = S.tile("aT", [128, NFF, 512], BF16, AR)
    qc32 = S.tile("qc32", [128, 4, 512], F32, AR + 36 * KB)
    qn = S.tile("qn", [128, 4, 512], BF16, AR + 44 * KB)
    ona32 = S.tile("ona32", [128, 8, 512], F32, AR + 32 * KB)
    omla32 = S.tile("omla32", [128, 8, 512], F32, AR + 48 * KB)
    qna = S.tile("qna", [128, 8, 512], BF16, AR + 64 * KB)
    qnope = S.tile("qnope", [128, 8, 512], BF16, AR + 72 * KB)
    qrope = S.tile("qrope", [128, 4, 512], BF16, AR + 80 * KB)
    wd = [S.tile("wd%d" % i, [128, NFF, 128], BF16, AR + 44 * KB + i * 11 * KB) for i in range(2)]
    wd_ds = [S.dsem("wd%d" % i) for i in range(2)]
    ytok = [S.tile("ytok%d" % i, [128, D], F32, AR + 66 * KB + i * 8 * KB) for i in range(2)]
    y_ds = [S.dsem("y%d" % i) for i in range(2)]
    nbias = [S.tile("nbias%d" % i, [128, 896], F32, HS + i * 3584) for i in range(2)]
    nb_ds = [S.dsem("nbias%d" % i) for i in range(2)]
    sbna = S.tile("sbna", [128, 896], F32, HS + 7168)
    pTna = [S.tile("pTna%d" % i, [128, 896], BF16, HS + 10752 + i * 1792) for i in range(2)]
    rlna = S.tile("rlna", [128, 128], F32, HS + 14336)
    pT = [S.tile("pT%d" % i, [128, 512], BF16, HS + i * KB) for i in range(4)]
    acc = S.tile("acc", [128, 512], F32, HS + 4 * KB)
    accb = S.tile("accb", [128, 512], BF16, HS + 6 * KB)
    rl = S.tile("rl", [128, 512], F32, HS + 7 * KB)
    tab2_ds, kw_ds, vw_ds = S.dsem("tab2"), S.dsem("kwin"), S.dsem("vwin")
    QS_NA = 128.0 ** -0.5
    QS_MLA = 192.0 ** -0.5

    def wr_next():
        i = wrk[0] % NWR
        wrk[0] += 1
        return i

    def ones_rstd(chunks, dim, rt):
        bk = nb()
        mm_group(bk, bk.t[:, :], [(ones.t[:, :], sq.t[:, c, :]) for c in chunks], [ones, sq])
        rstd_from_bank(bk, dim, rt)

    own_blocks = [(0, bb) for bb in range(OWNP // 512)] + [(1, bb) for bb in range(OWNS // 512)]
    if stage == 3:
        own_blocks = own_blocks[:1]
    for (s, bb) in own_blocks:
        jrot = 1 + bb
        L = SP_ if s == 0 else SS_
        row0 = (0 if s == 0 else SP_) + jrot * 512
        nt_seg = (OWNP if s == 0 else OWNS) // 128
        S.dma(SP, tab2_ds, tab2.t[:, :, :], tabs_d.t[(0 if s == 0 else NBP) + jrot], reads=[tabs_d], writes=[tab2])
        xts = [xr2.load(row0 + t * 128) for t in range(4)]
        for kc in range(16):
            bk = pbanks[pk[0] % 2]
            pk[0] += 1
            for t in range(4):
                S.op(PE, lambda t=t, kc=kc, bk=bk: T.transpose(bk.t[:, t * 128:(t + 1) * 128], xts[t].t[:, kc * 128:(kc + 1) * 128], ident.t[:, :]),
                     reads=[xts[t], ident], writes=[bk] if t == 0 else [], inc=(t == 3))
            evac_copy(kc, xT.t[:, kc, :], bk.t[:, :], [bk], [xT])
        for t in range(4):
            token_stats_scale(xts[t], junk2, ssb2, t)
        transpose_to_h(xts, h2, s, pbanks, pk)
        for m in range(12):
            if m % 2 == 0:
                wi = wr_next()
                S.dma(SP, wr_ds[wi], wr_pair[wi].t[:, :, :, :], wbf["WQ"].t[m:m + 2].rearrange("m p k n -> p m k n"), reads=[wbf["WQ"]], writes=[wr_pair[wi]])
            w = wr_pair[wi]
            bk = nb()
            mm_group(bk, bk.t[:, :], [(w.t[:, m % 2, kc, :], h2.t[:, kc, :]) for kc in range(16)], [w, h2])
            if m < 8:
                evac_copy(m, qna.t[:, m, :], bk.t[:, :], [bk], [qna], scale=QS_NA)
            else:
                S.op(DVE, lambda m=m, bk=bk: V.tensor_copy(out=qc32.t[:, m - 8, :], in_=bk.t[:, :]), reads=[bk], writes=[qc32])
                S.op(ACT, lambda m=m: A.activation(out=sq.t[:, m - 8, :], in_=qc32.t[:, m - 8, :], func=AF.Square), reads=[qc32], writes=[sq])
        ones_rstd(range(4), 512, rA)
        for m in range(4):
            S.op(DVE, lambda m=m: V.scalar_tensor_tensor(out=qn.t[:, m, :], in0=qc32.t[:, m, :], scalar=vecs.t[:, V_GQ + m:V_GQ + m + 1],
                                                         in1=rA.t[:, :], op0=ALU.mult, op1=ALU.mult), reads=[qc32, rA, vecs], writes=[qn])
        wu = []
        for i in range(2):
            wi = wr_next()
            S.dma(SP, wr_ds[wi], wr_uq[wi].t[:, :, :, :], wbf["WUQ"].t[8 * i:8 * i + 8].rearrange("m p k n -> p m k n"), reads=[wbf["WUQ"]], writes=[wr_uq[wi]])
            wu.append(wr_uq[wi])
        for hh in range(8):
            bk = nb()
            mm_group(bk, bk.t[:, :], [(wu[0].t[:, hh, m, :], qn.t[:, m, :]) for m in range(4)], [wu[0], qn])
            evac_copy(hh, qnope.t[:, hh, :], bk.t[:, :], [bk], [qnope], scale=QS_MLA)
        for pr in range(4):
            bkA = nb()
            mm_group(bkA, bkA.t[:, :], [(wu[1].t[:, pr, m, :], qn.t[:, m, :]) for m in range(4)], [wu[1], qn])
            bkB = nb()
            mm_group(bkB, bkB.t[:, :], [(wu[1].t[:, 4 + pr, m, :], qn.t[:, m, :]) for m in range(4)], [wu[1], qn])
            S.op(DVE, lambda bkA=bkA: V.scalar_tensor_tensor(out=tmp[0].t[:, :], in0=bkA.t[:, :], scalar=QS_MLA, in1=tab2.t[:, 0, :], op0=ALU.mult, op1=ALU.mult),
                 reads=[bkA, tab2], writes=[tmp[0]])
            S.op(DVE, lambda bkB=bkB: V.scalar_tensor_tensor(out=tmp[1].t[:, :], in0=bkB.t[:, :], scalar=QS_MLA, in1=tab2.t[:, 1, :], op0=ALU.mult, op1=ALU.mult),
                 reads=[bkB, tab2], writes=[tmp[1]])
            S.op(POOL, lambda pr=pr: G.tensor_tensor(out=qrope.t[:, pr, :], in0=tmp[0].t[:, :], in1=tmp[1].t[:, :], op=ALU.add),
                 reads=[tmp[0], tmp[1]], writes=[qrope])
        if KPART >= 2:
            for qt in range(4):
                ti = bb * 4 + qt
                slot = _slot_of(ti, nt_seg)
                k0 = 128 + 128 * ti
                S.dma(SP, kw_ds, Kwin.t[:, :, :], KNA[s].t[:, :, k0:k0 + 896].rearrange("h d t -> d h t"), reads=[KNA[s]], writes=[Kwin])
                S.dma(SP, vw_ds, Vwin.t[:, :, :], VNA[s].t[k0:k0 + 896, :].rearrange("(c p) n -> p c n", p=128), reads=[VNA[s]], writes=[Vwin])
                for hh in range(8):
                    i2 = hh % 2
                    S.dma(SP, nb_ds[i2], nbias[i2].t[:, :], nab_d.t[s, slot, hh], reads=[nab_d], writes=[nbias[i2]])
                    bS = [bank[2 * i2], bank[2 * i2 + 1]]
                    ps2 = ps_all[:, 2 * i2:2 * i2 + 2, :].rearrange("p a b -> p (a b)")
                    for c in range(7):
                        S.op(PE, lambda c=c, hh=hh, ps2=ps2: T.matmul(ps2[:, c * 128:(c + 1) * 128], Kwin.t[:, hh, c * 128:(c + 1) * 128],
                                                                      qna.t[:, hh, qt * 128:(qt + 1) * 128], start=True, stop=True),
                             reads=[Kwin, qna] if c == 0 else [], writes=bS if c == 0 else [], inc=(c == 6))
                    S.op(DVE, lambda ps2=ps2, i2=i2: V.tensor_tensor(out=sbna.t[:, :], in0=ps2[:, 0:896], in1=nbias[i2].t[:, :], op=ALU.add),
                         reads=bS + [nbias[i2]], writes=[sbna])
                    S.op(ACT, lambda i2=i2: A.activation(out=pTna[i2].t[:, :], in_=sbna.t[:, :], func=AF.Exp), reads=[sbna], writes=[pTna[i2]])
                    bO = bank[4 + i2]
                    bL = bank[6 + i2]
                    mm_group(bO, bO.t[:, 0:128], [(Vwin.t[:, c, hh * 128:(hh + 1) * 128], pTna[i2].t[:, c * 128:(c + 1) * 128]) for c in range(7)], [Vwin, pTna[i2]])
                    mm_group(bL, bL.t[:, 0:128], [(ones.t[:, :], pTna[i2].t[:, c * 128:(c + 1) * 128]) for c in range(7)], [ones, pTna[i2]])
                    S.op(DVE, lambda bL=bL: V.reciprocal(out=rlna.t[:, :], in_=bL.t[:, 0:128]), reads=[bL], writes=[rlna])
                    S.op(DVE, lambda bO=bO, hh=hh, qt=qt: V.tensor_tensor(out=ona32.t[:, hh, qt * 128:(qt + 1) * 128], in0=bO.t[:, 0:128], in1=rlna.t[:, :], op=ALU.mult),
                         reads=[bO, rlna], writes=[ona32])
            for hh in range(8):
                S.op(ACT, lambda hh=hh: A.activation(out=sq.t[:, hh, :], in_=ona32.t[:, hh, :], func=AF.Square), reads=[ona32], writes=[sq])
        if KPART >= 3:
            nch = L // 128
            kvk = [0]
            for hh in range(8):
                e, pr = hh % 2, hh // 2
                bO = bank[3 + hh % 2]
                pend = None

                def emit_pv(p, hh=hh, bO=bO):
                    g, si, c = p
                    S.op(PE, lambda: T.matmul(bO.t[:, :], Vsl[si].t[:, c, :], pT[g % 4].t[:, :], start=(g == 0), stop=(g == nch - 1)),
                         reads=[Vsl[si], pT[g % 4]], writes=[bO] if g in (0, nch - 1) else [], inc=True)

                for g in range(nch):
                    c = g % PER
                    if c == 0:
                        si = kvk[0] % NKV
                        kvk[0] += 1
                        sc_ = g // PER
                        S.dma(SP, kv_ds[si], Ksl[si].t[:, :], KT[s].t[hh, :, sc_ * SCK:(sc_ + 1) * SCK], reads=[KT[s]], writes=[Ksl[si]])
                        S.dma(SP, kv_ds[si], Pesl[si].t[:, :], KPE[s].t[:, sc_ * SCK:(sc_ + 1) * SCK], reads=[KPE[s]], writes=[Pesl[si]])
                        S.dma(SP, kv_ds[si], Vsl[si].t[:, :, :], VV[s].t[hh, :, sc_ * PER:(sc_ + 1) * PER, :], reads=[VV[s]], writes=[Vsl[si]])
                    bS = bank[g % 3]
                    S.op(PE, lambda si=si, c=c, bS=bS: T.matmul(bS.t[:, :], Ksl[si].t[:, c * 128:(c + 1) * 128], qnope.t[:, hh, :], start=True, stop=False),
                         reads=[Ksl[si], qnope], writes=[bS], inc=False)
                    S.op(PE, lambda si=si, c=c, bS=bS: T.matmul(bS.t[:, :], Pesl[si].t[64 * e:64 * e + 64, c * 128:(c + 1) * 128],
                                                                qrope.t[64 * e:64 * e + 64, pr, :], start=False, stop=True),
                         reads=[Pesl[si], qrope], writes=[], inc=True)
                    if pend is not None:
                        emit_pv(pend)
                    S.op(ACT, lambda g=g, bS=bS: A.activation(out=pT[g % 4].t[:, :], in_=bS.t[:, :], func=AF.Exp), reads=[bS], writes=[pT[g % 4]])
                    if g == 0:
                        S.op(DVE, lambda g=g: V.tensor_copy(out=acc.t[:, :], in_=pT[g % 4].t[:, :]), reads=[pT[g % 4]], writes=[acc])
                    else:
                        S.op(DVE, lambda g=g: V.tensor_tensor(out=acc.t[:, :], in0=acc.t[:, :], in1=pT[g % 4].t[:, :], op=ALU.add),
                             reads=[pT[g % 4], acc], writes=[acc])
                    pend = (g, si, c)
                emit_pv(pend)
                S.op(DVE, lambda: V.tensor_copy(out=accb.t[:, :], in_=acc.t[:, :]), reads=[acc], writes=[accb])
                bL = bank[5]
                mm_group(bL, bL.t[:, :], [(ones.t[:, :], accb.t[:, :])], [ones, accb])
                S.op(DVE, lambda bL=bL: V.reciprocal(out=rl.t[:, :], in_=bL.t[:, :]), reads=[bL], writes=[rl])
                S.op(DVE, lambda bO=bO, hh=hh: V.tensor_tensor(out=omla32.t[:, hh, :], in0=bO.t[:, :], in1=rl.t[:, :], op=ALU.mult),
                     reads=[bO, rl], writes=[omla32])
                S.op(ACT, lambda hh=hh: A.activation(out=sq.t[:, 8 + hh, :], in_=omla32.t[:, hh, :], func=AF.Square), reads=[omla32], writes=[sq])
        if KPART >= 4:
            ones_rstd(range(0, 8), 1024, rA)
            ones_rstd(range(8, 16), 1024, rB)
            for hh in range(8):
                S.op(DVE, lambda hh=hh: V.scalar_tensor_tensor(out=merged.t[:, hh, :], in0=ona32.t[:, hh, :], scalar=vecs.t[:, V_GNA + hh:V_GNA + hh + 1],
                                                               in1=rA.t[:, :], op0=ALU.mult, op1=ALU.mult), reads=[ona32, rA, vecs], writes=[merged])
                S.op(DVE, lambda hh=hh: V.scalar_tensor_tensor(out=merged.t[:, 8 + hh, :], in0=omla32.t[:, hh, :], scalar=vecs.t[:, V_GMLA + hh:V_GMLA + hh + 1],
                                                               in1=rB.t[:, :], op0=ALU.mult, op1=ALU.mult), reads=[omla32, rB, vecs], writes=[merged])
            for n in range(16):
                if n % 2 == 0:
                    wi = wr_next()
                    S.dma(SP, wr_ds[wi], wr_pair[wi].t[:, :, :, :], wbf["WO"].t[n:n + 2].rearrange("m p k n -> p m k n"), reads=[wbf["WO"]], writes=[wr_pair[wi]])
                w = wr_pair[wi]
                bk = nb()
                mm_group(bk, bk.t[:, :], [(w.t[:, n % 2, kc, :], merged.t[:, kc, :]) for kc in range(16)], [w, merged])
                S.op(DVE, lambda n=n, bk=bk: V.scalar_tensor_tensor(out=xT.t[:, n, :], in0=bk.t[:, :], scalar=gt1(s, n), in1=xT.t[:, n, :], op0=ALU.mult, op1=ALU.add),
                     reads=[bk, xT, der], writes=[xT])
                S.op(ACT, lambda n=n: A.activation(out=sq.t[:, n, :], in_=xT.t[:, n, :], func=AF.Square), reads=[xT], writes=[sq])
            ones_rstd(range(16), 2048, rA)
            for kc in range(16):
                tt = tmp[kc % 2]
                S.op(DVE, lambda kc=kc, tt=tt: V.scalar_tensor_tensor(out=tt.t[:, :], in0=xT.t[:, kc, :], scalar=gs2(s, kc), in1=rA.t[:, :], op0=ALU.mult, op1=ALU.mult),
                     reads=[xT, rA, der], writes=[tt])
                S.op(ACT, lambda kc=kc, tt=tt: A.activation(out=h2.t[:, kc, :], in_=tt.t[:, :], func=AF.Identity, bias=sh2(s, kc), scale=1.0),
                     reads=[tt, der], writes=[h2])
        if KPART >= 5:
            for c in range(NFF):
                wi = wr_next()
                w = wr_pair[wi]
                S.dma(SP, wr_ds[wi], w.t[:, :, :, :], wbf["WGU"].t[c], reads=[wbf["WGU"]], writes=[w])
                bG = nb()
                mm_group(bG, bG.t[:, :], [(w.t[:, 0, kc, :], h2.t[:, kc, :]) for kc in range(16)], [w, h2])
                bU = nb()
                mm_group(bU, bU.t[:, :], [(w.t[:, 1, kc, :], h2.t[:, kc, :]) for kc in range(16)], [w, h2])
                tt = tmp[c % 2]
                S.op(ACT, lambda bG=bG, tt=tt: A.activation(out=tt.t[:, :], in_=bG.t[:, :], func=AF.Silu), reads=[bG], writes=[tt])
                S.op(DVE, lambda c=c, bU=bU, tt=tt: V.tensor_tensor(out=aT.t[:, c, :], in0=bU.t[:, :], in1=tt.t[:, :], op=ALU.mult), reads=[bU, tt], writes=[aT])
            for n in range(16):
                w = wd[n % 2]
                S.dma(SP, wd_ds[n % 2], w.t[:, :, :], wbf["WD"].t[n], reads=[wbf["WD"]], writes=[w])
                bk = nb()
                mm_group(bk, bk.t[:, :], [(w.t[:, c, :], aT.t[:, c, :]) for c in range(NFF)], [w, aT])
                S.op(DVE, lambda n=n, bk=bk: V.scalar_tensor_tensor(out=xT.t[:, n, :], in0=bk.t[:, :], scalar=gt2(s, n), in1=xT.t[:, n, :], op0=ALU.mult, op1=ALU.add),
                     reads=[bk, xT, der], writes=[xT])
                S.op(ACT, lambda n=n: A.activation(out=sq.t[:, n, :], in_=xT.t[:, n, :], func=AF.Square), reads=[xT], writes=[sq])
        ones_rstd(range(16), 2048, rA)
        for n in range(16):
            S.op(DVE, lambda n=n: V.scalar_tensor_tensor(out=xT.t[:, n, :], in0=xT.t[:, n, :], scalar=vecs.t[:, V_GFIN + n:V_GFIN + n + 1], in1=rA.t[:, :],
                                                         op0=ALU.mult, op1=ALU.mult), reads=[xT, rA, vecs], writes=[xT])
        for t in range(4):
            yt = ytok[t % 2]
            for g4 in range(4):
                bk = pbanks[pk[0] % 2]
                pk[0] += 1
                for i in range(4):
                    S.op(PE, lambda i=i, g4=g4, t=t, bk=bk: T.transpose(bk.t[:, i * 128:(i + 1) * 128], xT.t[:, g4 * 4 + i, t * 128:(t + 1) * 128], ident.t[:, :]),
                         reads=[xT, ident], writes=[bk] if i == 0 else [], inc=(i == 3))
                evac_copy(g4, yt.t[:, g4 * 512:(g4 + 1) * 512], bk.t[:, :], [bk], [yt])
            r0 = bb * 512 + t * 128
            S.dma(POOL, y_ds[t % 2], yout[s].t[r0:r0 + 128, :], yt.t[:, :], reads=[yt], writes=[yout[s]])
    S.barrier()
    return nc, dbg_out


def kernel(x_prompt, x_sample, c_prompt, c_sample, w_ada, b_ada, g_attn, w_in, rpb, g_q, w_uq, g_kv, w_ukv,
           g_out_na, g_out_mla, w_o, g_ffn, w_gate, w_up, w_down, g_final):
    inp = dict(x_prompt=x_prompt, x_sample=x_sample, c_prompt=c_prompt, c_sample=c_sample, w_ada=w_ada, b_ada=b_ada,
               g_attn=g_attn, w_in=w_in, rpb=rpb, g_q=g_q, w_uq=w_uq, g_kv=g_kv, w_ukv=w_ukv, g_out_na=g_out_na,
               g_out_mla=g_out_mla, w_o=w_o, g_ffn=g_ffn, w_gate=w_gate, w_up=w_up, w_down=w_down, g_final=g_final)
    maps = _prep(inp)
    nc, _ = build()
    res = run_bass_kernel_spmd(nc, maps, core_ids=list(range(NCORES)))
    yp = np.concatenate([np.asarray(r["y0"], np.float32) for r in res.results], axis=0)[None]
    ys = np.concatenate([np.asarray(r["y1"], np.float32) for r in res.results], axis=0)[None]
    return (yp, ys)
```

```python
import numpy as np
import concourse.bass as bass
import concourse.mybir as mybir
from concourse.bass_utils import run_bass_kernel_spmd

F32 = mybir.dt.float32
BF16 = mybir.dt.bfloat16
AF = mybir.ActivationFunctionType
ALU = mybir.AluOpType

NCORES = 8
D = 2048
KC = 16
SP_, SS_ = 8192, 16384
OWNP, OWNS = 1024, 2048
HALO = 512
NAP, NAS = OWNP + 2 * HALO, OWNS + 2 * HALO
NBP, NBS = SP_ // 512, SS_ // 512
DFF = 5632
NFF = DFF // 128
EPS = 1e-6
NEG = -30000.0
SCK = 1024
NKV = 5
import os
SKIP = int(os.environ.get('KSKIP', '0'))
KPART = int(os.environ.get('KPART', '9'))

V_BADA, V_GATTN, V_GFFN, V_GFIN, V_GQ, V_GKV, V_GNA, V_GMLA, V_CP, V_CS, NV = 0, 96, 112, 128, 144, 148, 152, 160, 168, 184, 200


class Eng:
    def __init__(self, name, h, sem, compute=True):
        self.name, self.h, self.sem, self.cnt, self.seen, self.compute = name, h, sem, 0, {}, compute


class DSem:
    def __init__(self, sem):
        self.sem, self.cnt = sem, 0


class Buf:
    def __init__(self, name, space, lo, hi):
        self.name, self.space, self.lo, self.hi = name, space, lo, hi
        self.w = None
        self.r = {}
        self.ov = [self]


class Tile:
    def __init__(self, t, buf):
        self.t, self.buf = t, buf


class Sched:
    def __init__(self, nc):
        self.nc = nc
        self.bufs = []
        self.dsems = []
        mk = lambda n, h, c=True: Eng(n, h, nc.alloc_semaphore("sem_" + n), c)
        self.PE = mk("pe", nc.tensor)
        self.ACT = mk("act", nc.scalar)
        self.DVE = mk("dve", nc.vector)
        self.POOL = mk("pool", nc.gpsimd)
        self.SP = mk("sp", nc.sync, False)
        self.engs = [self.PE, self.ACT, self.DVE, self.POOL, self.SP]
        self.nsem = 5
        self.npe = 0
        self.marks = []

    def dsem(self, name):
        d = DSem(self.nc.alloc_semaphore("ds_" + name))
        self.dsems.append(d)
        self.nsem += 1
        return d

    def _reg(self, b):
        for o in self.bufs:
            if o.space == b.space and o.lo < b.hi and b.lo < o.hi:
                o.ov.append(b)
                b.ov.append(o)
        self.bufs.append(b)
        return b

    def tile(self, name, shape, dtype, off):
        esz = 2 if dtype == BF16 else 4
        n = 1
        for s in shape[1:]:
            n *= s
        t = self.nc.alloc_sbuf_tensor_at(name, list(shape), dtype, offset=int(off))
        return Tile(t, self._reg(Buf(name, "sb", int(off), int(off) + n * esz)))

    def dram(self, name, shape, dtype, kind):
        t = self.nc.dram_tensor(name, list(shape), dtype, kind=kind).ap()
        return Tile(t, self._reg(Buf(name, "dram_" + name, 0, 1)))

    def _wait(self, eng, ev):
        sem, val, _ = ev
        k = id(sem)
        if eng.seen.get(k, 0) >= val:
            return
        eng.seen[k] = val
        eng.h.wait_ge(sem, val)

    def _deps(self, eng, reads, writes, is_dma):
        for t in reads:
            for o in t.buf.ov:
                if o.w is not None:
                    if o.w[2] is eng and not is_dma and eng is self.PE:
                        continue
                    self._wait(eng, o.w)
                if o.space == "ps":
                    for e in o.r.values():
                        if e[2] is not eng:
                            self._wait(eng, e)
        for t in writes:
            for o in t.buf.ov:
                if o.w is not None and (is_dma or o.w[2] is not eng):
                    self._wait(eng, o.w)
                for e in o.r.values():
                    if is_dma or e[2] is not eng:
                        self._wait(eng, e)

    def mark(self, name):
        self.marks.append((name, self.npe))

    def op(self, eng, fn, reads=(), writes=(), inc=True):
        if eng is self.PE:
            self.npe += 1
        self._deps(eng, reads, writes, False)
        ins = fn()
        if inc:
            eng.cnt += 1
            ins.then_inc(eng.sem, 1)
            ev = (eng.sem, eng.cnt, eng)
        else:
            ev = (eng.sem, eng.cnt + 1, eng)
        for t in reads:
            t.buf.r[eng.name] = ev
        for t in writes:
            t.buf.w = ev
            t.buf.r = {}
        return ev

    def dma(self, q, ds, out, in_, reads=(), writes=()):
        self._deps(q, reads, writes, True)
        ds.cnt += 16
        q.h.dma_start(out=out, in_=in_).then_inc(ds.sem, 16)
        ev = (ds.sem, ds.cnt, None)
        for t in reads:
            t.buf.r[("d", id(ds))] = ev
        for t in writes:
            t.buf.w = ev
            t.buf.r = {}
        return ev

    def barrier(self):
        for e in self.engs:
            for o in self.engs:
                if o is not e and o.cnt > 0:
                    self._wait(e, (o.sem, o.cnt, o))
            for d in self.dsems:
                if d.cnt > 0:
                    self._wait(e, (d.sem, d.cnt, None))


def _SL(w):
    K, N = w.shape
    return np.ascontiguousarray(w.reshape(K // 128, 128, N // 128, 128).transpose(2, 1, 0, 3))


def _ML(w):
    K, N = w.shape
    return np.ascontiguousarray(w.reshape(K // 128, 128, N).transpose(1, 0, 2))


def _fm(v):
    return np.ascontiguousarray(v.reshape(-1, 128).T)


def _rope_tables(pos):
    inv = (np.float32(10000.0) ** (-(np.arange(0, 64, 2, dtype=np.float32)) / np.float32(64))).astype(np.float32)
    ang = pos.astype(np.float32)[:, None] * inv[None, :]
    ang = np.concatenate([ang, ang], axis=-1).astype(np.float32)
    c = np.cos(ang).astype(np.float32).T
    s = np.sin(ang).astype(np.float32).T
    s = np.concatenate([-s[:32], s[32:]], axis=0)
    return np.concatenate([c, c], 0), np.concatenate([s, s], 0)


def _na_bias_variant(rpb, r, rows):
    i = np.arange(14)[:, None, None, None]
    kc = np.arange(64)[None, :, None, None]
    qq = np.arange(2)[None, None, :, None]
    qc = np.arange(64)[None, None, None, :]
    kr = r - 6 + i
    qr = r + qq
    rs = np.clip(qr - 4, 0, rows - 8)
    cs = np.clip(qc - 8, 0, 64 - 16)
    valid = (kr >= 0) & (kr < rows) & (kr >= rs) & (kr < rs + 8) & (kc >= cs) & (kc < cs + 16)
    dr = np.clip(kr - qr + 7, 0, 14)
    dc = np.clip(kc - qc + 15, 0, 30)
    dr, dc, valid = np.broadcast_arrays(dr, dc, valid)
    g = rpb[:, dr, dc]
    g = np.where(valid[None], g, np.float32(NEG)).astype(np.float32)
    g = g.reshape(8, 7, 128, 128)
    return np.ascontiguousarray(g.transpose(0, 2, 1, 3).reshape(8, 128, 896))


def _slot_of(i, n):
    return 1 if i == 0 else 2 if i == 1 else 3 if i == n - 2 else 4 if i == n - 1 else 0


def _prep(inp, cores=None):
    f32 = np.float32
    w_in = np.asarray(inp["w_in"][0], f32)
    w_uq = np.asarray(inp["w_uq"][0], f32)
    w_ukv = np.asarray(inp["w_ukv"][0], f32)
    perm = (np.arange(64) + 32) % 64
    kpe = w_in[:, 4096:4160]
    WA = np.concatenate([w_in[:, 3584:4096], kpe, kpe, kpe[:, perm], kpe[:, perm]], axis=1)
    uk = w_ukv.reshape(512, 8, 256)[:, :, :128].reshape(512, 1024)
    uv = w_ukv.reshape(512, 8, 256)[:, :, 128:].reshape(512, 1024)
    uq = w_uq.reshape(512, 8, 192)
    uqn = uq[:, :, :128].reshape(512, 1024)
    uqr = uq[:, :, 128:].reshape(512, 512)
    uqp = uq[:, :, 128:][:, :, perm].reshape(512, 512)
    WQ = np.concatenate([w_in[:, 0:1024], w_in[:, 3072:3584]], axis=1)
    WUQ = np.concatenate([uqn, uqr, uqp], axis=1)
    wg = _SL(np.asarray(inp["w_gate"][0], f32))
    wu = _SL(np.asarray(inp["w_up"][0], f32))
    WGU = np.ascontiguousarray(np.stack([wg, wu], axis=2))
    vecs = np.zeros((128, NV), f32)
    vecs[:, V_BADA:V_BADA + 96] = _fm(np.asarray(inp["b_ada"][0], f32))
    vecs[:, V_GATTN:V_GATTN + 16] = _fm(np.asarray(inp["g_attn"][0], f32))
    vecs[:, V_GFFN:V_GFFN + 16] = _fm(np.asarray(inp["g_ffn"][0], f32))
    vecs[:, V_GFIN:V_GFIN + 16] = _fm(np.asarray(inp["g_final"], f32))
    vecs[:, V_GQ:V_GQ + 4] = _fm(np.asarray(inp["g_q"][0], f32))
    vecs[:, V_GKV:V_GKV + 4] = _fm(np.asarray(inp["g_kv"][0], f32))
    vecs[:, V_GNA:V_GNA + 8] = _fm(np.asarray(inp["g_out_na"][0], f32))
    vecs[:, V_GMLA:V_GMLA + 8] = _fm(np.asarray(inp["g_out_mla"][0], f32))
    vecs[:, V_CP:V_CP + 16] = _fm(np.asarray(inp["c_prompt"][0], f32))
    vecs[:, V_CS:V_CS + 16] = _fm(np.asarray(inp["c_sample"][0], f32))
    shared = {
        "vecs": vecs,
        "w_ada": np.ascontiguousarray(np.asarray(inp["w_ada"][0], f32)),
        "ident": np.eye(128, dtype=f32),
        "WA": _SL(WA), "WUK": _SL(uk), "WUV": _ML(uv),
        "WKNA": _SL(w_in[:, 1024:2048]), "WVNA": _ML(w_in[:, 2048:3072]),
        "WQ": _SL(WQ), "WUQ": _SL(WUQ), "WO": _SL(np.asarray(inp["w_o"][0], f32)),
        "WGU": WGU, "WD": _SL(np.asarray(inp["w_down"][0], f32)),
    }
    rpb = np.asarray(inp["rpb"][0], f32)
    xp = np.asarray(inp["x_prompt"][0], f32)
    xs = np.asarray(inp["x_sample"][0], f32)
    variants = {}

    def variant(r, rows):
        key = ("t", r) if r < 4 else ("b", rows - r) if r >= rows - 4 else ("i",)
        if key not in variants:
            variants[key] = _na_bias_variant(rpb, r, rows)
        return variants[key]

    maps = []
    for c in (range(NCORES) if cores is None else cores):
        sp0 = (c * OWNP - HALO) % SP_
        ss0 = (c * OWNS - HALO) % SS_
        xall = np.concatenate([np.roll(xp, -sp0, axis=0), np.roll(xs, -ss0, axis=0)], axis=0)
        pos = np.concatenate([(sp0 + np.arange(SP_)) % SP_, (ss0 + np.arange(SS_)) % SS_])
        c2, s2 = _rope_tables(pos)
        tab = np.stack([c2, s2], axis=1).reshape(128, 2, NBP + NBS, 512).transpose(2, 0, 1, 3)
        nab = np.zeros((2, 5, 8, 128, 896), f32)
        for sg, (n, rows, row0) in enumerate(((OWNP // 128, SP_ // 64, c * (OWNP // 64)), (OWNS // 128, SS_ // 64, c * (OWNS // 64)))):
            tiles = {0: 2, 1: 0, 2: 1, 3: n - 2, 4: n - 1}
            for slot, i in tiles.items():
                nab[sg, slot] = variant(row0 + 2 * i, rows)
        m = dict(shared)
        m["xall"] = xall
        m["tabs"] = np.ascontiguousarray(tab)
        m["nab"] = nab
        maps.append(m)
    return maps


def build(stage=99, dbg=False):
    nc = bass.Bass("TRN2", target_bir_lowering=False)
    S = Sched(nc)
    PE, ACT, DVE, POOL, SP = S.PE, S.ACT, S.DVE, S.POOL, S.SP
    T, V, G, A = nc.tensor, nc.vector, nc.gpsimd, nc.scalar
    BASE = 16512
    KB = 1024

    din = lambda n, sh: S.dram(n, sh, F32, "ExternalInput")
    xall = din("xall", [SP_ + SS_, D])
    vecs_d = din("vecs", [128, NV])
    wada_d = din("w_ada", [D, 6 * D])
    ident_d = din("ident", [128, 128])
    tabs_d = din("tabs", [NBP + NBS, 128, 2, 512])
    nab_d = din("nab", [2, 5, 8, 128, 896])
    wshapes = {"WA": [6, 128, 16, 128], "WUK": [8, 128, 4, 128], "WUV": [128, 4, 1024],
               "WKNA": [8, 128, 16, 128], "WVNA": [128, 16, 1024], "WQ": [12, 128, 16, 128],
               "WUQ": [16, 128, 4, 128], "WO": [16, 128, 16, 128], "WGU": [NFF, 128, 2, 16, 128],
               "WD": [16, 128, NFF, 128]}
    w32 = {k: din(k, sh) for k, sh in wshapes.items()}
    wbf = {k: S.dram(k + "_bf", sh, BF16, "Internal") for k, sh in wshapes.items()}
    KT = [S.dram("KT%d" % s, [8, 128, L], BF16, "Internal") for s, L in enumerate((SP_, SS_))]
    KPE = [S.dram("KPE%d" % s, [128, L], BF16, "Internal") for s, L in enumerate((SP_, SS_))]
    VV = [S.dram("VV%d" % s, [8, 128, L // 128, 128], BF16, "Internal") for s, L in enumerate((SP_, SS_))]
    KNA = [S.dram("KNA%d" % s, [8, 128, L], BF16, "Internal") for s, L in enumerate((NAP, NAS))]
    VNA = [S.dram("VNA%d" % s, [L, 1024], BF16, "Internal") for s, L in enumerate((NAP, NAS))]
    yout = [S.dram("y%d" % s, [L, D], F32, "ExternalOutput") for s, L in enumerate((OWNP, OWNS))]
    dbg_out = {}

    def dbg_dump(name, tile_, shape, dtype=F32):
        if not dbg:
            return
        o = S.dram("dbg_" + name, shape, dtype, "ExternalOutput")
        dbg_out[name] = o
        S.dma(SP, S.dsem("dbg_" + name), o.t, tile_.t.ap() if isinstance(tile_, Tile) else tile_[0], reads=[tile_ if isinstance(tile_, Tile) else tile_[1]], writes=[o])

    def cast_weight(k):
        ds = S.dsem("cast_" + k)
        n = 1
        for s_ in wshapes[k]:
            n *= s_
        rows = n // 2048
        src = w32[k].t
        dst = wbf[k].t
        names = "abcde"[:len(wshapes[k])]
        pat = " ".join(names)
        src2 = src.rearrange(f"{pat} -> ({pat})").rearrange("(r c) -> r c", c=2048)
        dst2 = dst.rearrange(f"{pat} -> ({pat})").rearrange("(r c) -> r c", c=2048)
        step = 2048
        for r0 in range(0, rows, step):
            r1 = min(rows, r0 + step)
            S.dma(POOL, ds, dst2[r0:r1, :], src2[r0:r1, :], reads=[w32[k]], writes=[wbf[k]])

    for k in ("WA", "WUK", "WUV", "WKNA", "WVNA", "WQ", "WUQ", "WO", "WGU", "WD"):
        cast_weight(k)

    ps_all = nc.alloc_psum_tensor("ps_all", [128, 8, 512], F32)
    bank = [Tile(ps_all[:, i, :], S._reg(Buf("bank%d" % i, "ps", i, i + 1))) for i in range(8)]

    off = BASE
    ident = S.tile("ident", [128, 128], F32, off); off += 512
    ones = S.tile("ones", [128, 128], BF16, off); off += 256
    vecs = S.tile("vecs", [128, NV], F32, off); off += NV * 4
    mod = S.tile("mod", [128, 96, 2], F32, off); off += 768
    der = S.tile("der", [128, 2, 6, 16], F32, off); off += 768
    epsT = S.tile("epsT", [128, 1], F32, off); off += 32
    sc = S.tile("sc", [128, 16, 2], F32, off); off += 128
    gq2 = S.tile("gq2", [128, 8], F32, off); off += 32
    CONST_END = BASE + 6 * KB
    assert off <= CONST_END
    cds = S.dsem("const")
    S.dma(SP, cds, ident.t[:, :], ident_d.t[:, :], reads=[ident_d], writes=[ident])
    S.dma(SP, cds, vecs.t[:, :], vecs_d.t[:, :], reads=[vecs_d], writes=[vecs])
    S.op(DVE, lambda: V.memset(ones.t[:, :], 1.0), writes=[ones])
    S.op(DVE, lambda: V.memset(epsT.t[:, :], EPS), writes=[epsT])
    for s in range(2):
        c0 = V_CP if s == 0 else V_CS
        S.op(ACT, lambda s=s, c0=c0: A.activation(out=sc.t[:, :, s], in_=vecs.t[:, c0:c0 + 16], func=AF.Silu),
             reads=[vecs], writes=[sc])

    P0 = CONST_END
    wad = [S.tile("wad%d" % i, [128, 16, 512], F32, P0 + i * 32 * KB) for i in range(2)]
    wad_ds = [S.dsem("wad%d" % i) for i in range(2)]
    psmod = bank[7]
    psmod_v = ps_all[:, 7, 0:192].rearrange("p (n s) -> p n s", s=2)
    mod_groups_done = [0]

    def mod_phase(groups):
        for g in groups:
            i = mod_groups_done[0] % 2
            mod_groups_done[0] += 1
            S.dma(SP, wad_ds[i], wad[i].t[:, :, :],
                  wada_d.t[:, g * 512:(g + 1) * 512].rearrange("(kc p) n -> p kc n", p=128),
                  reads=[wada_d], writes=[wad[i]])
            for q in range(4):
                n = g * 4 + q
                for kc in range(16):
                    S.op(PE, lambda i=i, q=q, kc=kc, n=n: T.matmul(ps_all[:, 7, 2 * n:2 * n + 2], wad[i].t[:, kc, q * 128:(q + 1) * 128],
                                                                   sc.t[:, kc, :], start=(kc == 0), stop=(kc == 15)),
                         reads=[wad[i], sc] if kc == 0 else [], writes=[psmod] if kc == 0 else [], inc=(kc == 15))
                    S.npe -= 1
        for s in range(2):
            for g in groups:
                S.op(DVE, lambda s=s, g=g: V.tensor_tensor(out=mod.t[:, g * 4:(g + 1) * 4, s], in0=psmod_v[:, g * 4:(g + 1) * 4, s],
                                                           in1=vecs.t[:, V_BADA + g * 4:V_BADA + (g + 1) * 4], op=ALU.add),
                     reads=[psmod, vecs], writes=[mod])

    def derive(s, which):
        mv = lambda v: mod.t[:, v * 16:(v + 1) * 16, s]
        if which == 0:
            S.op(DVE, lambda: V.scalar_tensor_tensor(out=der.t[:, s, 0, :], in0=mv(1), scalar=1.0, in1=vecs.t[:, V_GATTN:V_GATTN + 16],
                                                     op0=ALU.add, op1=ALU.mult), reads=[mod, vecs], writes=[der])
            S.op(DVE, lambda: V.tensor_copy(out=der.t[:, s, 1, :], in_=mv(0)), reads=[mod], writes=[der])
        else:
            S.op(DVE, lambda: V.tensor_copy(out=der.t[:, s, 2, :], in_=mv(2)), reads=[mod], writes=[der])
            S.op(DVE, lambda: V.scalar_tensor_tensor(out=der.t[:, s, 3, :], in0=mv(4), scalar=1.0, in1=vecs.t[:, V_GFFN:V_GFFN + 16],
                                                     op0=ALU.add, op1=ALU.mult), reads=[mod, vecs], writes=[der])
            S.op(DVE, lambda: V.tensor_copy(out=der.t[:, s, 4, :], in_=mv(3)), reads=[mod], writes=[der])
            S.op(DVE, lambda: V.tensor_copy(out=der.t[:, s, 5, :], in_=mv(5)), reads=[mod], writes=[der])

    mod_phase(list(range(0, 8)))
    for s in range(2):
        derive(s, 0)
    gs1 = lambda s, kc: der.t[:, s, 0, kc:kc + 1]
    sh1 = lambda s, kc: der.t[:, s, 1, kc:kc + 1]
    gt1 = lambda s, kc: der.t[:, s, 2, kc:kc + 1]
    gs2 = lambda s, kc: der.t[:, s, 3, kc:kc + 1]
    sh2 = lambda s, kc: der.t[:, s, 4, kc:kc + 1]
    gt2 = lambda s, kc: der.t[:, s, 5, kc:kc + 1]
    if stage == 0:
        mod_phase(list(range(8, 24)))
        for s in range(2):
            derive(s, 1)
        dbg_dump("mod", mod, [128, 96, 2])
        dbg_dump("der", der, [128, 2, 6, 16])
        S.barrier()
        return nc, dbg_out

    def mm_group(bk, out_ap, pairs, reads):
        n = len(pairs)
        for i, (l, r) in enumerate(pairs):
            S.op(PE, lambda l=l, r=r, i=i: T.matmul(out_ap, l, r, start=(i == 0), stop=(i == n - 1)),
                 reads=reads if i == 0 else [], writes=[bk] if i == 0 else [], inc=(i == n - 1))

    class XRing:
        def __init__(self, base, nslots, tag):
            self.tiles = [S.tile("xt%s%d" % (tag, i), [128, D], F32, base + i * 8 * KB) for i in range(nslots)]
            self.ds = [S.dsem("xt%s%d" % (tag, i)) for i in range(nslots)]
            self.n = nslots
            self.k = 0

        def load(self, row0):
            i = self.k % self.n
            self.k += 1
            S.dma(SP, self.ds[i], self.tiles[i].t[:, :], xall.t[row0:row0 + 128, :], reads=[xall], writes=[self.tiles[i]])
            return self.tiles[i]

    def token_stats_scale(xt, junk, ssb, k):
        S.op(ACT, lambda: A.activation(out=junk.t[:, :], in_=xt.t[:, :], func=AF.Square, accum_out=ssb.t[:, k:k + 1]),
             reads=[xt], writes=[junk, ssb])
        S.op(ACT, lambda: A.activation(out=ssb.t[:, k:k + 1], in_=ssb.t[:, k:k + 1], func=AF.Sqrt, bias=epsT.t[:, 0:1], scale=1.0 / D),
             reads=[ssb, epsT], writes=[ssb])
        S.op(DVE, lambda: V.reciprocal(out=ssb.t[:, k:k + 1], in_=ssb.t[:, k:k + 1]), reads=[ssb], writes=[ssb])
        S.op(DVE, lambda: V.tensor_scalar(out=xt.t[:, :], in0=xt.t[:, :], scalar1=ssb.t[:, k:k + 1], scalar2=None, op0=ALU.mult),
             reads=[xt, ssb], writes=[xt])

    def transpose_to_h(xts, h, s, pbanks, pk):
        for kc in range(16):
            bk = pbanks[pk[0] % len(pbanks)]
            pk[0] += 1
            for t in range(4):
                S.op(PE, lambda t=t, kc=kc, bk=bk: T.transpose(bk.t[:, t * 128:(t + 1) * 128], xts[t].t[:, kc * 128:(kc + 1) * 128], ident.t[:, :]),
                     reads=[xts[t], ident], writes=[bk] if t == 0 else [], inc=(t == 3))
            if kc % 2 == 0:
                S.op(ACT, lambda kc=kc, bk=bk: A.activation(out=h.t[:, kc, :], in_=bk.t[:, :], func=AF.Identity, scale=gs1(s, kc), bias=sh1(s, kc)),
                     reads=[bk, der], writes=[h])
            else:
                S.op(DVE, lambda kc=kc, bk=bk: V.tensor_scalar(out=h.t[:, kc, :], in0=bk.t[:, :], scalar1=gs1(s, kc), scalar2=sh1(s, kc),
                                                                op0=ALU.mult, op1=ALU.add),
                     reads=[bk, der], writes=[h])

    def evac_copy(i, out_ap, in_ap, reads, writes, scale=None):
        if i % 2 == 0:
            if scale is None:
                S.op(ACT, lambda: A.copy(out=out_ap, in_=in_ap), reads=reads, writes=writes)
            else:
                S.op(ACT, lambda: A.mul(out=out_ap, in_=in_ap, mul=scale) if False else A.activation(out=out_ap, in_=in_ap, func=AF.Copy, scale=scale),
                     reads=reads, writes=writes)
        else:
            if scale is None:
                S.op(DVE, lambda: V.tensor_copy(out=out_ap, in_=in_ap), reads=reads, writes=writes)
            else:
                S.op(DVE, lambda: V.tensor_scalar(out=out_ap, in0=in_ap, scalar1=scale, scalar2=None, op0=ALU.mult), reads=reads, writes=writes)

    def rstd_from_bank(bk, dim, rt):
        S.op(ACT, lambda: A.activation(out=rt.t[:, :], in_=bk.t[:, :], func=AF.Sqrt, bias=epsT.t[:, 0:1], scale=1.0 / dim),
             reads=[bk, epsT], writes=[rt])
        S.op(DVE, lambda: V.reciprocal(out=rt.t[:, :], in_=rt.t[:, :]), reads=[rt], writes=[rt])

    S.mark('p1a')
    P1 = CONST_END
    o = P1
    xr = XRing(o, 8, "a"); o += 64 * KB
    hA = [S.tile("hA%d" % i, [128, 16, 512], BF16, o + i * 16 * KB) for i in range(2)]; o += 32 * KB
    wa = S.tile("wa", [128, 6, 16, 128], BF16, o); o += 24 * KB
    wuk = S.tile("wuk", [128, 8, 4, 128], BF16, o); o += 8 * KB
    wuv = S.tile("wuv", [128, 4, 1024], BF16, o); o += 8 * KB
    kvc32 = S.tile("kvc32", [128, 4, 512], F32, o); o += 8 * KB
    sqkv = S.tile("sqkv", [128, 4, 512], BF16, o); o += 4 * KB
    kvn = S.tile("kvn", [128, 4, 512], BF16, o); o += 4 * KB
    kT_out = S.tile("kT_out", [128, 8, 512], BF16, o); o += 8 * KB
    v_out = S.tile("v_out", [128, 8, 4, 128], BF16, o); o += 8 * KB
    t1 = S.tile("t1", [128, 512], F32, o); o += 2 * KB
    t2 = S.tile("t2", [128, 512], F32, o); o += 2 * KB
    krd = S.tile("krd", [128, 512], BF16, o); o += 1 * KB
    rk32 = S.tile("rk32", [128, 512], F32, o); o += 2 * KB
    tabA = [S.tile("tabA%d" % i, [128, 2, 512], F32, o + i * 4 * KB) for i in range(3)]; o += 12 * KB
    junk = S.tile("junk", [128, D], BF16, o); o += 4 * KB
    ssb = S.tile("ssb", [128, 8], F32, o); o += 32
    assert o <= BASE + 207 * KB, o
    tab_ds = [S.dsem("tabA%d" % i) for i in range(3)]
    wds = S.dsem("w1a")
    S.dma(SP, wds, wa.t[:, :, :, :], wbf["WA"].t.rearrange("m p k n -> p m k n"), reads=[wbf["WA"]], writes=[wa])
    S.dma(SP, wds, wuk.t[:, :, :, :], wbf["WUK"].t.rearrange("m p k n -> p m k n"), reads=[wbf["WUK"]], writes=[wuk])
    S.dma(SP, wds, wuv.t[:, :, :], wbf["WUV"].t, reads=[wbf["WUV"]], writes=[wuv])
    st_k, st_v, st_pe = S.dsem("st_k"), S.dsem("st_v"), S.dsem("st_pe")

    blocks = [(0, j) for j in range(NBP)] + [(1, j) for j in range(NBS)]
    if stage == 1:
        blocks = blocks[:3]
    pbanks = [bank[0], bank[1]]
    mbanks = [bank[2], bank[3], bank[4], bank[5], bank[6]]
    pk = [0]
    mk = [0]

    def nb():
        b = mbanks[mk[0] % len(mbanks)]
        mk[0] += 1
        return b

    xtiles = {}

    def p1_load(bi):
        s, j = blocks[bi]
        row0 = (0 if s == 0 else SP_) + j * 512
        xtiles[bi] = [xr.load(row0 + t * 128) for t in range(4)]
        S.dma(SP, tab_ds[bi % 3], tabA[bi % 3].t[:, :, :], tabs_d.t[(0 if s == 0 else NBP) + j], reads=[tabs_d], writes=[tabA[bi % 3]])

    def p1_prologue(bi):
        s, j = blocks[bi]
        for t in range(4):
            token_stats_scale(xtiles[bi][t], junk, ssb, (bi * 4 + t) % 8)
        transpose_to_h(xtiles[bi], hA[bi % 2], s, pbanks, pk)

    def p1_main(bi):
        s, j = blocks[bi]
        h = hA[bi % 2]
        tab = tabA[bi % 3]
        for m in range(4):
            bk = nb()
            mm_group(bk, bk.t[:, :], [(wa.t[:, m, kc, :], h.t[:, kc, :]) for kc in range(16)], [wa, h])
            S.op(DVE, lambda m=m, bk=bk: V.tensor_copy(out=kvc32.t[:, m, :], in_=bk.t[:, :]), reads=[bk], writes=[kvc32])
            S.op(ACT, lambda m=m: A.activation(out=sqkv.t[:, m, :], in_=kvc32.t[:, m, :], func=AF.Square), reads=[kvc32], writes=[sqkv])
        if KPART < 2:
            return
        bkA = nb()
        mm_group(bkA, bkA.t[:, :], [(wa.t[:, 4, kc, :], h.t[:, kc, :]) for kc in range(16)], [wa, h])
        bkB = nb()
        mm_group(bkB, bkB.t[:, :], [(wa.t[:, 5, kc, :], h.t[:, kc, :]) for kc in range(16)], [wa, h])
        S.op(DVE, lambda: V.tensor_tensor(out=t1.t[:, :], in0=bkA.t[:, :], in1=tab.t[:, 0, :], op=ALU.mult), reads=[bkA, tab], writes=[t1])
        S.op(DVE, lambda: V.tensor_tensor(out=t2.t[:, :], in0=bkB.t[:, :], in1=tab.t[:, 1, :], op=ALU.mult), reads=[bkB, tab], writes=[t2])
        S.op(POOL, lambda: G.tensor_tensor(out=krd.t[:, :], in0=t1.t[:, :], in1=t2.t[:, :], op=ALU.add), reads=[t1, t2], writes=[krd])
        if KPART < 3:
            return
        bk = nb()
        mm_group(bk, bk.t[:, :], [(ones.t[:, :], sqkv.t[:, m, :]) for m in range(4)], [ones, sqkv])
        rstd_from_bank(bk, 512, rk32)
        for m in range(4):
            S.op(DVE, lambda m=m: V.scalar_tensor_tensor(out=kvn.t[:, m, :], in0=kvc32.t[:, m, :], scalar=vecs.t[:, V_GKV + m:V_GKV + m + 1],
                                                         in1=rk32.t[:, :], op0=ALU.mult, op1=ALU.mult),
                 reads=[kvc32, rk32, vecs], writes=[kvn])
        if KPART < 4:
            return
        for hh in range(8):
            bk = nb()
            mm_group(bk, bk.t[:, :], [(wuk.t[:, hh, m, :], kvn.t[:, m, :]) for m in range(4)], [wuk, kvn])
            evac_copy(hh, kT_out.t[:, hh, :], bk.t[:, :], [bk], [kT_out])
        if KPART < 5:
            return
        for t in range(4):
            for c in range(2):
                bk = nb()
                mm_group(bk, bk.t[:, :], [(kvn.t[:, m, t * 128:(t + 1) * 128], wuv.t[:, m, c * 512:(c + 1) * 512]) for m in range(4)], [wuv, kvn])
                evac_copy(t * 2 + c, v_out.t[:, 4 * c:4 * c + 4, t, :], bk.t[:, :].rearrange("p (h d) -> p h d", d=128), [bk], [v_out])

    def p1_store(bi):
        s, j = blocks[bi]
        S.dma(POOL, st_k, KT[s].t[:, :, j * 512:(j + 1) * 512].rearrange("h d t -> d h t"), kT_out.t[:, :, :], reads=[kT_out], writes=[KT[s]])
        S.dma(POOL, st_pe, KPE[s].t[:, j * 512:(j + 1) * 512], krd.t[:, :], reads=[krd], writes=[KPE[s]])
        S.dma(POOL, st_v, VV[s].t[:, :, 4 * j:4 * j + 4, :].rearrange("h p c d -> p h c d"), v_out.t[:, :, :, :], reads=[v_out], writes=[VV[s]])

    nblk = len(blocks)
    p1_load(0)
    if nblk > 1:
        p1_load(1)
    p1_prologue(0)
    for bi in range(nblk):
        if bi + 2 < nblk:
            p1_load(bi + 2)
        if bi + 1 < nblk:
            p1_prologue(bi + 1)
        if SKIP < 2:
            p1_main(bi)
        if SKIP < 1:
            p1_store(bi)
    if stage == 1:
        S.barrier()
        dbg_dump("h0", hA[0], [128, 16, 512], BF16)
        dbg_dump("KT0", (KT[0].t[:, :, 0:1536], KT[0]), [8, 128, 1536], BF16)
        dbg_dump("KPE0", (KPE[0].t[:, 0:1536], KPE[0]), [128, 1536], BF16)
        dbg_dump("VV0", (VV[0].t[:, :, 0:12, :], VV[0]), [8, 128, 12, 128], BF16)
        S.barrier()
        return nc, dbg_out

    S.mark('p1b')
    o = P1 + 96 * KB
    wkna = S.tile("wkna", [128, 8, 16, 128], BF16, o); o += 32 * KB
    wvna = S.tile("wvna", [128, 16, 1024], BF16, o); o += 32 * KB
    kna_out = S.tile("kna_out", [128, 8, 512], BF16, o); o += 8 * KB
    vna_out = S.tile("vna_out", [128, 4, 1024], BF16, o); o += 8 * KB
    wds2 = S.dsem("w1b")
    S.dma(SP, wds2, wkna.t[:, :, :, :], wbf["WKNA"].t.rearrange("m p k n -> p m k n"), reads=[wbf["WKNA"]], writes=[wkna])
    S.dma(SP, wds2, wvna.t[:, :, :], wbf["WVNA"].t, reads=[wbf["WVNA"]], writes=[wvna])
    st_kn, st_vn = S.dsem("st_kn"), S.dsem("st_vn")
    nblocks = [(0, j) for j in range(NAP // 512)] + [(1, j) for j in range(NAS // 512)]
    if stage == 2:
        nblocks = nblocks[:5]
    xt2 = {}

    def p1b_load(bi):
        s, j = nblocks[bi]
        row0 = (0 if s == 0 else SP_) + j * 512
        xt2[bi] = [xr.load(row0 + t * 128) for t in range(4)]

    def p1b_prologue(bi):
        s, j = nblocks[bi]
        for t in range(4):
            token_stats_scale(xt2[bi][t], junk, ssb, (bi * 4 + t) % 8)
        transpose_to_h(xt2[bi], hA[bi % 2], s, pbanks, pk)

    def p1b_main(bi):
        s, j = nblocks[bi]
        h = hA[bi % 2]
        for hh in range(8):
            bk = nb()
            mm_group(bk, bk.t[:, :], [(wkna.t[:, hh, kc, :], h.t[:, kc, :]) for kc in range(16)], [wkna, h])
            evac_copy(hh, kna_out.t[:, hh, :], bk.t[:, :], [bk], [kna_out])
        for t in range(4):
            for c in range(2):
                bk = nb()
                mm_group(bk, bk.t[:, :], [(h.t[:, kc, t * 128:(t + 1) * 128], wvna.t[:, kc, c * 512:(c + 1) * 512]) for kc in range(16)], [wvna, h])
                evac_copy(t * 2 + c, vna_out.t[:, t, c * 512:(c + 1) * 512], bk.t[:, :], [bk], [vna_out])
        S.dma(POOL, st_kn, KNA[s].t[:, :, j * 512:(j + 1) * 512].rearrange("h d t -> d h t"), kna_out.t[:, :, :], reads=[kna_out], writes=[KNA[s]])
        S.dma(POOL, st_vn, VNA[s].t[j * 512:(j + 1) * 512, :].rearrange("(t p) n -> p t n", p=128), vna_out.t[:, :, :], reads=[vna_out], writes=[VNA[s]])

    nnb = len(nblocks)
    p1b_load(0)
    p1b_load(1)
    p1b_prologue(0)
    for bi in range(nnb):
        if bi + 2 < nnb:
            p1b_load(bi + 2)
        if bi + 1 < nnb:
            p1b_prologue(bi + 1)
        p1b_main(bi)

    S.mark('modB')
    mod_phase(list(range(8, 24)))
    for s in range(2):
        derive(s, 1)
    S.barrier()

    Q = CONST_END
    xT = S.tile("xT", [128, 16, 512], F32, Q)
    h2 = S.tile("h2", [128, 16, 512], BF16, Q + 32 * KB)
    HS = Q + 32 * KB
    sq = S.tile("sq", [128, 16, 512], BF16, Q + 48 * KB)
    rA = S.tile("rA", [128, 512], F32, Q + 64 * KB)
    rB = S.tile("rB", [128, 512], F32, Q + 66 * KB)
    tab2 = S.tile("tab2", [128, 2, 512], F32, Q + 68 * KB)
    tmp = [S.tile("tmp%d" % i, [128, 512], F32, Q + (72 + 2 * i) * KB) for i in range(2)]
    junk2 = S.tile("junk2", [128, D], BF16, Q + 72 * KB)
    ssb2 = S.tile("ssb2", [128, 8], F32, BASE + 6 * KB - 64)
    NWR = 4
    WR = Q + 76 * KB
    wr_pair = [S.tile("wrp%d" % i, [128, 2, 16, 128], BF16, WR + i * 8 * KB) for i in range(NWR)]
    wr_uq = [S.tile("wru%d" % i, [128, 8, 4, 128], BF16, WR + i * 8 * KB) for i in range(NWR)]
    wr_ds = [S.dsem("wr%d" % i) for i in range(NWR)]
    wrk = [0]
    AR = Q + 108 * KB
    assert AR + 93 * KB <= BASE + 207 * KB
    xr2 = XRing(AR, 4, "b")
    Kwin = S.tile("Kwin", [128, 8, 896], BF16, AR)
    Vwin = S.tile("Vwin", [128, 7, 1024], BF16, AR + 14 * KB)
    kv_ds = [S.dsem("kv%d" % i) for i in range(NKV)]
    PER = SCK // 128
    Ksl = [S.tile("Ksl%d" % i, [128, SCK], BF16, AR + i * 6 * KB) for i in range(NKV)]
    Pesl = [S.tile("Pesl%d" % i, [128, SCK], BF16, AR + i * 6 * KB + 2 * KB) for i in range(NKV)]
    Vsl = [S.tile("Vsl%d" % i, [128, PER, 128], BF16, AR + i * 6 * KB + 4 * KB) for i in range(NKV)]
    merged = S.tile("merged", [128, 16, 512], BF16, AR)
    aT = S.tile("aT", [128, NFF, 512], BF16, AR)
    qc32 = S.tile("qc32", [128, 4, 512], F32, AR + 36 * KB)
    qn = S.tile("qn", [128, 4, 512], BF16, AR + 44 * KB)
    ona32 = S.tile("ona32", [128, 8, 512], F32, AR + 32 * KB)
    omla32 = S.tile("omla32", [128, 8, 512], F32, AR + 48 * KB)
    qna = S.tile("qna", [128, 8, 512], BF16, AR + 64 * KB)
    qnope = S.tile("qnope", [128, 8, 512], BF16, AR + 72 * KB)
    qrope = S.tile("qrope", [128, 8, 512], BF16, AR + 80 * KB)
    wd = [S.tile("wd%d" % i, [128, NFF, 128], BF16, AR + 44 * KB + i * 11 * KB) for i in range(2)]
    wd_ds = [S.dsem("wd%d" % i) for i in range(2)]
    ytok = [S.tile("ytok%d" % i, [128, D], F32, AR + i * 8 * KB) for i in range(2)]
    y_ds = [S.dsem("y%d" % i) for i in range(2)]
    nbias = [S.tile("nbias%d" % i, [128, 896], F32, HS + i * 3584) for i in range(2)]
    nb_ds = [S.dsem("nbias%d" % i) for i in range(2)]
    sbna = S.tile("sbna", [128, 896], F32, HS + 7168)
    pTna = [S.tile("pTna%d" % i, [128, 896], BF16, HS + 10752 + i * 1792) for i in range(2)]
    rlna = S.tile("rlna", [128, 128], F32, HS + 14336)
    pT = [S.tile("pT%d" % i, [128, 512], BF16, HS + i * KB) for i in range(4)]
    acc = S.tile("acc", [128, 512], F32, HS + 4 * KB)
    accb = S.tile("accb", [128, 512], BF16, HS + 6 * KB)
    rl = S.tile("rl", [128, 512], F32, HS + 7 * KB)
    accp = S.tile("accp", [128, 512], F32, HS + 9 * KB)
    tab2_ds, kw_ds, vw_ds = S.dsem("tab2"), S.dsem("kwin"), S.dsem("vwin")
    QS_NA = 128.0 ** -0.5
    QS_MLA = 192.0 ** -0.5

    def wr_next():
        i = wrk[0] % NWR
        wrk[0] += 1
        return i

    def ones_rstd(chunks, dim, rt):
        bk = nb()
        mm_group(bk, bk.t[:, :], [(ones.t[:, :], sq.t[:, c, :]) for c in chunks], [ones, sq])
        rstd_from_bank(bk, dim, rt)

    first_block = [True]
    own_blocks = [(0, bb) for bb in range(OWNP // 512)] + [(1, bb) for bb in range(OWNS // 512)]
    if stage == 3:
        own_blocks = own_blocks[:1]
    for (s, bb) in own_blocks:
        jrot = 1 + bb
        L = SP_ if s == 0 else SS_
        row0 = (0 if s == 0 else SP_) + jrot * 512
        nt_seg = (OWNP if s == 0 else OWNS) // 128
        S.mark('A')
        S.dma(SP, tab2_ds, tab2.t[:, :, :], tabs_d.t[(0 if s == 0 else NBP) + jrot], reads=[tabs_d], writes=[tab2])
        xts = [xr2.load(row0 + t * 128) for t in range(4)]
        for kc in range(16):
            bk = pbanks[pk[0] % 2]
            pk[0] += 1
            for t in range(4):
                S.op(PE, lambda t=t, kc=kc, bk=bk: T.transpose(bk.t[:, t * 128:(t + 1) * 128], xts[t].t[:, kc * 128:(kc + 1) * 128], ident.t[:, :]),
                     reads=[xts[t], ident], writes=[bk] if t == 0 else [], inc=(t == 3))
            evac_copy(kc, xT.t[:, kc, :], bk.t[:, :], [bk], [xT])
        for t in range(4):
            token_stats_scale(xts[t], junk2, ssb2, t)
        transpose_to_h(xts, h2, s, pbanks, pk)
        for m in range(12):
            if m % 2 == 0:
                wi = wr_next()
                S.dma(SP, wr_ds[wi], wr_pair[wi].t[:, :, :, :], wbf["WQ"].t[m:m + 2].rearrange("m p k n -> p m k n"), reads=[wbf["WQ"]], writes=[wr_pair[wi]])
            w = wr_pair[wi]
            bk = nb()
            mm_group(bk, bk.t[:, :], [(w.t[:, m % 2, kc, :], h2.t[:, kc, :]) for kc in range(16)], [w, h2])
            if m < 8:
                evac_copy(m, qna.t[:, m, :], bk.t[:, :], [bk], [qna], scale=QS_NA)
            else:
                S.op(DVE, lambda m=m, bk=bk: V.tensor_copy(out=qc32.t[:, m - 8, :], in_=bk.t[:, :]), reads=[bk], writes=[qc32])
                S.op(ACT, lambda m=m: A.activation(out=sq.t[:, m - 8, :], in_=qc32.t[:, m - 8, :], func=AF.Square), reads=[qc32], writes=[sq])
        ones_rstd(range(4), 512, rA)
        for m in range(4):
            S.op(DVE, lambda m=m: V.scalar_tensor_tensor(out=qn.t[:, m, :], in0=qc32.t[:, m, :], scalar=vecs.t[:, V_GQ + m:V_GQ + m + 1],
                                                         in1=rA.t[:, :], op0=ALU.mult, op1=ALU.mult), reads=[qc32, rA, vecs], writes=[qn])
        wu = []
        for i in range(2):
            wi = wr_next()
            S.dma(SP, wr_ds[wi], wr_uq[wi].t[:, :, :, :], wbf["WUQ"].t[8 * i:8 * i + 8].rearrange("m p k n -> p m k n"), reads=[wbf["WUQ"]], writes=[wr_uq[wi]])
            wu.append(wr_uq[wi])
        for hh in range(8):
            bk = nb()
            mm_group(bk, bk.t[:, :], [(wu[0].t[:, hh, m, :], qn.t[:, m, :]) for m in range(4)], [wu[0], qn])
            evac_copy(hh, qnope.t[:, hh, :], bk.t[:, :], [bk], [qnope], scale=QS_MLA)
        if first_block[0]:
            first_block[0] = False
            S.op(POOL, lambda: G.memset(qrope.t[:, :, :], 0.0), writes=[qrope])
        for hh in range(8):
            pr, e = hh // 2, hh % 2
            bkA = nb()
            mm_group(bkA, bkA.t[0:64, :], [(wu[1].t[:, pr, m, 64 * e:64 * e + 64], qn.t[:, m, :]) for m in range(4)], [wu[1], qn])
            bkB = nb()
            mm_group(bkB, bkB.t[0:64, :], [(wu[1].t[:, 4 + pr, m, 64 * e:64 * e + 64], qn.t[:, m, :]) for m in range(4)], [wu[1], qn])
            S.op(DVE, lambda bkA=bkA: V.scalar_tensor_tensor(out=tmp[0].t[0:64, :], in0=bkA.t[0:64, :], scalar=QS_MLA, in1=tab2.t[0:64, 0, :], op0=ALU.mult, op1=ALU.mult),
                 reads=[bkA, tab2], writes=[tmp[0]])
            S.op(DVE, lambda bkB=bkB: V.scalar_tensor_tensor(out=tmp[1].t[0:64, :], in0=bkB.t[0:64, :], scalar=QS_MLA, in1=tab2.t[0:64, 1, :], op0=ALU.mult, op1=ALU.mult),
                 reads=[bkB, tab2], writes=[tmp[1]])
            S.op(POOL, lambda hh=hh: G.tensor_tensor(out=qrope.t[0:64, hh, :], in0=tmp[0].t[0:64, :], in1=tmp[1].t[0:64, :], op=ALU.add),
                 reads=[tmp[0], tmp[1]], writes=[qrope])
        if KPART >= 2:
            S.mark('B')
            for qt in range(4):
                ti = bb * 4 + qt
                slot = _slot_of(ti, nt_seg)
                k0 = 128 + 128 * ti
                S.dma(SP, kw_ds, Kwin.t[:, :, :], KNA[s].t[:, :, k0:k0 + 896].rearrange("h d t -> d h t"), reads=[KNA[s]], writes=[Kwin])
                S.dma(SP, vw_ds, Vwin.t[:, :, :], VNA[s].t[k0:k0 + 896, :].rearrange("(c p) n -> p c n", p=128), reads=[VNA[s]], writes=[Vwin])
                for hh in range(8):
                    i2 = hh % 2
                    S.dma(SP, nb_ds[i2], nbias[i2].t[:, :], nab_d.t[s, slot, hh], reads=[nab_d], writes=[nbias[i2]])
                    bS = [bank[2 * i2], bank[2 * i2 + 1]]
                    ps2 = ps_all[:, 2 * i2:2 * i2 + 2, :].rearrange("p a b -> p (a b)")
                    for c in range(7):
                        S.op(PE, lambda c=c, hh=hh, ps2=ps2: T.matmul(ps2[:, c * 128:(c + 1) * 128], Kwin.t[:, hh, c * 128:(c + 1) * 128],
                                                                      qna.t[:, hh, qt * 128:(qt + 1) * 128], start=True, stop=True),
                             reads=[Kwin, qna] if c == 0 else [], writes=bS if c == 0 else [], inc=(c == 6))
                    S.op(DVE, lambda ps2=ps2, i2=i2: V.tensor_tensor(out=sbna.t[:, :], in0=ps2[:, 0:896], in1=nbias[i2].t[:, :], op=ALU.add),
                         reads=bS + [nbias[i2]], writes=[sbna])
                    S.op(ACT, lambda i2=i2: A.activation(out=pTna[i2].t[:, :], in_=sbna.t[:, :], func=AF.Exp), reads=[sbna], writes=[pTna[i2]])
                    bO = bank[4 + i2]
                    bL = bank[6 + i2]
                    mm_group(bO, bO.t[:, 0:128], [(Vwin.t[:, c, hh * 128:(hh + 1) * 128], pTna[i2].t[:, c * 128:(c + 1) * 128]) for c in range(7)], [Vwin, pTna[i2]])
                    mm_group(bL, bL.t[:, 0:128], [(ones.t[:, :], pTna[i2].t[:, c * 128:(c + 1) * 128]) for c in range(7)], [ones, pTna[i2]])
                    S.op(DVE, lambda bL=bL: V.reciprocal(out=rlna.t[:, :], in_=bL.t[:, 0:128]), reads=[bL], writes=[rlna])
                    S.op(DVE, lambda bO=bO, hh=hh, qt=qt: V.tensor_tensor(out=ona32.t[:, hh, qt * 128:(qt + 1) * 128], in0=bO.t[:, 0:128], in1=rlna.t[:, :], op=ALU.mult),
                         reads=[bO, rlna], writes=[ona32])
            for hh in range(8):
                S.op(ACT, lambda hh=hh: A.activation(out=sq.t[:, hh, :], in_=ona32.t[:, hh, :], func=AF.Square), reads=[ona32], writes=[sq])
        if KPART >= 3:
            S.mark('C')
            nch = L // 128
            kvk = [0]
            for hh in range(8):
                e, pr = hh % 2, hh // 2
                bO = bank[3 + hh % 2]
                pend = None

                def emit_pv(p, hh=hh, bO=bO):
                    g, si, c = p
                    S.op(PE, lambda: T.matmul(bO.t[:, :], Vsl[si].t[:, c, :], pT[g % 4].t[:, :], start=(g == 0), stop=(g == nch - 1)),
                         reads=[Vsl[si], pT[g % 4]], writes=[bO] if g in (0, nch - 1) else [], inc=True)

                for g in range(nch):
                    c = g % PER
                    if c == 0:
                        si = kvk[0] % NKV
                        kvk[0] += 1
                        sc_ = g // PER
                        S.dma(SP, kv_ds[si], Ksl[si].t[:, :], KT[s].t[hh, :, sc_ * SCK:(sc_ + 1) * SCK], reads=[KT[s]], writes=[Ksl[si]])
                        S.dma(SP, kv_ds[si], Pesl[si].t[:, :], KPE[s].t[:, sc_ * SCK:(sc_ + 1) * SCK], reads=[KPE[s]], writes=[Pesl[si]])
                        S.dma(SP, kv_ds[si], Vsl[si].t[:, :, :], VV[s].t[hh, :, sc_ * PER:(sc_ + 1) * PER, :], reads=[VV[s]], writes=[Vsl[si]])
                    bS = bank[g % 3]
                    S.op(PE, lambda si=si, c=c, bS=bS: T.matmul(bS.t[:, :], Ksl[si].t[:, c * 128:(c + 1) * 128], qnope.t[:, hh, :], start=True, stop=False),
                         reads=[Ksl[si], qnope], writes=[bS], inc=False)
                    S.op(PE, lambda si=si, c=c, bS=bS: T.matmul(bS.t[:, :], Pesl[si].t[:, c * 128:(c + 1) * 128],
                                                                qrope.t[:, hh, :], start=False, stop=True),
                         reads=[Pesl[si], qrope], writes=[], inc=True)
                    if pend is not None:
                        emit_pv(pend)
                    S.op(ACT, lambda g=g, bS=bS: A.activation(out=pT[g % 4].t[:, :], in_=bS.t[:, :], func=AF.Exp), reads=[bS], writes=[pT[g % 4]])
                    if g == 0:
                        S.op(DVE, lambda g=g: V.tensor_copy(out=acc.t[:, :], in_=pT[g % 4].t[:, :]), reads=[pT[g % 4]], writes=[acc])
                    elif g == 1:
                        S.op(POOL, lambda g=g: G.tensor_copy(out=accp.t[:, :], in_=pT[g % 4].t[:, :]), reads=[pT[g % 4]], writes=[accp])
                    elif g % 2 == 0:
                        S.op(DVE, lambda g=g: V.tensor_tensor(out=acc.t[:, :], in0=acc.t[:, :], in1=pT[g % 4].t[:, :], op=ALU.add),
                             reads=[pT[g % 4], acc], writes=[acc])
                    else:
                        S.op(POOL, lambda g=g: G.tensor_tensor(out=accp.t[:, :], in0=accp.t[:, :], in1=pT[g % 4].t[:, :], op=ALU.add),
                             reads=[pT[g % 4], accp], writes=[accp])
                    pend = (g, si, c)
                emit_pv(pend)
                S.op(DVE, lambda: V.tensor_tensor(out=accb.t[:, :], in0=acc.t[:, :], in1=accp.t[:, :], op=ALU.add), reads=[acc, accp], writes=[accb])
                bL = bank[5]
                mm_group(bL, bL.t[:, :], [(ones.t[:, :], accb.t[:, :])], [ones, accb])
                S.op(DVE, lambda bL=bL: V.reciprocal(out=rl.t[:, :], in_=bL.t[:, :]), reads=[bL], writes=[rl])
                S.op(DVE, lambda bO=bO, hh=hh: V.tensor_tensor(out=omla32.t[:, hh, :], in0=bO.t[:, :], in1=rl.t[:, :], op=ALU.mult),
                     reads=[bO, rl], writes=[omla32])
                S.op(ACT, lambda hh=hh: A.activation(out=sq.t[:, 8 + hh, :], in_=omla32.t[:, hh, :], func=AF.Square), reads=[omla32], writes=[sq])
        if KPART >= 4:
            S.mark('D')
            ones_rstd(range(0, 8), 1024, rA)
            ones_rstd(range(8, 16), 1024, rB)
            for hh in range(8):
                S.op(DVE, lambda hh=hh: V.scalar_tensor_tensor(out=merged.t[:, hh, :], in0=ona32.t[:, hh, :], scalar=vecs.t[:, V_GNA + hh:V_GNA + hh + 1],
                                                               in1=rA.t[:, :], op0=ALU.mult, op1=ALU.mult), reads=[ona32, rA, vecs], writes=[merged])
                S.op(DVE, lambda hh=hh: V.scalar_tensor_tensor(out=merged.t[:, 8 + hh, :], in0=omla32.t[:, hh, :], scalar=vecs.t[:, V_GMLA + hh:V_GMLA + hh + 1],
                                                               in1=rB.t[:, :], op0=ALU.mult, op1=ALU.mult), reads=[omla32, rB, vecs], writes=[merged])
            for n in range(16):
                if n % 2 == 0:
                    wi = wr_next()
                    S.dma(SP, wr_ds[wi], wr_pair[wi].t[:, :, :, :], wbf["WO"].t[n:n + 2].rearrange("m p k n -> p m k n"), reads=[wbf["WO"]], writes=[wr_pair[wi]])
                w = wr_pair[wi]
                bk = nb()
                mm_group(bk, bk.t[:, :], [(w.t[:, n % 2, kc, :], merged.t[:, kc, :]) for kc in range(16)], [w, merged])
                S.op(DVE, lambda n=n, bk=bk: V.scalar_tensor_tensor(out=xT.t[:, n, :], in0=bk.t[:, :], scalar=gt1(s, n), in1=xT.t[:, n, :], op0=ALU.mult, op1=ALU.add),
                     reads=[bk, xT, der], writes=[xT])
                S.op(ACT, lambda n=n: A.activation(out=sq.t[:, n, :], in_=xT.t[:, n, :], func=AF.Square), reads=[xT], writes=[sq])
            ones_rstd(range(16), 2048, rA)
            for kc in range(16):
                tt = tmp[kc % 2]
                S.op(DVE, lambda kc=kc, tt=tt: V.scalar_tensor_tensor(out=tt.t[:, :], in0=xT.t[:, kc, :], scalar=gs2(s, kc), in1=rA.t[:, :], op0=ALU.mult, op1=ALU.mult),
                     reads=[xT, rA, der], writes=[tt])
                S.op(ACT, lambda kc=kc, tt=tt: A.activation(out=h2.t[:, kc, :], in_=tt.t[:, :], func=AF.Identity, bias=sh2(s, kc), scale=1.0),
                     reads=[tt, der], writes=[h2])
        if KPART >= 5:
            S.mark('E')
            for c in range(NFF):
                wi = wr_next()
                w = wr_pair[wi]
                S.dma(SP, wr_ds[wi], w.t[:, :, :, :], wbf["WGU"].t[c], reads=[wbf["WGU"]], writes=[w])
                bG = nb()
                mm_group(bG, bG.t[:, :], [(w.t[:, 0, kc, :], h2.t[:, kc, :]) for kc in range(16)], [w, h2])
                bU = nb()
                mm_group(bU, bU.t[:, :], [(w.t[:, 1, kc, :], h2.t[:, kc, :]) for kc in range(16)], [w, h2])
                tt = tmp[c % 2]
                S.op(ACT, lambda bG=bG, tt=tt: A.activation(out=tt.t[:, :], in_=bG.t[:, :], func=AF.Silu), reads=[bG], writes=[tt])
                S.op(DVE, lambda c=c, bU=bU, tt=tt: V.tensor_tensor(out=aT.t[:, c, :], in0=bU.t[:, :], in1=tt.t[:, :], op=ALU.mult), reads=[bU, tt], writes=[aT])
            for n in range(16):
                w = wd[n % 2]
                S.dma(SP, wd_ds[n % 2], w.t[:, :, :], wbf["WD"].t[n], reads=[wbf["WD"]], writes=[w])
                bk = nb()
                mm_group(bk, bk.t[:, :], [(w.t[:, c, :], aT.t[:, c, :]) for c in range(NFF)], [w, aT])
                S.op(DVE, lambda n=n, bk=bk: V.scalar_tensor_tensor(out=xT.t[:, n, :], in0=bk.t[:, :], scalar=gt2(s, n), in1=xT.t[:, n, :], op0=ALU.mult, op1=ALU.add),
                     reads=[bk, xT, der], writes=[xT])
                S.op(ACT, lambda n=n: A.activation(out=sq.t[:, n, :], in_=xT.t[:, n, :], func=AF.Square), reads=[xT], writes=[sq])
        S.mark('F')
        ones_rstd(range(16), 2048, rA)
        for n in range(16):
            S.op(DVE, lambda n=n: V.scalar_tensor_tensor(out=xT.t[:, n, :], in0=xT.t[:, n, :], scalar=vecs.t[:, V_GFIN + n:V_GFIN + n + 1], in1=rA.t[:, :],
                                                         op0=ALU.mult, op1=ALU.mult), reads=[xT, rA, vecs], writes=[xT])
        for t in range(4):
            yt = ytok[t % 2]
            for g4 in range(4):
                bk = pbanks[pk[0] % 2]
                pk[0] += 1
                for i in range(4):
                    S.op(PE, lambda i=i, g4=g4, t=t, bk=bk: T.transpose(bk.t[:, i * 128:(i + 1) * 128], xT.t[:, g4 * 4 + i, t * 128:(t + 1) * 128], ident.t[:, :]),
                         reads=[xT, ident], writes=[bk] if i == 0 else [], inc=(i == 3))
                evac_copy(g4, yt.t[:, g4 * 512:(g4 + 1) * 512], bk.t[:, :], [bk], [yt])
            r0 = bb * 512 + t * 128
            S.dma(POOL, y_ds[t % 2], yout[s].t[r0:r0 + 128, :], yt.t[:, :], reads=[yt], writes=[yout[s]])
    S.barrier()
    S.mark('end')
    dbg_out['marks'] = S.marks
    return nc, dbg_out


def kernel(x_prompt, x_sample, c_prompt, c_sample, w_ada, b_ada, g_attn, w_in, rpb, g_q, w_uq, g_kv, w_ukv,
           g_out_na, g_out_mla, w_o, g_ffn, w_gate, w_up, w_down, g_final):
    inp = dict(x_prompt=x_prompt, x_sample=x_sample, c_prompt=c_prompt, c_sample=c_sample, w_ada=w_ada, b_ada=b_ada,
               g_attn=g_attn, w_in=w_in, rpb=rpb, g_q=g_q, w_uq=w_uq, g_kv=g_kv, w_ukv=w_ukv, g_out_na=g_out_na,
               g_out_mla=g_out_mla, w_o=w_o, g_ffn=g_ffn, w_gate=w_gate, w_up=w_up, w_down=w_down, g_final=g_final)
    maps = _prep(inp)
    nc, _ = build()
    res = run_bass_kernel_spmd(nc, maps, core_ids=list(range(NCORES)))
    yp = np.concatenate([np.asarray(r["y0"], np.float32) for r in res.results], axis=0)[None]
    ys = np.concatenate([np.asarray(r["y1"], np.float32) for r in res.results], axis=0)[None]
    return (yp, ys)
```

```python
import numpy as np
import concourse.bass as bass
import concourse.mybir as mybir
from concourse.bass_utils import run_bass_kernel_spmd

F32 = mybir.dt.float32
BF16 = mybir.dt.bfloat16
AF = mybir.ActivationFunctionType
ALU = mybir.AluOpType

NCORES = 8
D = 2048
KC = 16
SP_, SS_ = 8192, 16384
OWNP, OWNS = 1024, 2048
HALO = 512
NAP, NAS = OWNP + 2 * HALO, OWNS + 2 * HALO
NBP, NBS = SP_ // 512, SS_ // 512
DFF = 5632
NFF = DFF // 128
EPS = 1e-6
NEG = -30000.0
SCK = 1024
NKV = 5
import os
SKIP = int(os.environ.get('KSKIP', '0'))
KPART = int(os.environ.get('KPART', '9'))

V_BADA, V_GATTN, V_GFFN, V_GFIN, V_GQ, V_GKV, V_GNA, V_GMLA, V_CP, V_CS, NV = 0, 96, 112, 128, 144, 148, 152, 160, 168, 184, 200


class Eng:
    def __init__(self, name, h, sem, compute=True):
        self.name, self.h, self.sem, self.cnt, self.seen, self.compute = name, h, sem, 0, {}, compute


class DSem:
    def __init__(self, sem):
        self.sem, self.cnt = sem, 0


class Buf:
    def __init__(self, name, space, lo, hi):
        self.name, self.space, self.lo, self.hi = name, space, lo, hi
        self.w = None
        self.r = {}
        self.ov = [self]


class Tile:
    def __init__(self, t, buf):
        self.t, self.buf = t, buf


class Sched:
    def __init__(self, nc):
        self.nc = nc
        self.bufs = []
        self.dsems = []
        mk = lambda n, h, c=True: Eng(n, h, nc.alloc_semaphore("sem_" + n), c)
        self.PE = mk("pe", nc.tensor)
        self.ACT = mk("act", nc.scalar)
        self.DVE = mk("dve", nc.vector)
        self.POOL = mk("pool", nc.gpsimd)
        self.SP = mk("sp", nc.sync, False)
        self.engs = [self.PE, self.ACT, self.DVE, self.POOL, self.SP]
        self.nsem = 5
        self.npe = 0
        self.marks = []

    def dsem(self, name):
        d = DSem(self.nc.alloc_semaphore("ds_" + name))
        d.name = name
        self.dsems.append(d)
        self.nsem += 1
        return d

    def _reg(self, b):
        for o in self.bufs:
            if o.space == b.space and o.lo < b.hi and b.lo < o.hi:
                o.ov.append(b)
                b.ov.append(o)
        self.bufs.append(b)
        return b

    def tile(self, name, shape, dtype, off):
        esz = 2 if dtype == BF16 else 4
        n = 1
        for s in shape[1:]:
            n *= s
        t = self.nc.alloc_sbuf_tensor_at(name, list(shape), dtype, offset=int(off))
        return Tile(t, self._reg(Buf(name, "sb", int(off), int(off) + n * esz)))

    def dram(self, name, shape, dtype, kind):
        t = self.nc.dram_tensor(name, list(shape), dtype, kind=kind).ap()
        return Tile(t, self._reg(Buf(name, "dram_" + name, 0, 1)))

    def _wait(self, eng, ev):
        sem, val, _ = ev
        k = id(sem)
        if eng.seen.get(k, 0) >= val:
            return
        eng.seen[k] = val
        eng.h.wait_ge(sem, val)

    def _deps(self, eng, reads, writes, is_dma):
        for t in reads:
            for o in t.buf.ov:
                if o.w is not None:
                    if o.w[2] is eng and not is_dma and eng is self.PE:
                        continue
                    self._wait(eng, o.w)
                if o.space == "ps":
                    for e in o.r.values():
                        if e[2] is not eng:
                            self._wait(eng, e)
        for t in writes:
            for o in t.buf.ov:
                if o.w is not None and (is_dma or o.w[2] is not eng):
                    self._wait(eng, o.w)
                for e in o.r.values():
                    if is_dma or e[2] is not eng:
                        self._wait(eng, e)

    def mark(self, name):
        self.marks.append((name, self.npe))

    def op(self, eng, fn, reads=(), writes=(), inc=True):
        if eng is self.PE:
            self.npe += 1
        self._deps(eng, reads, writes, False)
        ins = fn()
        if inc:
            eng.cnt += 1
            ins.then_inc(eng.sem, 1)
            ev = (eng.sem, eng.cnt, eng)
        else:
            ev = (eng.sem, eng.cnt + 1, eng)
        for t in reads:
            t.buf.r[eng.name] = ev
        for t in writes:
            t.buf.w = ev
            t.buf.r = {}
        return ev

    def dma(self, q, ds, out, in_, reads=(), writes=()):
        self._deps(q, reads, writes, True)
        ds.cnt += 16
        q.h.dma_start(out=out, in_=in_).then_inc(ds.sem, 16)
        ev = (ds.sem, ds.cnt, None)
        for t in reads:
            t.buf.r[("d", id(ds))] = ev
        for t in writes:
            t.buf.w = ev
            t.buf.r = {}
        return ev

    def barrier(self):
        for e in self.engs:
            for o in self.engs:
                if o is not e and o.cnt > 0:
                    self._wait(e, (o.sem, o.cnt, o))
            for d in self.dsems:
                if d.cnt > 0:
                    self._wait(e, (d.sem, d.cnt, None))


def _SL(w):
    K, N = w.shape
    return np.ascontiguousarray(w.reshape(K // 128, 128, N // 128, 128).transpose(2, 1, 0, 3))


def _ML(w):
    K, N = w.shape
    return np.ascontiguousarray(w.reshape(K // 128, 128, N).transpose(1, 0, 2))


def _fm(v):
    return np.ascontiguousarray(v.reshape(-1, 128).T)


def _rope_tables(pos):
    inv = (np.float32(10000.0) ** (-(np.arange(0, 64, 2, dtype=np.float32)) / np.float32(64))).astype(np.float32)
    ang = pos.astype(np.float32)[:, None] * inv[None, :]
    ang = np.concatenate([ang, ang], axis=-1).astype(np.float32)
    c = np.cos(ang).astype(np.float32).T
    s = np.sin(ang).astype(np.float32).T
    s = np.concatenate([-s[:32], s[32:]], axis=0)
    return np.concatenate([c, c], 0), np.concatenate([s, s], 0)


def _na_bias_variant(rpb, r, rows):
    i = np.arange(14)[:, None, None, None]
    kc = np.arange(64)[None, :, None, None]
    qq = np.arange(2)[None, None, :, None]
    qc = np.arange(64)[None, None, None, :]
    kr = r - 6 + i
    qr = r + qq
    rs = np.clip(qr - 4, 0, rows - 8)
    cs = np.clip(qc - 8, 0, 64 - 16)
    valid = (kr >= 0) & (kr < rows) & (kr >= rs) & (kr < rs + 8) & (kc >= cs) & (kc < cs + 16)
    dr = np.clip(kr - qr + 7, 0, 14)
    dc = np.clip(kc - qc + 15, 0, 30)
    dr, dc, valid = np.broadcast_arrays(dr, dc, valid)
    g = rpb[:, dr, dc]
    g = np.where(valid[None], g, np.float32(NEG)).astype(np.float32)
    g = g.reshape(8, 7, 128, 128)
    return np.ascontiguousarray(g.transpose(0, 2, 1, 3).reshape(8, 128, 896))


def _slot_of(i, n):
    return 1 if i == 0 else 2 if i == 1 else 3 if i == n - 2 else 4 if i == n - 1 else 0


def _prep(inp, cores=None):
    f32 = np.float32
    w_in = np.asarray(inp["w_in"][0], f32)
    w_uq = np.asarray(inp["w_uq"][0], f32)
    w_ukv = np.asarray(inp["w_ukv"][0], f32)
    perm = (np.arange(64) + 32) % 64
    kpe = w_in[:, 4096:4160]
    WA = np.concatenate([w_in[:, 3584:4096], kpe, kpe, kpe[:, perm], kpe[:, perm]], axis=1)
    uk = w_ukv.reshape(512, 8, 256)[:, :, :128].reshape(512, 1024)
    uv = w_ukv.reshape(512, 8, 256)[:, :, 128:].reshape(512, 1024)
    uq = w_uq.reshape(512, 8, 192)
    uqn = uq[:, :, :128].reshape(512, 1024)
    uqr = uq[:, :, 128:].reshape(512, 512)
    uqp = uq[:, :, 128:][:, :, perm].reshape(512, 512)
    WQ = np.concatenate([w_in[:, 0:1024], w_in[:, 3072:3584]], axis=1)
    WUQ = np.concatenate([uqn, uqr, uqp], axis=1)
    wg = _SL(np.asarray(inp["w_gate"][0], f32))
    wu = _SL(np.asarray(inp["w_up"][0], f32))
    WGU = np.ascontiguousarray(np.stack([wg, wu], axis=2))
    vecs = np.zeros((128, NV), f32)
    vecs[:, V_BADA:V_BADA + 96] = _fm(np.asarray(inp["b_ada"][0], f32))
    vecs[:, V_GATTN:V_GATTN + 16] = _fm(np.asarray(inp["g_attn"][0], f32))
    vecs[:, V_GFFN:V_GFFN + 16] = _fm(np.asarray(inp["g_ffn"][0], f32))
    vecs[:, V_GFIN:V_GFIN + 16] = _fm(np.asarray(inp["g_final"], f32))
    vecs[:, V_GQ:V_GQ + 4] = _fm(np.asarray(inp["g_q"][0], f32))
    vecs[:, V_GKV:V_GKV + 4] = _fm(np.asarray(inp["g_kv"][0], f32))
    vecs[:, V_GNA:V_GNA + 8] = _fm(np.asarray(inp["g_out_na"][0], f32))
    vecs[:, V_GMLA:V_GMLA + 8] = _fm(np.asarray(inp["g_out_mla"][0], f32))
    vecs[:, V_CP:V_CP + 16] = _fm(np.asarray(inp["c_prompt"][0], f32))
    vecs[:, V_CS:V_CS + 16] = _fm(np.asarray(inp["c_sample"][0], f32))
    shared = {
        "vecs": vecs,
        "w_ada": np.ascontiguousarray(np.asarray(inp["w_ada"][0], f32)),
        "ident": np.eye(128, dtype=f32),
        "WA": _SL(WA), "WUK": _SL(uk), "WUV": _ML(uv),
        "WKNA": _SL(w_in[:, 1024:2048]), "WVNA": _ML(w_in[:, 2048:3072]),
        "WQ": _SL(WQ), "WUQ": _SL(WUQ), "WO": _SL(np.asarray(inp["w_o"][0], f32)),
        "WGU": WGU, "WD": _SL(np.asarray(inp["w_down"][0], f32)),
    }
    rpb = np.asarray(inp["rpb"][0], f32)
    xp = np.asarray(inp["x_prompt"][0], f32)
    xs = np.asarray(inp["x_sample"][0], f32)
    variants = {}

    def variant(r, rows):
        key = ("t", r) if r < 4 else ("b", rows - r) if r >= rows - 4 else ("i",)
        if key not in variants:
            variants[key] = _na_bias_variant(rpb, r, rows)
        return variants[key]

    maps = []
    for c in (range(NCORES) if cores is None else cores):
        sp0 = (c * OWNP - HALO) % SP_
        ss0 = (c * OWNS - HALO) % SS_
        xall = np.concatenate([np.roll(xp, -sp0, axis=0), np.roll(xs, -ss0, axis=0)], axis=0)
        pos = np.concatenate([(sp0 + np.arange(SP_)) % SP_, (ss0 + np.arange(SS_)) % SS_])
        c2, s2 = _rope_tables(pos)
        tab = np.stack([c2, s2], axis=1).reshape(128, 2, NBP + NBS, 512).transpose(2, 0, 1, 3)
        nab = np.zeros((2, 5, 8, 128, 896), f32)
        for sg, (n, rows, row0) in enumerate(((OWNP // 128, SP_ // 64, c * (OWNP // 64)), (OWNS // 128, SS_ // 64, c * (OWNS // 64)))):
            tiles = {0: 2, 1: 0, 2: 1, 3: n - 2, 4: n - 1}
            for slot, i in tiles.items():
                nab[sg, slot] = variant(row0 + 2 * i, rows)
        m = dict(shared)
        m["xall"] = xall
        m["tabs"] = np.ascontiguousarray(tab)
        m["nab"] = nab
        maps.append(m)
    return maps


def build(stage=99, dbg=False):
    nc = bass.Bass("TRN2", target_bir_lowering=False)
    S = Sched(nc)
    PE, ACT, DVE, POOL, SP = S.PE, S.ACT, S.DVE, S.POOL, S.SP
    T, V, G, A = nc.tensor, nc.vector, nc.gpsimd, nc.scalar
    BASE = 16512
    KB = 1024

    din = lambda n, sh: S.dram(n, sh, F32, "ExternalInput")
    xall = din("xall", [SP_ + SS_, D])
    vecs_d = din("vecs", [128, NV])
    wada_d = din("w_ada", [D, 6 * D])
    ident_d = din("ident", [128, 128])
    tabs_d = din("tabs", [NBP + NBS, 128, 2, 512])
    nab_d = din("nab", [2, 5, 8, 128, 896])
    wshapes = {"WA": [6, 128, 16, 128], "WUK": [8, 128, 4, 128], "WUV": [128, 4, 1024],
               "WKNA": [8, 128, 16, 128], "WVNA": [128, 16, 1024], "WQ": [12, 128, 16, 128],
               "WUQ": [16, 128, 4, 128], "WO": [16, 128, 16, 128], "WGU": [NFF, 128, 2, 16, 128],
               "WD": [16, 128, NFF, 128]}
    w32 = {k: din(k, sh) for k, sh in wshapes.items()}
    wbf = {k: S.dram(k + "_bf", sh, BF16, "Internal") for k, sh in wshapes.items()}
    KT = [S.dram("KT%d" % s, [8, 128, L], BF16, "Internal") for s, L in enumerate((SP_, SS_))]
    KPE = [S.dram("KPE%d" % s, [128, L], BF16, "Internal") for s, L in enumerate((SP_, SS_))]
    VV = [S.dram("VV%d" % s, [8, 128, L // 128, 128], BF16, "Internal") for s, L in enumerate((SP_, SS_))]
    KNA = [S.dram("KNA%d" % s, [8, 128, L], BF16, "Internal") for s, L in enumerate((NAP, NAS))]
    VNA = [S.dram("VNA%d" % s, [L, 1024], BF16, "Internal") for s, L in enumerate((NAP, NAS))]
    yout = [S.dram("y%d" % s, [L, D], F32, "ExternalOutput") for s, L in enumerate((OWNP, OWNS))]
    dbg_out = {}

    def dbg_dump(name, tile_, shape, dtype=F32):
        if not dbg:
            return
        o = S.dram("dbg_" + name, shape, dtype, "ExternalOutput")
        dbg_out[name] = o
        S.dma(SP, S.dsem("dbg_" + name), o.t, tile_.t.ap() if isinstance(tile_, Tile) else tile_[0], reads=[tile_ if isinstance(tile_, Tile) else tile_[1]], writes=[o])

    pending_casts = []

    def cast_weight(k, defer=False):
        ds = S.dsem("cast_" + k)
        n = 1
        for s_ in wshapes[k]:
            n *= s_
        rows = n // 2048
        src = w32[k].t
        dst = wbf[k].t
        names = "abcde"[:len(wshapes[k])]
        pat = " ".join(names)
        src2 = src.rearrange(f"{pat} -> ({pat})").rearrange("(r c) -> r c", c=2048)
        dst2 = dst.rearrange(f"{pat} -> ({pat})").rearrange("(r c) -> r c", c=2048)
        step = 2048
        pieces = list(range(0, rows, step))

        def piece(r0, last):
            r1 = min(rows, r0 + step)
            S.dma(POOL, ds, dst2[r0:r1, :], src2[r0:r1, :], reads=[w32[k]], writes=[])
            if last:
                wbf[k].buf.w = (ds.sem, ds.cnt, None)

        for r0 in pieces:
            fn = (lambda r0=r0: piece(r0, r0 == pieces[-1]))
            if defer:
                pending_casts.append(fn)
            else:
                fn()

    ps_all = nc.alloc_psum_tensor("ps_all", [128, 8, 512], F32)
    bank = [Tile(ps_all[:, i, :], S._reg(Buf("bank%d" % i, "ps", i, i + 1))) for i in range(8)]

    off = BASE
    ident = S.tile("ident", [128, 128], F32, off); off += 512
    ones = S.tile("ones", [128, 128], BF16, off); off += 256
    vecs = S.tile("vecs", [128, NV], F32, off); off += NV * 4
    mod = S.tile("mod", [128, 96, 2], F32, off); off += 768
    der = S.tile("der", [128, 2, 6, 16], F32, off); off += 768
    epsT = S.tile("epsT", [128, 1], F32, off); off += 32
    sc = S.tile("sc", [128, 16, 2], BF16, off); off += 64
    gq2 = S.tile("gq2", [128, 8], F32, off); off += 32
    CONST_END = BASE + 6 * KB
    assert off <= CONST_END
    cds = S.dsem("const")
    S.dma(SP, cds, ident.t[:, :], ident_d.t[:, :], reads=[ident_d], writes=[ident])
    S.dma(SP, cds, vecs.t[:, :], vecs_d.t[:, :], reads=[vecs_d], writes=[vecs])
    S.op(DVE, lambda: V.memset(ones.t[:, :], 1.0), writes=[ones])
    S.op(DVE, lambda: V.memset(epsT.t[:, :], EPS), writes=[epsT])
    for s in range(2):
        c0 = V_CP if s == 0 else V_CS
        S.op(ACT, lambda s=s, c0=c0: A.activation(out=sc.t[:, :, s], in_=vecs.t[:, c0:c0 + 16], func=AF.Silu),
             reads=[vecs], writes=[sc])

    P0 = CONST_END
    NWAD = 4
    wad = [S.tile("wad%d" % i, [128, 16, 512], BF16, P0 + i * 16 * KB) for i in range(NWAD)]
    wad_ds = [S.dsem("wad%d" % i) for i in range(NWAD)]
    psmod = bank[7]
    psmod_v = ps_all[:, 7, 0:192].rearrange("p (n s) -> p n s", s=2)
    mod_groups_done = [0]

    def mod_phase(groups):
        for g in groups:
            i = mod_groups_done[0] % NWAD
            mod_groups_done[0] += 1
            S.dma(POOL, wad_ds[i], wad[i].t[:, :, :],
                  wada_d.t[:, g * 512:(g + 1) * 512].rearrange("(kc p) n -> p kc n", p=128),
                  reads=[wada_d], writes=[wad[i]])
            for q in range(4):
                n = g * 4 + q
                for kc in range(16):
                    S.op(PE, lambda i=i, q=q, kc=kc, n=n: T.matmul(ps_all[:, 7, 2 * n:2 * n + 2], wad[i].t[:, kc, q * 128:(q + 1) * 128],
                                                                   sc.t[:, kc, :], start=(kc == 0), stop=(kc == 15)),
                         reads=[wad[i], sc] if kc == 0 else [], writes=[psmod] if kc == 0 else [], inc=(kc == 15))
                    S.npe -= 1
        for s in range(2):
            for g in groups:
                S.op(DVE, lambda s=s, g=g: V.tensor_tensor(out=mod.t[:, g * 4:(g + 1) * 4, s], in0=psmod_v[:, g * 4:(g + 1) * 4, s],
                                                           in1=vecs.t[:, V_BADA + g * 4:V_BADA + (g + 1) * 4], op=ALU.add),
                     reads=[psmod, vecs], writes=[mod])

    def derive(s, which):
        mv = lambda v: mod.t[:, v * 16:(v + 1) * 16, s]
        if which == 0:
            S.op(DVE, lambda: V.scalar_tensor_tensor(out=der.t[:, s, 0, :], in0=mv(1), scalar=1.0, in1=vecs.t[:, V_GATTN:V_GATTN + 16],
                                                     op0=ALU.add, op1=ALU.mult), reads=[mod, vecs], writes=[der])
            S.op(DVE, lambda: V.tensor_copy(out=der.t[:, s, 1, :], in_=mv(0)), reads=[mod], writes=[der])
        else:
            S.op(DVE, lambda: V.tensor_copy(out=der.t[:, s, 2, :], in_=mv(2)), reads=[mod], writes=[der])
            S.op(DVE, lambda: V.scalar_tensor_tensor(out=der.t[:, s, 3, :], in0=mv(4), scalar=1.0, in1=vecs.t[:, V_GFFN:V_GFFN + 16],
                                                     op0=ALU.add, op1=ALU.mult), reads=[mod, vecs], writes=[der])
            S.op(DVE, lambda: V.tensor_copy(out=der.t[:, s, 4, :], in_=mv(3)), reads=[mod], writes=[der])
            S.op(DVE, lambda: V.tensor_copy(out=der.t[:, s, 5, :], in_=mv(5)), reads=[mod], writes=[der])

    for k in ("WA", "WUK", "WUV"):
        cast_weight(k)
    mod_phase(list(range(0, 8)))
    for k in ("WKNA", "WVNA"):
        cast_weight(k)
    for k in ("WQ", "WUQ", "WO", "WGU", "WD"):
        cast_weight(k, defer=True)
    for s in range(2):
        derive(s, 0)
    gs1 = lambda s, kc: der.t[:, s, 0, kc:kc + 1]
    sh1 = lambda s, kc: der.t[:, s, 1, kc:kc + 1]
    gt1 = lambda s, kc: der.t[:, s, 2, kc:kc + 1]
    gs2 = lambda s, kc: der.t[:, s, 3, kc:kc + 1]
    sh2 = lambda s, kc: der.t[:, s, 4, kc:kc + 1]
    gt2 = lambda s, kc: der.t[:, s, 5, kc:kc + 1]
    if stage == 0:
        mod_phase(list(range(8, 24)))
        for s in range(2):
            derive(s, 1)
        dbg_dump("mod", mod, [128, 96, 2])
        dbg_dump("der", der, [128, 2, 6, 16])
        S.barrier()
        return nc, dbg_out

    def mm_group(bk, out_ap, pairs, reads):
        n = len(pairs)
        for i, (l, r) in enumerate(pairs):
            S.op(PE, lambda l=l, r=r, i=i: T.matmul(out_ap, l, r, start=(i == 0), stop=(i == n - 1)),
                 reads=reads if i == 0 else [], writes=[bk] if i == 0 else [], inc=(i == n - 1))

    class XRing:
        def __init__(self, base, nslots, tag):
            self.tiles = [S.tile("xt%s%d" % (tag, i), [128, D], F32, base + i * 8 * KB) for i in range(nslots)]
            self.ds = [S.dsem("xt%s%d" % (tag, i)) for i in range(nslots)]
            self.n = nslots
            self.k = 0

        def load(self, row0):
            i = self.k % self.n
            self.k += 1
            S.dma(SP, self.ds[i], self.tiles[i].t[:, :], xall.t[row0:row0 + 128, :], reads=[xall], writes=[self.tiles[i]])
            return self.tiles[i]

    def token_stats_scale(xt, junk, ssb, k):
        if isinstance(ssb, list):
            sb_ = ssb[k % len(ssb)]
            col = sb_.t[:, 0:1]
        else:
            sb_ = ssb
            col = ssb.t[:, k:k + 1]
        jk = junk[k % len(junk)] if isinstance(junk, list) else junk
        S.op(ACT, lambda: A.activation(out=jk.t[:, :], in_=xt.t[:, :], func=AF.Square, accum_out=col),
             reads=[xt], writes=[jk, sb_])
        S.op(ACT, lambda: A.activation(out=col, in_=col, func=AF.Sqrt, bias=epsT.t[:, 0:1], scale=1.0 / D),
             reads=[sb_, epsT], writes=[sb_])
        S.op(DVE, lambda: V.reciprocal(out=col, in_=col), reads=[sb_], writes=[sb_])
        S.op(DVE, lambda: V.tensor_scalar(out=xt.t[:, :], in0=xt.t[:, :], scalar1=col, scalar2=None, op0=ALU.mult),
             reads=[xt, sb_], writes=[xt])

    def transpose_to_h(xts, h, s, pbanks, pk):
        for kc in range(16):
            bk = pbanks[pk[0] % len(pbanks)]
            pk[0] += 1
            for t in range(4):
                S.op(PE, lambda t=t, kc=kc, bk=bk: T.transpose(bk.t[:, t * 128:(t + 1) * 128], xts[t].t[:, kc * 128:(kc + 1) * 128], ident.t[:, :]),
                     reads=[xts[t], ident], writes=[bk] if t == 0 else [], inc=(t == 3))
            if kc % 2 == 0:
                S.op(ACT, lambda kc=kc, bk=bk: A.activation(out=h.t[:, kc, :], in_=bk.t[:, :], func=AF.Identity, scale=gs1(s, kc), bias=sh1(s, kc)),
                     reads=[bk, der], writes=[h])
            else:
                S.op(DVE, lambda kc=kc, bk=bk: V.tensor_scalar(out=h.t[:, kc, :], in0=bk.t[:, :], scalar1=gs1(s, kc), scalar2=sh1(s, kc),
                                                                op0=ALU.mult, op1=ALU.add),
                     reads=[bk, der], writes=[h])

    def evac_copy(i, out_ap, in_ap, reads, writes, scale=None):
        if i % 2 == 0:
            if scale is None:
                S.op(ACT, lambda: A.copy(out=out_ap, in_=in_ap), reads=reads, writes=writes)
            else:
                S.op(ACT, lambda: A.mul(out=out_ap, in_=in_ap, mul=scale) if False else A.activation(out=out_ap, in_=in_ap, func=AF.Copy, scale=scale),
                     reads=reads, writes=writes)
        else:
            if scale is None:
                S.op(DVE, lambda: V.tensor_copy(out=out_ap, in_=in_ap), reads=reads, writes=writes)
            else:
                S.op(DVE, lambda: V.tensor_scalar(out=out_ap, in0=in_ap, scalar1=scale, scalar2=None, op0=ALU.mult), reads=reads, writes=writes)

    def rstd_from_bank(bk, dim, rt):
        S.op(ACT, lambda: A.activation(out=rt.t[:, :], in_=bk.t[:, :], func=AF.Sqrt, bias=epsT.t[:, 0:1], scale=1.0 / dim),
             reads=[bk, epsT], writes=[rt])
        S.op(DVE, lambda: V.reciprocal(out=rt.t[:, :], in_=rt.t[:, :]), reads=[rt], writes=[rt])

    S.mark('p1a')
    P1 = CONST_END
    o = P1
    xr = XRing(o, 8, "a"); o += 64 * KB
    hA = [S.tile("hA%d" % i, [128, 16, 512], BF16, o + i * 16 * KB) for i in range(2)]; o += 32 * KB
    wa = S.tile("wa", [128, 6, 16, 128], BF16, o); o += 24 * KB
    wuk = S.tile("wuk", [128, 8, 4, 128], BF16, o); o += 8 * KB
    wuv = S.tile("wuv", [128, 4, 1024], BF16, o); o += 8 * KB
    kvc32 = S.tile("kvc32", [128, 4, 512], F32, o); o += 8 * KB
    sqkv = S.tile("sqkv", [128, 4, 512], BF16, o); o += 4 * KB
    kvn = S.tile("kvn", [128, 4, 512], BF16, o); o += 4 * KB
    kT_out = S.tile("kT_out", [128, 8, 512], BF16, o); o += 8 * KB
    v_out = S.tile("v_out", [128, 8, 4, 128], BF16, o); o += 8 * KB
    t1 = S.tile("t1", [128, 512], F32, o); o += 2 * KB
    t2 = S.tile("t2", [128, 512], F32, o); o += 2 * KB
    krd = S.tile("krd", [128, 512], BF16, o); o += 1 * KB
    rk32 = S.tile("rk32", [128, 512], F32, o); o += 2 * KB
    tabA = [S.tile("tabA%d" % i, [128, 2, 512], F32, o + i * 4 * KB) for i in range(4)]; o += 16 * KB
    junk = [S.tile("junk%d" % i, [128, D], BF16, o + i * 4 * KB) for i in range(2)]; o += 8 * KB
    ssb = [S.tile("ssb%d" % i, [128, 1], F32, o + i * 32) for i in range(8)]; o += 256
    assert o <= BASE + 207 * KB, o
    tab_ds = [S.dsem("tabA%d" % i) for i in range(4)]
    wds = S.dsem("w1a")
    S.dma(SP, wds, wa.t[:, :, :, :], wbf["WA"].t.rearrange("m p k n -> p m k n"), reads=[wbf["WA"]], writes=[wa])
    S.dma(SP, wds, wuk.t[:, :, :, :], wbf["WUK"].t.rearrange("m p k n -> p m k n"), reads=[wbf["WUK"]], writes=[wuk])
    S.dma(SP, wds, wuv.t[:, :, :], wbf["WUV"].t, reads=[wbf["WUV"]], writes=[wuv])
    st_k, st_v, st_pe = S.dsem("st_k"), S.dsem("st_v"), S.dsem("st_pe")

    blocks = [(0, j) for j in range(NBP)] + [(1, j) for j in range(NBS)]
    if stage == 1:
        blocks = blocks[:3]
    pbanks = [bank[0], bank[1]]
    mbanks = [bank[2], bank[3], bank[4], bank[5], bank[6]]
    pk = [0]
    mk = [0]

    def nb():
        b = mbanks[mk[0] % len(mbanks)]
        mk[0] += 1
        return b

    xtiles = {}

    def p1_load(bi):
        s, j = blocks[bi]
        row0 = (0 if s == 0 else SP_) + j * 512
        xtiles[bi] = [xr.load(row0 + t * 128) for t in range(4)]
        S.dma(SP, tab_ds[bi % 4], tabA[bi % 4].t[:, :, :], tabs_d.t[(0 if s == 0 else NBP) + j], reads=[tabs_d], writes=[tabA[bi % 4]])

    def p1_stats(bi):
        for t in range(4):
            token_stats_scale(xtiles[bi][t], junk, ssb, (bi * 4 + t) % 8)

    def p1_trans(bi):
        s, j = blocks[bi]
        transpose_to_h(xtiles[bi], hA[bi % 2], s, pbanks, pk)

    def p1_main(bi):
        s, j = blocks[bi]
        h = hA[bi % 2]
        tab = tabA[bi % 4]
        for m in range(4):
            bk = nb()
            mm_group(bk, bk.t[:, :], [(wa.t[:, m, kc, :], h.t[:, kc, :]) for kc in range(16)], [wa, h])
            S.op(DVE, lambda m=m, bk=bk: V.tensor_copy(out=kvc32.t[:, m, :], in_=bk.t[:, :]), reads=[bk], writes=[kvc32])
            S.op(ACT, lambda m=m: A.activation(out=sqkv.t[:, m, :], in_=kvc32.t[:, m, :], func=AF.Square), reads=[kvc32], writes=[sqkv])
        if KPART < 2:
            return
        bkA = nb()
        mm_group(bkA, bkA.t[:, :], [(wa.t[:, 4, kc, :], h.t[:, kc, :]) for kc in range(16)], [wa, h])
        bkB = nb()
        mm_group(bkB, bkB.t[:, :], [(wa.t[:, 5, kc, :], h.t[:, kc, :]) for kc in range(16)], [wa, h])
        S.op(DVE, lambda: V.tensor_tensor(out=t1.t[:, :], in0=bkA.t[:, :], in1=tab.t[:, 0, :], op=ALU.mult), reads=[bkA, tab], writes=[t1])
        S.op(DVE, lambda: V.tensor_tensor(out=t2.t[:, :], in0=bkB.t[:, :], in1=tab.t[:, 1, :], op=ALU.mult), reads=[bkB, tab], writes=[t2])
        S.op(POOL, lambda: G.tensor_tensor(out=krd.t[:, :], in0=t1.t[:, :], in1=t2.t[:, :], op=ALU.add), reads=[t1, t2], writes=[krd])
        if KPART < 3:
            return
        bk = nb()
        mm_group(bk, bk.t[:, :], [(ones.t[:, :], sqkv.t[:, m, :]) for m in range(4)], [ones, sqkv])
        rstd_from_bank(bk, 512, rk32)
        for m in range(4):
            S.op(DVE, lambda m=m: V.scalar_tensor_tensor(out=kvn.t[:, m, :], in0=kvc32.t[:, m, :], scalar=vecs.t[:, V_GKV + m:V_GKV + m + 1],
                                                         in1=rk32.t[:, :], op0=ALU.mult, op1=ALU.mult),
                 reads=[kvc32, rk32, vecs], writes=[kvn])
        if KPART < 4:
            return
        for hh in range(8):
            bk = nb()
            mm_group(bk, bk.t[:, :], [(wuk.t[:, hh, m, :], kvn.t[:, m, :]) for m in range(4)], [wuk, kvn])
            evac_copy(hh, kT_out.t[:, hh, :], bk.t[:, :], [bk], [kT_out])
        if KPART < 5:
            return
        for t in range(4):
            for c in range(2):
                bk = nb()
                mm_group(bk, bk.t[:, :], [(kvn.t[:, m, t * 128:(t + 1) * 128], wuv.t[:, m, c * 512:(c + 1) * 512]) for m in range(4)], [wuv, kvn])
                evac_copy(t * 2 + c, v_out.t[:, 4 * c:4 * c + 4, t, :], bk.t[:, :].rearrange("p (h d) -> p h d", d=128), [bk], [v_out])

    def p1_store(bi):
        s, j = blocks[bi]
        S.dma(POOL, st_k, KT[s].t[:, :, j * 512:(j + 1) * 512].rearrange("h d t -> d h t"), kT_out.t[:, :, :], reads=[kT_out], writes=[KT[s]])
        S.dma(POOL, st_pe, KPE[s].t[:, j * 512:(j + 1) * 512], krd.t[:, :], reads=[krd], writes=[KPE[s]])
        S.dma(POOL, st_v, VV[s].t[:, :, 4 * j:4 * j + 4, :].rearrange("h p c d -> p h c d"), v_out.t[:, :, :, :], reads=[v_out], writes=[VV[s]])

    nblk = len(blocks)
    p1_load(0)
    p1_load(1)
    p1_stats(0)
    p1_stats(1)
    p1_trans(0)
    p1_load(2)
    for bi in range(nblk):
        if bi + 1 < nblk:
            p1_trans(bi + 1)
        if bi + 2 < nblk:
            p1_stats(bi + 2)
        if bi + 3 < nblk:
            p1_load(bi + 3)
        if SKIP < 2:
            p1_main(bi)
        if SKIP < 1:
            p1_store(bi)
        if pending_casts:
            pending_casts.pop(0)()
    while pending_casts:
        pending_casts.pop(0)()
    if stage == 1:
        S.barrier()
        dbg_dump("h0", hA[0], [128, 16, 512], BF16)
        dbg_dump("KT0", (KT[0].t[:, :, 0:1536], KT[0]), [8, 128, 1536], BF16)
        dbg_dump("KPE0", (KPE[0].t[:, 0:1536], KPE[0]), [128, 1536], BF16)
        dbg_dump("VV0", (VV[0].t[:, :, 0:12, :], VV[0]), [8, 128, 12, 128], BF16)
        S.barrier()
        return nc, dbg_out

    S.mark('p1b')
    o = P1 + 96 * KB
    wkna = S.tile("wkna", [128, 8, 16, 128], BF16, o); o += 32 * KB
    wvna = S.tile("wvna", [128, 16, 1024], BF16, o); o += 32 * KB
    kna_out = S.tile("kna_out", [128, 8, 512], BF16, o); o += 8 * KB
    vna_out = S.tile("vna_out", [128, 4, 1024], BF16, o); o += 8 * KB
    wds2 = S.dsem("w1b")
    S.dma(SP, wds2, wkna.t[:, :, :, :], wbf["WKNA"].t.rearrange("m p k n -> p m k n"), reads=[wbf["WKNA"]], writes=[wkna])
    S.dma(SP, wds2, wvna.t[:, :, :], wbf["WVNA"].t, reads=[wbf["WVNA"]], writes=[wvna])
    st_kn, st_vn = S.dsem("st_kn"), S.dsem("st_vn")
    nblocks = [(0, j) for j in range(NAP // 512)] + [(1, j) for j in range(NAS // 512)]
    if stage == 2:
        nblocks = nblocks[:5]
    xt2 = {}

    def p1b_load(bi):
        s, j = nblocks[bi]
        row0 = (0 if s == 0 else SP_) + j * 512
        xt2[bi] = [xr.load(row0 + t * 128) for t in range(4)]

    def p1b_stats(bi):
        for t in range(4):
            token_stats_scale(xt2[bi][t], junk, ssb, (bi * 4 + t) % 8)

    def p1b_trans(bi):
        s, j = nblocks[bi]
        transpose_to_h(xt2[bi], hA[bi % 2], s, pbanks, pk)

    def p1b_main(bi):
        s, j = nblocks[bi]
        h = hA[bi % 2]
        for hh in range(8):
            bk = nb()
            mm_group(bk, bk.t[:, :], [(wkna.t[:, hh, kc, :], h.t[:, kc, :]) for kc in range(16)], [wkna, h])
            evac_copy(hh, kna_out.t[:, hh, :], bk.t[:, :], [bk], [kna_out])
        for t in range(4):
            for c in range(2):
                bk = nb()
                mm_group(bk, bk.t[:, :], [(h.t[:, kc, t * 128:(t + 1) * 128], wvna.t[:, kc, c * 512:(c + 1) * 512]) for kc in range(16)], [wvna, h])
                evac_copy(t * 2 + c, vna_out.t[:, t, c * 512:(c + 1) * 512], bk.t[:, :], [bk], [vna_out])
        S.dma(POOL, st_kn, KNA[s].t[:, :, j * 512:(j + 1) * 512].rearrange("h d t -> d h t"), kna_out.t[:, :, :], reads=[kna_out], writes=[KNA[s]])
        S.dma(POOL, st_vn, VNA[s].t[j * 512:(j + 1) * 512, :].rearrange("(t p) n -> p t n", p=128), vna_out.t[:, :, :], reads=[vna_out], writes=[VNA[s]])

    nnb = len(nblocks)
    p1b_load(0)
    p1b_load(1)
    p1b_stats(0)
    p1b_stats(1)
    p1b_trans(0)
    p1b_load(2)
    for bi in range(nnb):
        if bi + 1 < nnb:
            p1b_trans(bi + 1)
        if bi + 2 < nnb:
            p1b_stats(bi + 2)
        if bi + 3 < nnb:
            p1b_load(bi + 3)
        p1b_main(bi)

    S.mark('modB')
    mod_phase(list(range(8, 24)))
    for s in range(2):
        derive(s, 1)
    S.barrier()

    Q = CONST_END
    xT = S.tile("xT", [128, 16, 512], F32, Q)
    h2 = S.tile("h2", [128, 16, 512], BF16, Q + 32 * KB)
    HS = Q + 32 * KB
    sq = S.tile("sq", [128, 16, 512], BF16, Q + 48 * KB)
    rA = S.tile("rA", [128, 512], F32, Q + 64 * KB)
    rB = S.tile("rB", [128, 512], F32, Q + 66 * KB)
    tab2 = S.tile("tab2", [128, 2, 512], F32, Q + 68 * KB)
    tmp = [S.tile("tmp%d" % i, [128, 512], F32, Q + (72 + 2 * i) * KB) for i in range(2)]
    junk2 = S.tile("junk2", [128, D], BF16, Q + 72 * KB)
    ssb2 = S.tile("ssb2", [128, 8], F32, BASE + 6 * KB - 64)
    NWR = 4
    WR = Q + 76 * KB
    wr_pair = [S.tile("wrp%d" % i, [128, 2, 16, 128], BF16, WR + i * 8 * KB) for i in range(NWR)]
    wr_uq = [S.tile("wru%d" % i, [128, 8, 4, 128], BF16, WR + i * 8 * KB) for i in range(NWR)]
    wr_ds = [S.dsem("wr%d" % i) for i in range(NWR)]
    wrk = [0]
    AR = Q + 108 * KB
    assert AR + 93 * KB <= BASE + 207 * KB
    xr2 = XRing(AR, 4, "b")
    Kwin = S.tile("Kwin", [128, 8, 896], BF16, AR)
    Vwin = S.tile("Vwin", [128, 7, 1024], BF16, AR + 14 * KB)
    kv_ds = [S.dsem("kv%d" % i) for i in range(NKV)]
    PER = SCK // 128
    Ksl = [S.tile("Ksl%d" % i, [128, SCK], BF16, AR + i * 6 * KB) for i in range(NKV)]
    Pesl = [S.tile("Pesl%d" % i, [128, SCK], BF16, AR + i * 6 * KB + 2 * KB) for i in range(NKV)]
    Vsl = [S.tile("Vsl%d" % i, [128, PER, 128], BF16, AR + i * 6 * KB + 4 * KB) for i in range(NKV)]
    merged = S.tile("merged", [128, 16, 512], BF16, AR)
    aT = S.tile("aT", [128, NFF, 512], BF16, AR)
    qc32 = S.tile("qc32", [128, 4, 512], F32, AR + 36 * KB)
    qn = S.tile("qn", [128, 4, 512], BF16, AR + 44 * KB)
    ona32 = S.tile("ona32", [128, 8, 512], F32, AR + 32 * KB)
    omla32 = S.tile("omla32", [128, 8, 512], F32, AR + 48 * KB)
    qna = S.tile("qna", [128, 8, 512], BF16, AR + 64 * KB)
    qnope = S.tile("qnope", [128, 8, 512], BF16, AR + 72 * KB)
    qrope = S.tile("qrope", [128, 8, 512], BF16, AR + 80 * KB)
    wd = [S.tile("wd%d" % i, [128, NFF, 128], BF16, AR + 44 * KB + i * 11 * KB) for i in range(2)]
    wd_ds = [S.dsem("wd%d" % i) for i in range(2)]
    ytok = [S.tile("ytok%d" % i, [128, D], F32, AR + i * 8 * KB) for i in range(2)]
    y_ds = [S.dsem("y%d" % i) for i in range(2)]
    nbias = [S.tile("nbias%d" % i, [128, 896], F32, HS + i * 3584) for i in range(2)]
    nb_ds = [S.dsem("nbias%d" % i) for i in range(2)]
    sbna = S.tile("sbna", [128, 896], F32, HS + 7168)
    pTna = [S.tile("pTna%d" % i, [128, 896], BF16, HS + 10752 + i * 1792) for i in range(2)]
    rlna = S.tile("rlna", [128, 128], F32, HS + 14336)
    NPT = 6
    pT = [S.tile("pT%d" % i, [128, 512], BF16, HS + i * KB) for i in range(NPT)]
    acc = S.tile("acc", [128, 512], F32, HS + 6 * KB)
    accb = S.tile("accb", [128, 512], BF16, HS + 8 * KB)
    rl = S.tile("rl", [128, 512], F32, HS + 9 * KB)
    accp = S.tile("accp", [128, 512], F32, HS + 11 * KB)
    tab2_ds, kw_ds, vw_ds = S.dsem("tab2"), S.dsem("kwin"), S.dsem("vwin")
    QS_NA = 128.0 ** -0.5
    QS_MLA = 192.0 ** -0.5

    def wr_next():
        i = wrk[0] % NWR
        wrk[0] += 1
        return i

    def ones_rstd(chunks, dim, rt):
        bk = nb()
        mm_group(bk, bk.t[:, :], [(ones.t[:, :], sq.t[:, c, :]) for c in chunks], [ones, sq])
        rstd_from_bank(bk, dim, rt)

    first_block = [True]
    own_blocks = [(0, bb) for bb in range(OWNP // 512)] + [(1, bb) for bb in range(OWNS // 512)]
    if stage == 3:
        own_blocks = own_blocks[:1]
    for (s, bb) in own_blocks:
        jrot = 1 + bb
        L = SP_ if s == 0 else SS_
        row0 = (0 if s == 0 else SP_) + jrot * 512
        nt_seg = (OWNP if s == 0 else OWNS) // 128
        S.mark('A')
        S.dma(SP, tab2_ds, tab2.t[:, :, :], tabs_d.t[(0 if s == 0 else NBP) + jrot], reads=[tabs_d], writes=[tab2])
        xts = [xr2.load(row0 + t * 128) for t in range(4)]
        for kc in range(16):
            bk = pbanks[pk[0] % 2]
            pk[0] += 1
            for t in range(4):
                S.op(PE, lambda t=t, kc=kc, bk=bk: T.transpose(bk.t[:, t * 128:(t + 1) * 128], xts[t].t[:, kc * 128:(kc + 1) * 128], ident.t[:, :]),
                     reads=[xts[t], ident], writes=[bk] if t == 0 else [], inc=(t == 3))
            evac_copy(kc, xT.t[:, kc, :], bk.t[:, :], [bk], [xT])
        for t in range(4):
            token_stats_scale(xts[t], junk2, ssb2, t)
        transpose_to_h(xts, h2, s, pbanks, pk)
        for m in range(12):
            if m % 2 == 0:
                wi = wr_next()
                S.dma(SP, wr_ds[wi], wr_pair[wi].t[:, :, :, :], wbf["WQ"].t[m:m + 2].rearrange("m p k n -> p m k n"), reads=[wbf["WQ"]], writes=[wr_pair[wi]])
            w = wr_pair[wi]
            bk = nb()
            mm_group(bk, bk.t[:, :], [(w.t[:, m % 2, kc, :], h2.t[:, kc, :]) for kc in range(16)], [w, h2])
            if m < 8:
                evac_copy(m, qna.t[:, m, :], bk.t[:, :], [bk], [qna], scale=QS_NA)
            else:
                S.op(DVE, lambda m=m, bk=bk: V.tensor_copy(out=qc32.t[:, m - 8, :], in_=bk.t[:, :]), reads=[bk], writes=[qc32])
                S.op(ACT, lambda m=m: A.activation(out=sq.t[:, m - 8, :], in_=qc32.t[:, m - 8, :], func=AF.Square), reads=[qc32], writes=[sq])
        ones_rstd(range(4), 512, rA)
        for m in range(4):
            S.op(DVE, lambda m=m: V.scalar_tensor_tensor(out=qn.t[:, m, :], in0=qc32.t[:, m, :], scalar=vecs.t[:, V_GQ + m:V_GQ + m + 1],
                                                         in1=rA.t[:, :], op0=ALU.mult, op1=ALU.mult), reads=[qc32, rA, vecs], writes=[qn])
        wu = []
        for i in range(2):
            wi = wr_next()
            S.dma(SP, wr_ds[wi], wr_uq[wi].t[:, :, :, :], wbf["WUQ"].t[8 * i:8 * i + 8].rearrange("m p k n -> p m k n"), reads=[wbf["WUQ"]], writes=[wr_uq[wi]])
            wu.append(wr_uq[wi])
        for hh in range(8):
            bk = nb()
            mm_group(bk, bk.t[:, :], [(wu[0].t[:, hh, m, :], qn.t[:, m, :]) for m in range(4)], [wu[0], qn])
            evac_copy(hh, qnope.t[:, hh, :], bk.t[:, :], [bk], [qnope], scale=QS_MLA)
        if first_block[0]:
            first_block[0] = False
            S.op(POOL, lambda: G.memset(qrope.t[:, :, :], 0.0), writes=[qrope])
        for hh in range(8):
            pr, e = hh // 2, hh % 2
            bkA = nb()
            mm_group(bkA, bkA.t[0:64, :], [(wu[1].t[:, pr, m, 64 * e:64 * e + 64], qn.t[:, m, :]) for m in range(4)], [wu[1], qn])
            bkB = nb()
            mm_group(bkB, bkB.t[0:64, :], [(wu[1].t[:, 4 + pr, m, 64 * e:64 * e + 64], qn.t[:, m, :]) for m in range(4)], [wu[1], qn])
            S.op(DVE, lambda bkA=bkA: V.scalar_tensor_tensor(out=tmp[0].t[0:64, :], in0=bkA.t[0:64, :], scalar=QS_MLA, in1=tab2.t[0:64, 0, :], op0=ALU.mult, op1=ALU.mult),
                 reads=[bkA, tab2], writes=[tmp[0]])
            S.op(DVE, lambda bkB=bkB: V.scalar_tensor_tensor(out=tmp[1].t[0:64, :], in0=bkB.t[0:64, :], scalar=QS_MLA, in1=tab2.t[0:64, 1, :], op0=ALU.mult, op1=ALU.mult),
                 reads=[bkB, tab2], writes=[tmp[1]])
            S.op(POOL, lambda hh=hh: G.tensor_tensor(out=qrope.t[0:64, hh, :], in0=tmp[0].t[0:64, :], in1=tmp[1].t[0:64, :], op=ALU.add),
                 reads=[tmp[0], tmp[1]], writes=[qrope])
        if KPART >= 2:
            S.mark('B')
            for qt in range(4):
                ti = bb * 4 + qt
                slot = _slot_of(ti, nt_seg)
                k0 = 128 + 128 * ti
                S.dma(SP, kw_ds, Kwin.t[:, :, :], KNA[s].t[:, :, k0:k0 + 896].rearrange("h d t -> d h t"), reads=[KNA[s]], writes=[Kwin])
                S.dma(SP, vw_ds, Vwin.t[:, :, :], VNA[s].t[k0:k0 + 896, :].rearrange("(c p) n -> p c n", p=128), reads=[VNA[s]], writes=[Vwin])
                for hh in range(8):
                    i2 = hh % 2
                    S.dma(SP, nb_ds[i2], nbias[i2].t[:, :], nab_d.t[s, slot, hh], reads=[nab_d], writes=[nbias[i2]])
                    bS = [bank[2 * i2], bank[2 * i2 + 1]]
                    ps2 = ps_all[:, 2 * i2:2 * i2 + 2, :].rearrange("p a b -> p (a b)")
                    for c in range(7):
                        S.op(PE, lambda c=c, hh=hh, ps2=ps2: T.matmul(ps2[:, c * 128:(c + 1) * 128], Kwin.t[:, hh, c * 128:(c + 1) * 128],
                                                                      qna.t[:, hh, qt * 128:(qt + 1) * 128], start=True, stop=True),
                             reads=[Kwin, qna] if c == 0 else [], writes=bS if c == 0 else [], inc=(c == 6))
                    S.op(DVE, lambda ps2=ps2, i2=i2: V.tensor_tensor(out=sbna.t[:, :], in0=ps2[:, 0:896], in1=nbias[i2].t[:, :], op=ALU.add),
                         reads=bS + [nbias[i2]], writes=[sbna])
                    S.op(ACT, lambda i2=i2: A.activation(out=pTna[i2].t[:, :], in_=sbna.t[:, :], func=AF.Exp), reads=[sbna], writes=[pTna[i2]])
                    bO = bank[4 + i2]
                    bL = bank[6 + i2]
                    mm_group(bO, bO.t[:, 0:128], [(Vwin.t[:, c, hh * 128:(hh + 1) * 128], pTna[i2].t[:, c * 128:(c + 1) * 128]) for c in range(7)], [Vwin, pTna[i2]])
                    mm_group(bL, bL.t[:, 0:128], [(ones.t[:, :], pTna[i2].t[:, c * 128:(c + 1) * 128]) for c in range(7)], [ones, pTna[i2]])
                    S.op(DVE, lambda bL=bL: V.reciprocal(out=rlna.t[:, :], in_=bL.t[:, 0:128]), reads=[bL], writes=[rlna])
                    S.op(DVE, lambda bO=bO, hh=hh, qt=qt: V.tensor_tensor(out=ona32.t[:, hh, qt * 128:(qt + 1) * 128], in0=bO.t[:, 0:128], in1=rlna.t[:, :], op=ALU.mult),
                         reads=[bO, rlna], writes=[ona32])
            for hh in range(8):
                S.op(ACT, lambda hh=hh: A.activation(out=sq.t[:, hh, :], in_=ona32.t[:, hh, :], func=AF.Square), reads=[ona32], writes=[sq])
        if KPART >= 3:
            S.mark('C')
            nch = L // 128
            kvk = [0]
            for hh in range(8):
                e, pr = hh % 2, hh // 2
                bO = bank[4 + hh % 2]
                pend = []

                def emit_pv(p, hh=hh, bO=bO):
                    g, si, c = p
                    S.op(PE, lambda: T.matmul(bO.t[:, :], Vsl[si].t[:, c, :], pT[g % NPT].t[:, :], start=(g == 0), stop=(g == nch - 1)),
                         reads=[Vsl[si], pT[g % NPT]], writes=[bO] if g in (0, nch - 1) else [], inc=True)

                for g in range(nch):
                    c = g % PER
                    if c == 0:
                        si = kvk[0] % NKV
                        kvk[0] += 1
                        sc_ = g // PER
                        S.dma(SP, kv_ds[si], Ksl[si].t[:, :], KT[s].t[hh, :, sc_ * SCK:(sc_ + 1) * SCK], reads=[KT[s]], writes=[Ksl[si]])
                        S.dma(SP, kv_ds[si], Pesl[si].t[:, :], KPE[s].t[:, sc_ * SCK:(sc_ + 1) * SCK], reads=[KPE[s]], writes=[Pesl[si]])
                        S.dma(SP, kv_ds[si], Vsl[si].t[:, :, :], VV[s].t[hh, :, sc_ * PER:(sc_ + 1) * PER, :], reads=[VV[s]], writes=[Vsl[si]])
                    bS = bank[g % 4]
                    S.op(PE, lambda si=si, c=c, bS=bS: T.matmul(bS.t[:, :], Ksl[si].t[:, c * 128:(c + 1) * 128], qnope.t[:, hh, :], start=True, stop=False),
                         reads=[Ksl[si], qnope], writes=[bS], inc=False)
                    S.op(PE, lambda si=si, c=c, bS=bS: T.matmul(bS.t[:, :], Pesl[si].t[:, c * 128:(c + 1) * 128],
                                                                qrope.t[:, hh, :], start=False, stop=True),
                         reads=[Pesl[si], qrope], writes=[], inc=True)
                    if len(pend) >= 2:
                        emit_pv(pend.pop(0))
                    S.op(ACT, lambda g=g, bS=bS: A.activation(out=pT[g % NPT].t[:, :], in_=bS.t[:, :], func=AF.Exp), reads=[bS], writes=[pT[g % NPT]])
                    if g == 0:
                        S.op(DVE, lambda g=g: V.tensor_copy(out=acc.t[:, :], in_=pT[g % NPT].t[:, :]), reads=[pT[g % NPT]], writes=[acc])
                    elif g == 2:
                        S.op(POOL, lambda g=g: G.tensor_copy(out=accp.t[:, :], in_=pT[g % NPT].t[:, :]), reads=[pT[g % NPT]], writes=[accp])
                    elif g % 3 != 2:
                        S.op(DVE, lambda g=g: V.tensor_tensor(out=acc.t[:, :], in0=acc.t[:, :], in1=pT[g % NPT].t[:, :], op=ALU.add),
                             reads=[pT[g % NPT], acc], writes=[acc])
                    else:
                        S.op(POOL, lambda g=g: G.tensor_tensor(out=accp.t[:, :], in0=accp.t[:, :], in1=pT[g % NPT].t[:, :], op=ALU.add),
                             reads=[pT[g % NPT], accp], writes=[accp])
                    pend.append((g, si, c))
                for p in pend:
                    emit_pv(p)
                S.op(DVE, lambda: V.tensor_tensor(out=accb.t[:, :], in0=acc.t[:, :], in1=accp.t[:, :], op=ALU.add), reads=[acc, accp], writes=[accb])
                bL = bank[6]
                mm_group(bL, bL.t[:, :], [(ones.t[:, :], accb.t[:, :])], [ones, accb])
                S.op(DVE, lambda bL=bL: V.reciprocal(out=rl.t[:, :], in_=bL.t[:, :]), reads=[bL], writes=[rl])
                S.op(DVE, lambda bO=bO, hh=hh: V.tensor_tensor(out=omla32.t[:, hh, :], in0=bO.t[:, :], in1=rl.t[:, :], op=ALU.mult),
                     reads=[bO, rl], writes=[omla32])
                S.op(ACT, lambda hh=hh: A.activation(out=sq.t[:, 8 + hh, :], in_=omla32.t[:, hh, :], func=AF.Square), reads=[omla32], writes=[sq])
        if KPART >= 4:
            S.mark('D')
            ones_rstd(range(0, 8), 1024, rA)
            ones_rstd(range(8, 16), 1024, rB)
            for hh in range(8):
                S.op(DVE, lambda hh=hh: V.scalar_tensor_tensor(out=merged.t[:, hh, :], in0=ona32.t[:, hh, :], scalar=vecs.t[:, V_GNA + hh:V_GNA + hh + 1],
                                                               in1=rA.t[:, :], op0=ALU.mult, op1=ALU.mult), reads=[ona32, rA, vecs], writes=[merged])
                S.op(DVE, lambda hh=hh: V.scalar_tensor_tensor(out=merged.t[:, 8 + hh, :], in0=omla32.t[:, hh, :], scalar=vecs.t[:, V_GMLA + hh:V_GMLA + hh + 1],
                                                               in1=rB.t[:, :], op0=ALU.mult, op1=ALU.mult), reads=[omla32, rB, vecs], writes=[merged])
            for n in range(16):
                if n % 2 == 0:
                    wi = wr_next()
                    S.dma(SP, wr_ds[wi], wr_pair[wi].t[:, :, :, :], wbf["WO"].t[n:n + 2].rearrange("m p k n -> p m k n"), reads=[wbf["WO"]], writes=[wr_pair[wi]])
                w = wr_pair[wi]
                bk = nb()
                mm_group(bk, bk.t[:, :], [(w.t[:, n % 2, kc, :], merged.t[:, kc, :]) for kc in range(16)], [w, merged])
                S.op(DVE, lambda n=n, bk=bk: V.scalar_tensor_tensor(out=xT.t[:, n, :], in0=bk.t[:, :], scalar=gt1(s, n), in1=xT.t[:, n, :], op0=ALU.mult, op1=ALU.add),
                     reads=[bk, xT, der], writes=[xT])
                S.op(ACT, lambda n=n: A.activation(out=sq.t[:, n, :], in_=xT.t[:, n, :], func=AF.Square), reads=[xT], writes=[sq])
            ones_rstd(range(16), 2048, rA)
            for kc in range(16):
                tt = tmp[kc % 2]
                S.op(DVE, lambda kc=kc, tt=tt: V.scalar_tensor_tensor(out=tt.t[:, :], in0=xT.t[:, kc, :], scalar=gs2(s, kc), in1=rA.t[:, :], op0=ALU.mult, op1=ALU.mult),
                     reads=[xT, rA, der], writes=[tt])
                S.op(ACT, lambda kc=kc, tt=tt: A.activation(out=h2.t[:, kc, :], in_=tt.t[:, :], func=AF.Identity, bias=sh2(s, kc), scale=1.0),
                     reads=[tt, der], writes=[h2])
        if KPART >= 5:
            S.mark('E')
            for c in range(NFF):
                wi = wr_next()
                w = wr_pair[wi]
                S.dma(SP, wr_ds[wi], w.t[:, :, :, :], wbf["WGU"].t[c], reads=[wbf["WGU"]], writes=[w])
                bG = nb()
                mm_group(bG, bG.t[:, :], [(w.t[:, 0, kc, :], h2.t[:, kc, :]) for kc in range(16)], [w, h2])
                bU = nb()
                mm_group(bU, bU.t[:, :], [(w.t[:, 1, kc, :], h2.t[:, kc, :]) for kc in range(16)], [w, h2])
                tt = tmp[c % 2]
                S.op(ACT, lambda bG=bG, tt=tt: A.activation(out=tt.t[:, :], in_=bG.t[:, :], func=AF.Silu), reads=[bG], writes=[tt])
                S.op(DVE, lambda c=c, bU=bU, tt=tt: V.tensor_tensor(out=aT.t[:, c, :], in0=bU.t[:, :], in1=tt.t[:, :], op=ALU.mult), reads=[bU, tt], writes=[aT])
            for n in range(16):
                w = wd[n % 2]
                S.dma(SP, wd_ds[n % 2], w.t[:, :, :], wbf["WD"].t[n], reads=[wbf["WD"]], writes=[w])
                bk = nb()
                mm_group(bk, bk.t[:, :], [(w.t[:, c, :], aT.t[:, c, :]) for c in range(NFF)], [w, aT])
                S.op(DVE, lambda n=n, bk=bk: V.scalar_tensor_tensor(out=xT.t[:, n, :], in0=bk.t[:, :], scalar=gt2(s, n), in1=xT.t[:, n, :], op0=ALU.mult, op1=ALU.add),
                     reads=[bk, xT, der], writes=[xT])
                S.op(ACT, lambda n=n: A.activation(out=sq.t[:, n, :], in_=xT.t[:, n, :], func=AF.Square), reads=[xT], writes=[sq])
        S.mark('F')
        ones_rstd(range(16), 2048, rA)
        for n in range(16):
            S.op(DVE, lambda n=n: V.scalar_tensor_tensor(out=xT.t[:, n, :], in0=xT.t[:, n, :], scalar=vecs.t[:, V_GFIN + n:V_GFIN + n + 1], in1=rA.t[:, :],
                                                         op0=ALU.mult, op1=ALU.mult), reads=[xT, rA, vecs], writes=[xT])
        for t in range(4):
            yt = ytok[t % 2]
            for g4 in range(4):
                bk = pbanks[pk[0] % 2]
                pk[0] += 1
                for i in range(4):
                    S.op(PE, lambda i=i, g4=g4, t=t, bk=bk: T.transpose(bk.t[:, i * 128:(i + 1) * 128], xT.t[:, g4 * 4 + i, t * 128:(t + 1) * 128], ident.t[:, :]),
                         reads=[xT, ident], writes=[bk] if i == 0 else [], inc=(i == 3))
                evac_copy(g4, yt.t[:, g4 * 512:(g4 + 1) * 512], bk.t[:, :], [bk], [yt])
            r0 = bb * 512 + t * 128
            S.dma(POOL, y_ds[t % 2], yout[s].t[r0:r0 + 128, :], yt.t[:, :], reads=[yt], writes=[yout[s]])
    S.barrier()
    S.mark('end')
    dbg_out['marks'] = S.marks
    return nc, dbg_out


def kernel(x_prompt, x_sample, c_prompt, c_sample, w_ada, b_ada, g_attn, w_in, rpb, g_q, w_uq, g_kv, w_ukv,
           g_out_na, g_out_mla, w_o, g_ffn, w_gate, w_up, w_down, g_final):
    inp = dict(x_prompt=x_prompt, x_sample=x_sample, c_prompt=c_prompt, c_sample=c_sample, w_ada=w_ada, b_ada=b_ada,
               g_attn=g_attn, w_in=w_in, rpb=rpb, g_q=g_q, w_uq=w_uq, g_kv=g_kv, w_ukv=w_ukv, g_out_na=g_out_na,
               g_out_mla=g_out_mla, w_o=w_o, g_ffn=g_ffn, w_gate=w_gate, w_up=w_up, w_down=w_down, g_final=g_final)
    maps = _prep(inp)
    nc, _ = build()
    res = run_bass_kernel_spmd(nc, maps, core_ids=list(range(NCORES)))
    yp = np.concatenate([np.asarray(r["y0"], np.float32) for r in res.results], axis=0)[None]
    ys = np.concatenate([np.asarray(r["y1"], np.float32) for r in res.results], axis=0)[None]
    return (yp, ys)
```

```python
import numpy as np
import concourse.bass as bass
import concourse.mybir as mybir
from concourse.bass_utils import run_bass_kernel_spmd

F32 = mybir.dt.float32
BF16 = mybir.dt.bfloat16
AF = mybir.ActivationFunctionType
ALU = mybir.AluOpType

NCORES = 8
D = 2048
KC = 16
SP_, SS_ = 8192, 16384
OWNP, OWNS = 1024, 2048
HALO = 512
NAP, NAS = OWNP + 2 * HALO, OWNS + 2 * HALO
NBP, NBS = SP_ // 512, SS_ // 512
DFF = 5632
NFF = DFF // 128
EPS = 1e-6
NEG = -30000.0
SCK = 1024
NKV = 5
import os
SKIP = int(os.environ.get('KSKIP', '0'))
KPART = int(os.environ.get('KPART', '9'))

V_BADA, V_GATTN, V_GFFN, V_GFIN, V_GQ, V_GKV, V_GNA, V_GMLA, V_CP, V_CS, NV = 0, 96, 112, 128, 144, 148, 152, 160, 168, 184, 200


class Eng:
    def __init__(self, name, h, sem, compute=True):
        self.name, self.h, self.sem, self.cnt, self.seen, self.compute = name, h, sem, 0, {}, compute


class DSem:
    def __init__(self, sem):
        self.sem, self.cnt = sem, 0


class Buf:
    def __init__(self, name, space, lo, hi):
        self.name, self.space, self.lo, self.hi = name, space, lo, hi
        self.w = None
        self.r = {}
        self.ov = [self]


class Tile:
    def __init__(self, t, buf):
        self.t, self.buf = t, buf


class Sched:
    def __init__(self, nc):
        self.nc = nc
        self.bufs = []
        self.dsems = []
        mk = lambda n, h, c=True: Eng(n, h, nc.alloc_semaphore("sem_" + n), c)
        self.PE = mk("pe", nc.tensor)
        self.ACT = mk("act", nc.scalar)
        self.DVE = mk("dve", nc.vector)
        self.POOL = mk("pool", nc.gpsimd)
        self.SP = mk("sp", nc.sync, False)
        self.engs = [self.PE, self.ACT, self.DVE, self.POOL, self.SP]
        self.nsem = 5
        self.npe = 0
        self.marks = []

    def dsem(self, name):
        d = DSem(self.nc.alloc_semaphore("ds_" + name))
        d.name = name
        self.dsems.append(d)
        self.nsem += 1
        return d

    def _reg(self, b):
        for o in self.bufs:
            if o.space == b.space and o.lo < b.hi and b.lo < o.hi:
                o.ov.append(b)
                b.ov.append(o)
        self.bufs.append(b)
        return b

    def tile(self, name, shape, dtype, off):
        esz = 2 if dtype == BF16 else 4
        n = 1
        for s in shape[1:]:
            n *= s
        t = self.nc.alloc_sbuf_tensor_at(name, list(shape), dtype, offset=int(off))
        return Tile(t, self._reg(Buf(name, "sb", int(off), int(off) + n * esz)))

    def dram(self, name, shape, dtype, kind):
        t = self.nc.dram_tensor(name, list(shape), dtype, kind=kind).ap()
        return Tile(t, self._reg(Buf(name, "dram_" + name, 0, 1)))

    def _wait(self, eng, ev):
        sem, val, _ = ev
        k = id(sem)
        if eng.seen.get(k, 0) >= val:
            return
        eng.seen[k] = val
        eng.h.wait_ge(sem, val)

    def _deps(self, eng, reads, writes, is_dma):
        for t in reads:
            for o in t.buf.ov:
                if o.w is not None:
                    if o.w[2] is eng and not is_dma and eng is self.PE:
                        continue
                    self._wait(eng, o.w)
                if o.space == "ps":
                    for e in o.r.values():
                        if e[2] is not eng:
                            self._wait(eng, e)
        for t in writes:
            for o in t.buf.ov:
                if o.w is not None and (is_dma or o.w[2] is not eng):
                    self._wait(eng, o.w)
                for e in o.r.values():
                    if is_dma or e[2] is not eng:
                        self._wait(eng, e)

    def mark(self, name):
        self.marks.append((name, self.npe))

    def op(self, eng, fn, reads=(), writes=(), inc=True):
        if eng is self.PE:
            self.npe += 1
        self._deps(eng, reads, writes, False)
        ins = fn()
        if inc:
            eng.cnt += 1
            ins.then_inc(eng.sem, 1)
            ev = (eng.sem, eng.cnt, eng)
        else:
            ev = (eng.sem, eng.cnt + 1, eng)
        for t in reads:
            t.buf.r[eng.name] = ev
        for t in writes:
            t.buf.w = ev
            t.buf.r = {}
        return ev

    def dma(self, q, ds, out, in_, reads=(), writes=()):
        self._deps(q, reads, writes, True)
        ds.cnt += 16
        q.h.dma_start(out=out, in_=in_).then_inc(ds.sem, 16)
        ev = (ds.sem, ds.cnt, None)
        for t in reads:
            t.buf.r[("d", id(ds))] = ev
        for t in writes:
            t.buf.w = ev
            t.buf.r = {}
        return ev

    def barrier(self):
        for e in self.engs:
            for o in self.engs:
                if o is not e and o.cnt > 0:
                    self._wait(e, (o.sem, o.cnt, o))
            for d in self.dsems:
                if d.cnt > 0:
                    self._wait(e, (d.sem, d.cnt, None))


def _SL(w):
    K, N = w.shape
    return np.ascontiguousarray(w.reshape(K // 128, 128, N // 128, 128).transpose(2, 1, 0, 3))


def _ML(w):
    K, N = w.shape
    return np.ascontiguousarray(w.reshape(K // 128, 128, N).transpose(1, 0, 2))


def _fm(v):
    return np.ascontiguousarray(v.reshape(-1, 128).T)


def _rope_tables(pos):
    inv = (np.float32(10000.0) ** (-(np.arange(0, 64, 2, dtype=np.float32)) / np.float32(64))).astype(np.float32)
    ang = pos.astype(np.float32)[:, None] * inv[None, :]
    ang = np.concatenate([ang, ang], axis=-1).astype(np.float32)
    c = np.cos(ang).astype(np.float32).T
    s = np.sin(ang).astype(np.float32).T
    s = np.concatenate([-s[:32], s[32:]], axis=0)
    return np.concatenate([c, c], 0), np.concatenate([s, s], 0)


def _na_bias_variant(rpb, r, rows):
    i = np.arange(14)[:, None, None, None]
    kc = np.arange(64)[None, :, None, None]
    qq = np.arange(2)[None, None, :, None]
    qc = np.arange(64)[None, None, None, :]
    kr = r - 6 + i
    qr = r + qq
    rs = np.clip(qr - 4, 0, rows - 8)
    cs = np.clip(qc - 8, 0, 64 - 16)
    valid = (kr >= 0) & (kr < rows) & (kr >= rs) & (kr < rs + 8) & (kc >= cs) & (kc < cs + 16)
    dr = np.clip(kr - qr + 7, 0, 14)
    dc = np.clip(kc - qc + 15, 0, 30)
    dr, dc, valid = np.broadcast_arrays(dr, dc, valid)
    g = rpb[:, dr, dc]
    g = np.where(valid[None], g, np.float32(NEG)).astype(np.float32)
    g = g.reshape(8, 7, 128, 128)
    return np.ascontiguousarray(g.transpose(0, 2, 1, 3).reshape(8, 128, 896))


def _slot_of(i, n):
    return 1 if i == 0 else 2 if i == 1 else 3 if i == n - 2 else 4 if i == n - 1 else 0


def _prep(inp, cores=None):
    f32 = np.float32
    w_in = np.asarray(inp["w_in"][0], f32)
    w_uq = np.asarray(inp["w_uq"][0], f32)
    w_ukv = np.asarray(inp["w_ukv"][0], f32)
    perm = (np.arange(64) + 32) % 64
    kpe = w_in[:, 4096:4160]
    WA = np.concatenate([w_in[:, 3584:4096], kpe, kpe, kpe[:, perm], kpe[:, perm]], axis=1)
    uk = w_ukv.reshape(512, 8, 256)[:, :, :128].reshape(512, 1024)
    uv = w_ukv.reshape(512, 8, 256)[:, :, 128:].reshape(512, 1024)
    uq = w_uq.reshape(512, 8, 192)
    uqn = uq[:, :, :128].reshape(512, 1024)
    uqr = uq[:, :, 128:].reshape(512, 512)
    uqp = uq[:, :, 128:][:, :, perm].reshape(512, 512)
    WQ = np.concatenate([w_in[:, 0:1024], w_in[:, 3072:3584]], axis=1)
    WUQ = np.concatenate([uqn, uqr, uqp], axis=1)
    wg = _SL(np.asarray(inp["w_gate"][0], f32))
    wu = _SL(np.asarray(inp["w_up"][0], f32))
    WGU = np.ascontiguousarray(np.stack([wg, wu], axis=2))
    vecs = np.zeros((128, NV), f32)
    vecs[:, V_BADA:V_BADA + 96] = _fm(np.asarray(inp["b_ada"][0], f32))
    vecs[:, V_GATTN:V_GATTN + 16] = _fm(np.asarray(inp["g_attn"][0], f32))
    vecs[:, V_GFFN:V_GFFN + 16] = _fm(np.asarray(inp["g_ffn"][0], f32))
    vecs[:, V_GFIN:V_GFIN + 16] = _fm(np.asarray(inp["g_final"], f32))
    vecs[:, V_GQ:V_GQ + 4] = _fm(np.asarray(inp["g_q"][0], f32))
    vecs[:, V_GKV:V_GKV + 4] = _fm(np.asarray(inp["g_kv"][0], f32))
    vecs[:, V_GNA:V_GNA + 8] = _fm(np.asarray(inp["g_out_na"][0], f32))
    vecs[:, V_GMLA:V_GMLA + 8] = _fm(np.asarray(inp["g_out_mla"][0], f32))
    vecs[:, V_CP:V_CP + 16] = _fm(np.asarray(inp["c_prompt"][0], f32))
    vecs[:, V_CS:V_CS + 16] = _fm(np.asarray(inp["c_sample"][0], f32))
    shared = {
        "vecs": vecs,
        "w_ada": np.ascontiguousarray(np.asarray(inp["w_ada"][0], f32)),
        "ident": np.eye(128, dtype=f32),
        "WA": _SL(WA), "WUK": _SL(uk), "WUV": _ML(uv),
        "WKNA": _SL(w_in[:, 1024:2048]), "WVNA": _ML(w_in[:, 2048:3072]),
        "WQ": _SL(WQ), "WUQ": _SL(WUQ), "WO": _SL(np.asarray(inp["w_o"][0], f32)),
        "WGU": WGU, "WD": _SL(np.asarray(inp["w_down"][0], f32)),
    }
    rpb = np.asarray(inp["rpb"][0], f32)
    xp = np.asarray(inp["x_prompt"][0], f32)
    xs = np.asarray(inp["x_sample"][0], f32)
    variants = {}

    def variant(r, rows):
        key = ("t", r) if r < 4 else ("b", rows - r) if r >= rows - 4 else ("i",)
        if key not in variants:
            variants[key] = _na_bias_variant(rpb, r, rows)
        return variants[key]

    maps = []
    for c in (range(NCORES) if cores is None else cores):
        sp0 = (c * OWNP - HALO) % SP_
        ss0 = (c * OWNS - HALO) % SS_
        xall = np.concatenate([np.roll(xp, -sp0, axis=0), np.roll(xs, -ss0, axis=0)], axis=0)
        pos = np.concatenate([(sp0 + np.arange(SP_)) % SP_, (ss0 + np.arange(SS_)) % SS_])
        c2, s2 = _rope_tables(pos)
        tab = np.stack([c2, s2], axis=1).reshape(128, 2, NBP + NBS, 512).transpose(2, 0, 1, 3)
        nab = np.zeros((2, 5, 8, 128, 896), f32)
        for sg, (n, rows, row0) in enumerate(((OWNP // 128, SP_ // 64, c * (OWNP // 64)), (OWNS // 128, SS_ // 64, c * (OWNS // 64)))):
            tiles = {0: 2, 1: 0, 2: 1, 3: n - 2, 4: n - 1}
            for slot, i in tiles.items():
                nab[sg, slot] = variant(row0 + 2 * i, rows)
        m = dict(shared)
        m["xall"] = xall
        m["tabs"] = np.ascontiguousarray(tab)
        m["nab"] = nab
        maps.append(m)
    return maps


def build(stage=99, dbg=False):
    nc = bass.Bass("TRN2", target_bir_lowering=False)
    S = Sched(nc)
    PE, ACT, DVE, POOL, SP = S.PE, S.ACT, S.DVE, S.POOL, S.SP
    T, V, G, A = nc.tensor, nc.vector, nc.gpsimd, nc.scalar
    BASE = 16512
    KB = 1024

    din = lambda n, sh: S.dram(n, sh, F32, "ExternalInput")
    xall = din("xall", [SP_ + SS_, D])
    vecs_d = din("vecs", [128, NV])
    wada_d = din("w_ada", [D, 6 * D])
    ident_d = din("ident", [128, 128])
    tabs_d = din("tabs", [NBP + NBS, 128, 2, 512])
    nab_d = din("nab", [2, 5, 8, 128, 896])
    wshapes = {"WA": [6, 128, 16, 128], "WUK": [8, 128, 4, 128], "WUV": [128, 4, 1024],
               "WKNA": [8, 128, 16, 128], "WVNA": [128, 16, 1024], "WQ": [12, 128, 16, 128],
               "WUQ": [16, 128, 4, 128], "WO": [16, 128, 16, 128], "WGU": [NFF, 128, 2, 16, 128],
               "WD": [16, 128, NFF, 128]}
    w32 = {k: din(k, sh) for k, sh in wshapes.items()}
    wbf = {k: S.dram(k + "_bf", sh, BF16, "Internal") for k, sh in wshapes.items()}
    KT = [S.dram("KT%d" % s, [8, 128, L], BF16, "Internal") for s, L in enumerate((SP_, SS_))]
    KPE = [S.dram("KPE%d" % s, [128, L], BF16, "Internal") for s, L in enumerate((SP_, SS_))]
    VV = [S.dram("VV%d" % s, [8, 128, L // 128, 128], BF16, "Internal") for s, L in enumerate((SP_, SS_))]
    KNA = [S.dram("KNA%d" % s, [8, 128, L], BF16, "Internal") for s, L in enumerate((NAP, NAS))]
    VNA = [S.dram("VNA%d" % s, [L, 1024], BF16, "Internal") for s, L in enumerate((NAP, NAS))]
    yout = [S.dram("y%d" % s, [L, D], F32, "ExternalOutput") for s, L in enumerate((OWNP, OWNS))]
    dbg_out = {}

    def dbg_dump(name, tile_, shape, dtype=F32):
        if not dbg:
            return
        o = S.dram("dbg_" + name, shape, dtype, "ExternalOutput")
        dbg_out[name] = o
        S.dma(SP, S.dsem("dbg_" + name), o.t, tile_.t.ap() if isinstance(tile_, Tile) else tile_[0], reads=[tile_ if isinstance(tile_, Tile) else tile_[1]], writes=[o])

    pending_casts = []

    def cast_weight(k, defer=False):
        ds = S.dsem("cast_" + k)
        n = 1
        for s_ in wshapes[k]:
            n *= s_
        rows = n // 2048
        src = w32[k].t
        dst = wbf[k].t
        names = "abcde"[:len(wshapes[k])]
        pat = " ".join(names)
        src2 = src.rearrange(f"{pat} -> ({pat})").rearrange("(r c) -> r c", c=2048)
        dst2 = dst.rearrange(f"{pat} -> ({pat})").rearrange("(r c) -> r c", c=2048)
        step = 2048
        pieces = list(range(0, rows, step))

        def piece(r0, last):
            r1 = min(rows, r0 + step)
            S.dma(POOL, ds, dst2[r0:r1, :], src2[r0:r1, :], reads=[w32[k]], writes=[])
            if last:
                wbf[k].buf.w = (ds.sem, ds.cnt, None)

        for r0 in pieces:
            fn = (lambda r0=r0: piece(r0, r0 == pieces[-1]))
            if defer:
                pending_casts.append(fn)
            else:
                fn()

    ps_all = nc.alloc_psum_tensor("ps_all", [128, 8, 512], F32)
    bank = [Tile(ps_all[:, i, :], S._reg(Buf("bank%d" % i, "ps", i, i + 1))) for i in range(8)]

    off = BASE
    ident = S.tile("ident", [128, 128], F32, off); off += 512
    ones = S.tile("ones", [128, 128], BF16, off); off += 256
    vecs = S.tile("vecs", [128, NV], F32, off); off += NV * 4
    mod = S.tile("mod", [128, 96, 2], F32, off); off += 768
    der = S.tile("der", [128, 2, 6, 16], F32, off); off += 768
    epsT = S.tile("epsT", [128, 1], F32, off); off += 32
    sc = S.tile("sc", [128, 16, 2], BF16, off); off += 64
    gq2 = S.tile("gq2", [128, 8], F32, off); off += 32
    CONST_END = BASE + 6 * KB
    assert off <= CONST_END
    cds = S.dsem("const")
    S.dma(SP, cds, ident.t[:, :], ident_d.t[:, :], reads=[ident_d], writes=[ident])
    S.dma(SP, cds, vecs.t[:, :], vecs_d.t[:, :], reads=[vecs_d], writes=[vecs])
    S.op(DVE, lambda: V.memset(ones.t[:, :], 1.0), writes=[ones])
    S.op(DVE, lambda: V.memset(epsT.t[:, :], EPS), writes=[epsT])
    for s in range(2):
        c0 = V_CP if s == 0 else V_CS
        S.op(ACT, lambda s=s, c0=c0: A.activation(out=sc.t[:, :, s], in_=vecs.t[:, c0:c0 + 16], func=AF.Silu),
             reads=[vecs], writes=[sc])

    P0 = CONST_END
    NWAD = 4
    wad = [S.tile("wad%d" % i, [128, 16, 512], BF16, P0 + i * 16 * KB) for i in range(NWAD)]
    wad_ds = [S.dsem("wad%d" % i) for i in range(NWAD)]
    psmod = bank[7]
    psmod_v = ps_all[:, 7, 0:192].rearrange("p (n s) -> p n s", s=2)
    mod_groups_done = [0]

    def mod_phase(groups):
        for g in groups:
            i = mod_groups_done[0] % NWAD
            mod_groups_done[0] += 1
            S.dma(POOL, wad_ds[i], wad[i].t[:, :, :],
                  wada_d.t[:, g * 512:(g + 1) * 512].rearrange("(kc p) n -> p kc n", p=128),
                  reads=[wada_d], writes=[wad[i]])
            for q in range(4):
                n = g * 4 + q
                for kc in range(16):
                    S.op(PE, lambda i=i, q=q, kc=kc, n=n: T.matmul(ps_all[:, 7, 2 * n:2 * n + 2], wad[i].t[:, kc, q * 128:(q + 1) * 128],
                                                                   sc.t[:, kc, :], start=(kc == 0), stop=(kc == 15)),
                         reads=[wad[i], sc] if kc == 0 else [], writes=[psmod] if kc == 0 else [], inc=(kc == 15))
                    S.npe -= 1
        for s in range(2):
            for g in groups:
                S.op(DVE, lambda s=s, g=g: V.tensor_tensor(out=mod.t[:, g * 4:(g + 1) * 4, s], in0=psmod_v[:, g * 4:(g + 1) * 4, s],
                                                           in1=vecs.t[:, V_BADA + g * 4:V_BADA + (g + 1) * 4], op=ALU.add),
                     reads=[psmod, vecs], writes=[mod])

    def derive(s, which):
        mv = lambda v: mod.t[:, v * 16:(v + 1) * 16, s]
        if which == 0:
            S.op(DVE, lambda: V.scalar_tensor_tensor(out=der.t[:, s, 0, :], in0=mv(1), scalar=1.0, in1=vecs.t[:, V_GATTN:V_GATTN + 16],
                                                     op0=ALU.add, op1=ALU.mult), reads=[mod, vecs], writes=[der])
            S.op(DVE, lambda: V.tensor_copy(out=der.t[:, s, 1, :], in_=mv(0)), reads=[mod], writes=[der])
        else:
            S.op(DVE, lambda: V.tensor_copy(out=der.t[:, s, 2, :], in_=mv(2)), reads=[mod], writes=[der])
            S.op(DVE, lambda: V.scalar_tensor_tensor(out=der.t[:, s, 3, :], in0=mv(4), scalar=1.0, in1=vecs.t[:, V_GFFN:V_GFFN + 16],
                                                     op0=ALU.add, op1=ALU.mult), reads=[mod, vecs], writes=[der])
            S.op(DVE, lambda: V.tensor_copy(out=der.t[:, s, 4, :], in_=mv(3)), reads=[mod], writes=[der])
            S.op(DVE, lambda: V.tensor_copy(out=der.t[:, s, 5, :], in_=mv(5)), reads=[mod], writes=[der])

    for k in ("WA", "WUK", "WUV"):
        cast_weight(k)
    mod_phase(list(range(0, 8)))
    for k in ("WKNA", "WVNA"):
        cast_weight(k)
    for k in ("WQ", "WUQ", "WO", "WGU", "WD"):
        cast_weight(k, defer=True)
    for s in range(2):
        derive(s, 0)
    gs1 = lambda s, kc: der.t[:, s, 0, kc:kc + 1]
    sh1 = lambda s, kc: der.t[:, s, 1, kc:kc + 1]
    gt1 = lambda s, kc: der.t[:, s, 2, kc:kc + 1]
    gs2 = lambda s, kc: der.t[:, s, 3, kc:kc + 1]
    sh2 = lambda s, kc: der.t[:, s, 4, kc:kc + 1]
    gt2 = lambda s, kc: der.t[:, s, 5, kc:kc + 1]
    if stage == 0:
        mod_phase(list(range(8, 24)))
        for s in range(2):
            derive(s, 1)
        dbg_dump("mod", mod, [128, 96, 2])
        dbg_dump("der", der, [128, 2, 6, 16])
        S.barrier()
        return nc, dbg_out

    def mm_group(bk, out_ap, pairs, reads):
        n = len(pairs)
        for i, (l, r) in enumerate(pairs):
            S.op(PE, lambda l=l, r=r, i=i: T.matmul(out_ap, l, r, start=(i == 0), stop=(i == n - 1)),
                 reads=reads if i == 0 else [], writes=[bk] if i == 0 else [], inc=(i == n - 1))

    class XRing:
        def __init__(self, base, nslots, tag):
            self.tiles = [S.tile("xt%s%d" % (tag, i), [128, D], F32, base + i * 8 * KB) for i in range(nslots)]
            self.ds = [S.dsem("xt%s%d" % (tag, i)) for i in range(nslots)]
            self.n = nslots
            self.k = 0

        def load(self, row0):
            i = self.k % self.n
            self.k += 1
            S.dma(SP, self.ds[i], self.tiles[i].t[:, :], xall.t[row0:row0 + 128, :], reads=[xall], writes=[self.tiles[i]])
            return self.tiles[i]

    def token_stats_scale(xt, junk, ssb, k):
        if isinstance(ssb, list):
            sb_ = ssb[k % len(ssb)]
            col = sb_.t[:, 0:1]
        else:
            sb_ = ssb
            col = ssb.t[:, k:k + 1]
        jk = junk[k % len(junk)] if isinstance(junk, list) else junk
        S.op(ACT, lambda: A.activation(out=jk.t[:, :], in_=xt.t[:, :], func=AF.Square, accum_out=col),
             reads=[xt], writes=[jk, sb_])
        S.op(ACT, lambda: A.activation(out=col, in_=col, func=AF.Sqrt, bias=epsT.t[:, 0:1], scale=1.0 / D),
             reads=[sb_, epsT], writes=[sb_])
        S.op(DVE, lambda: V.reciprocal(out=col, in_=col), reads=[sb_], writes=[sb_])
        S.op(DVE, lambda: V.tensor_scalar(out=xt.t[:, :], in0=xt.t[:, :], scalar1=col, scalar2=None, op0=ALU.mult),
             reads=[xt, sb_], writes=[xt])

    def transpose_to_h(xts, h, s, pbanks, pk):
        for kc in range(16):
            bk = pbanks[pk[0] % len(pbanks)]
            pk[0] += 1
            for t in range(4):
                S.op(PE, lambda t=t, kc=kc, bk=bk: T.transpose(bk.t[:, t * 128:(t + 1) * 128], xts[t].t[:, kc * 128:(kc + 1) * 128], ident.t[:, :]),
                     reads=[xts[t], ident], writes=[bk] if t == 0 else [], inc=(t == 3))
            if kc % 2 == 0:
                S.op(ACT, lambda kc=kc, bk=bk: A.activation(out=h.t[:, kc, :], in_=bk.t[:, :], func=AF.Identity, scale=gs1(s, kc), bias=sh1(s, kc)),
                     reads=[bk, der], writes=[h])
            else:
                S.op(DVE, lambda kc=kc, bk=bk: V.tensor_scalar(out=h.t[:, kc, :], in0=bk.t[:, :], scalar1=gs1(s, kc), scalar2=sh1(s, kc),
                                                                op0=ALU.mult, op1=ALU.add),
                     reads=[bk, der], writes=[h])

    def evac_copy(i, out_ap, in_ap, reads, writes, scale=None):
        if i % 2 == 0:
            if scale is None:
                S.op(ACT, lambda: A.copy(out=out_ap, in_=in_ap), reads=reads, writes=writes)
            else:
                S.op(ACT, lambda: A.mul(out=out_ap, in_=in_ap, mul=scale) if False else A.activation(out=out_ap, in_=in_ap, func=AF.Copy, scale=scale),
                     reads=reads, writes=writes)
        else:
            if scale is None:
                S.op(DVE, lambda: V.tensor_copy(out=out_ap, in_=in_ap), reads=reads, writes=writes)
            else:
                S.op(DVE, lambda: V.tensor_scalar(out=out_ap, in0=in_ap, scalar1=scale, scalar2=None, op0=ALU.mult), reads=reads, writes=writes)

    def rstd_from_bank(bk, dim, rt):
        S.op(ACT, lambda: A.activation(out=rt.t[:, :], in_=bk.t[:, :], func=AF.Sqrt, bias=epsT.t[:, 0:1], scale=1.0 / dim),
             reads=[bk, epsT], writes=[rt])
        S.op(DVE, lambda: V.reciprocal(out=rt.t[:, :], in_=rt.t[:, :]), reads=[rt], writes=[rt])

    S.mark('p1a')
    P1 = CONST_END
    o = P1
    xr = XRing(o, 8, "a"); o += 64 * KB
    hA = [S.tile("hA%d" % i, [128, 16, 512], BF16, o + i * 16 * KB) for i in range(2)]; o += 32 * KB
    wa = S.tile("wa", [128, 6, 16, 128], BF16, o); o += 24 * KB
    wuk = S.tile("wuk", [128, 8, 4, 128], BF16, o); o += 8 * KB
    wuv = S.tile("wuv", [128, 4, 1024], BF16, o); o += 8 * KB
    kvc32 = S.tile("kvc32", [128, 4, 512], F32, o); o += 8 * KB
    sqkv = S.tile("sqkv", [128, 4, 512], BF16, o); o += 4 * KB
    kvn = S.tile("kvn", [128, 4, 512], BF16, o); o += 4 * KB
    kT_out = S.tile("kT_out", [128, 8, 512], BF16, o); o += 8 * KB
    v_out = S.tile("v_out", [128, 8, 4, 128], BF16, o); o += 8 * KB
    t1 = S.tile("t1", [128, 512], F32, o); o += 2 * KB
    t2 = S.tile("t2", [128, 512], F32, o); o += 2 * KB
    krd = S.tile("krd", [128, 512], BF16, o); o += 1 * KB
    rk32 = S.tile("rk32", [128, 512], F32, o); o += 2 * KB
    tabA = [S.tile("tabA%d" % i, [128, 2, 512], F32, o + i * 4 * KB) for i in range(4)]; o += 16 * KB
    junk = [S.tile("junk%d" % i, [128, D], BF16, o + i * 4 * KB) for i in range(2)]; o += 8 * KB
    ssb = [S.tile("ssb%d" % i, [128, 1], F32, o + i * 32) for i in range(8)]; o += 256
    assert o <= BASE + 207 * KB, o
    tab_ds = [S.dsem("tabA%d" % i) for i in range(4)]
    wds = S.dsem("w1a")
    S.dma(SP, wds, wa.t[:, :, :, :], wbf["WA"].t.rearrange("m p k n -> p m k n"), reads=[wbf["WA"]], writes=[wa])
    S.dma(SP, wds, wuk.t[:, :, :, :], wbf["WUK"].t.rearrange("m p k n -> p m k n"), reads=[wbf["WUK"]], writes=[wuk])
    S.dma(SP, wds, wuv.t[:, :, :], wbf["WUV"].t, reads=[wbf["WUV"]], writes=[wuv])
    st_k, st_v, st_pe = S.dsem("st_k"), S.dsem("st_v"), S.dsem("st_pe")

    blocks = [(0, j) for j in range(NBP)] + [(1, j) for j in range(NBS)]
    if stage == 1:
        blocks = blocks[:3]
    pbanks = [bank[0], bank[1]]
    mbanks = [bank[2], bank[3], bank[4], bank[5], bank[6]]
    pk = [0]
    mk = [0]

    def nb():
        b = mbanks[mk[0] % len(mbanks)]
        mk[0] += 1
        return b

    xtiles = {}

    def p1_load(bi):
        s, j = blocks[bi]
        row0 = (0 if s == 0 else SP_) + j * 512
        xtiles[bi] = [xr.load(row0 + t * 128) for t in range(4)]
        S.dma(SP, tab_ds[bi % 4], tabA[bi % 4].t[:, :, :], tabs_d.t[(0 if s == 0 else NBP) + j], reads=[tabs_d], writes=[tabA[bi % 4]])

    def p1_stats(bi):
        for t in range(4):
            token_stats_scale(xtiles[bi][t], junk, ssb, (bi * 4 + t) % 8)

    def p1_trans(bi):
        s, j = blocks[bi]
        transpose_to_h(xtiles[bi], hA[bi % 2], s, pbanks, pk)

    def p1_main(bi):
        s, j = blocks[bi]
        h = hA[bi % 2]
        tab = tabA[bi % 4]
        for m in range(4):
            bk = nb()
            mm_group(bk, bk.t[:, :], [(wa.t[:, m, kc, :], h.t[:, kc, :]) for kc in range(16)], [wa, h])
            S.op(DVE, lambda m=m, bk=bk: V.tensor_copy(out=kvc32.t[:, m, :], in_=bk.t[:, :]), reads=[bk], writes=[kvc32])
            S.op(ACT, lambda m=m: A.activation(out=sqkv.t[:, m, :], in_=kvc32.t[:, m, :], func=AF.Square), reads=[kvc32], writes=[sqkv])
        if KPART < 2:
            return
        bkA = nb()
        mm_group(bkA, bkA.t[:, :], [(wa.t[:, 4, kc, :], h.t[:, kc, :]) for kc in range(16)], [wa, h])
        bkB = nb()
        mm_group(bkB, bkB.t[:, :], [(wa.t[:, 5, kc, :], h.t[:, kc, :]) for kc in range(16)], [wa, h])
        S.op(DVE, lambda: V.tensor_tensor(out=t1.t[:, :], in0=bkA.t[:, :], in1=tab.t[:, 0, :], op=ALU.mult), reads=[bkA, tab], writes=[t1])
        S.op(DVE, lambda: V.tensor_tensor(out=t2.t[:, :], in0=bkB.t[:, :], in1=tab.t[:, 1, :], op=ALU.mult), reads=[bkB, tab], writes=[t2])
        S.op(POOL, lambda: G.tensor_tensor(out=krd.t[:, :], in0=t1.t[:, :], in1=t2.t[:, :], op=ALU.add), reads=[t1, t2], writes=[krd])
        if KPART < 3:
            return
        bk = nb()
        mm_group(bk, bk.t[:, :], [(ones.t[:, :], sqkv.t[:, m, :]) for m in range(4)], [ones, sqkv])
        rstd_from_bank(bk, 512, rk32)
        for m in range(4):
            S.op(DVE, lambda m=m: V.scalar_tensor_tensor(out=kvn.t[:, m, :], in0=kvc32.t[:, m, :], scalar=vecs.t[:, V_GKV + m:V_GKV + m + 1],
                                                         in1=rk32.t[:, :], op0=ALU.mult, op1=ALU.mult),
                 reads=[kvc32, rk32, vecs], writes=[kvn])
        if KPART < 4:
            return
        for hh in range(8):
            bk = nb()
            mm_group(bk, bk.t[:, :], [(wuk.t[:, hh, m, :], kvn.t[:, m, :]) for m in range(4)], [wuk, kvn])
            evac_copy(hh, kT_out.t[:, hh, :], bk.t[:, :], [bk], [kT_out])
        if KPART < 5:
            return
        for t in range(4):
            for c in range(2):
                bk = nb()
                mm_group(bk, bk.t[:, :], [(kvn.t[:, m, t * 128:(t + 1) * 128], wuv.t[:, m, c * 512:(c + 1) * 512]) for m in range(4)], [wuv, kvn])
                evac_copy(t * 2 + c, v_out.t[:, 4 * c:4 * c + 4, t, :], bk.t[:, :].rearrange("p (h d) -> p h d", d=128), [bk], [v_out])

    def p1_store(bi):
        s, j = blocks[bi]
        S.dma(POOL, st_k, KT[s].t[:, :, j * 512:(j + 1) * 512].rearrange("h d t -> d h t"), kT_out.t[:, :, :], reads=[kT_out], writes=[KT[s]])
        S.dma(POOL, st_pe, KPE[s].t[:, j * 512:(j + 1) * 512], krd.t[:, :], reads=[krd], writes=[KPE[s]])
        S.dma(POOL, st_v, VV[s].t[:, :, 4 * j:4 * j + 4, :].rearrange("h p c d -> p h c d"), v_out.t[:, :, :, :], reads=[v_out], writes=[VV[s]])

    nblk = len(blocks)
    p1_load(0)
    p1_load(1)
    p1_stats(0)
    p1_stats(1)
    p1_trans(0)
    p1_load(2)
    for bi in range(nblk):
        if bi + 1 < nblk:
            p1_trans(bi + 1)
        if bi + 2 < nblk:
            p1_stats(bi + 2)
        if bi + 3 < nblk:
            p1_load(bi + 3)
        if SKIP < 2:
            p1_main(bi)
        if SKIP < 1:
            p1_store(bi)
        if pending_casts:
            pending_casts.pop(0)()
    while pending_casts:
        pending_casts.pop(0)()
    if stage == 1:
        S.barrier()
        dbg_dump("h0", hA[0], [128, 16, 512], BF16)
        dbg_dump("KT0", (KT[0].t[:, :, 0:1536], KT[0]), [8, 128, 1536], BF16)
        dbg_dump("KPE0", (KPE[0].t[:, 0:1536], KPE[0]), [128, 1536], BF16)
        dbg_dump("VV0", (VV[0].t[:, :, 0:12, :], VV[0]), [8, 128, 12, 128], BF16)
        S.barrier()
        return nc, dbg_out

    S.mark('p1b')
    o = P1 + 96 * KB
    wkna = S.tile("wkna", [128, 8, 16, 128], BF16, o); o += 32 * KB
    wvna = S.tile("wvna", [128, 16, 1024], BF16, o); o += 32 * KB
    kna_out = S.tile("kna_out", [128, 8, 512], BF16, o); o += 8 * KB
    vna_out = S.tile("vna_out", [128, 4, 1024], BF16, o); o += 8 * KB
    wds2 = S.dsem("w1b")
    S.dma(SP, wds2, wkna.t[:, :, :, :], wbf["WKNA"].t.rearrange("m p k n -> p m k n"), reads=[wbf["WKNA"]], writes=[wkna])
    S.dma(SP, wds2, wvna.t[:, :, :], wbf["WVNA"].t, reads=[wbf["WVNA"]], writes=[wvna])
    st_kn, st_vn = S.dsem("st_kn"), S.dsem("st_vn")
    nblocks = [(0, j) for j in range(NAP // 512)] + [(1, j) for j in range(NAS // 512)]
    if stage == 2:
        nblocks = nblocks[:5]
    xt2 = {}

    def p1b_load(bi):
        s, j = nblocks[bi]
        row0 = (0 if s == 0 else SP_) + j * 512
        xt2[bi] = [xr.load(row0 + t * 128) for t in range(4)]

    def p1b_stats(bi):
        for t in range(4):
            token_stats_scale(xt2[bi][t], junk, ssb, (bi * 4 + t) % 8)

    def p1b_trans(bi):
        s, j = nblocks[bi]
        transpose_to_h(xt2[bi], hA[bi % 2], s, pbanks, pk)

    def p1b_main(bi):
        s, j = nblocks[bi]
        h = hA[bi % 2]
        for hh in range(8):
            bk = nb()
            mm_group(bk, bk.t[:, :], [(wkna.t[:, hh, kc, :], h.t[:, kc, :]) for kc in range(16)], [wkna, h])
            evac_copy(hh, kna_out.t[:, hh, :], bk.t[:, :], [bk], [kna_out])
        for t in range(4):
            for c in range(2):
                bk = nb()
                mm_group(bk, bk.t[:, :], [(h.t[:, kc, t * 128:(t + 1) * 128], wvna.t[:, kc, c * 512:(c + 1) * 512]) for kc in range(16)], [wvna, h])
                evac_copy(t * 2 + c, vna_out.t[:, t, c * 512:(c + 1) * 512], bk.t[:, :], [bk], [vna_out])
        S.dma(POOL, st_kn, KNA[s].t[:, :, j * 512:(j + 1) * 512].rearrange("h d t -> d h t"), kna_out.t[:, :, :], reads=[kna_out], writes=[KNA[s]])
        S.dma(POOL, st_vn, VNA[s].t[j * 512:(j + 1) * 512, :].rearrange("(t p) n -> p t n", p=128), vna_out.t[:, :, :], reads=[vna_out], writes=[VNA[s]])

    nnb = len(nblocks)
    p1b_load(0)
    p1b_load(1)
    p1b_stats(0)
    p1b_stats(1)
    p1b_trans(0)
    p1b_load(2)
    for bi in range(nnb):
        if bi + 1 < nnb:
            p1b_trans(bi + 1)
        if bi + 2 < nnb:
            p1b_stats(bi + 2)
        if bi + 3 < nnb:
            p1b_load(bi + 3)
        p1b_main(bi)

    S.mark('modB')
    mod_phase(list(range(8, 24)))
    for s in range(2):
        derive(s, 1)
    S.barrier()

    Q = CONST_END
    xT = S.tile("xT", [128, 16, 512], F32, Q)
    h2 = S.tile("h2", [128, 16, 512], BF16, Q + 32 * KB)
    HS = Q + 32 * KB
    sq = S.tile("sq", [128, 16, 512], BF16, Q + 48 * KB)
    rA = S.tile("rA", [128, 512], F32, Q + 64 * KB)
    rB = S.tile("rB", [128, 512], F32, Q + 66 * KB)
    tab2 = S.tile("tab2", [128, 2, 512], F32, Q + 68 * KB)
    tmp = [S.tile("tmp%d" % i, [128, 512], F32, Q + (72 + 2 * i) * KB) for i in range(2)]
    junk2 = S.tile("junk2", [128, D], BF16, Q + 72 * KB)
    ssb2 = S.tile("ssb2", [128, 8], F32, BASE + 6 * KB - 64)
    NWR = 4
    WR = Q + 76 * KB
    wr_pair = [S.tile("wrp%d" % i, [128, 2, 16, 128], BF16, WR + i * 8 * KB) for i in range(NWR)]
    wr_uq = [S.tile("wru%d" % i, [128, 8, 4, 128], BF16, WR + i * 8 * KB) for i in range(NWR)]
    wr_ds = [S.dsem("wr%d" % i) for i in range(NWR)]
    wrk = [0]
    AR = Q + 108 * KB
    assert AR + 93 * KB <= BASE + 207 * KB
    xr2 = XRing(AR, 4, "b")
    Kwin = S.tile("Kwin", [128, 8, 896], BF16, AR)
    Vwin = S.tile("Vwin", [128, 7, 1024], BF16, AR + 14 * KB)
    kv_ds = [S.dsem("kv%d" % i) for i in range(NKV)]
    PER = SCK // 128
    SLOT = 6 * KB + 64
    assert NKV * SLOT <= 32 * KB
    Ksl = [S.tile("Ksl%d" % i, [128, SCK], BF16, AR + i * SLOT) for i in range(NKV)]
    Pesl = [S.tile("Pesl%d" % i, [128, SCK], BF16, AR + i * SLOT + 2 * KB) for i in range(NKV)]
    Vsl = [S.tile("Vsl%d" % i, [128, PER, 129], BF16, AR + i * SLOT + 4 * KB) for i in range(NKV)]
    merged = S.tile("merged", [128, 16, 512], BF16, AR)
    aT = S.tile("aT", [128, NFF, 512], BF16, AR)
    qc32 = S.tile("qc32", [128, 4, 512], F32, AR + 36 * KB)
    qn = S.tile("qn", [128, 4, 512], BF16, AR + 44 * KB)
    ona32 = S.tile("ona32", [128, 8, 512], F32, AR + 32 * KB)
    omla32 = S.tile("omla32", [128, 8, 512], F32, AR + 48 * KB)
    qna = S.tile("qna", [128, 8, 512], BF16, AR + 64 * KB)
    qnope = S.tile("qnope", [128, 8, 512], BF16, AR + 72 * KB)
    qrope = S.tile("qrope", [128, 8, 512], BF16, AR + 80 * KB)
    wd = [S.tile("wd%d" % i, [128, NFF, 128], BF16, AR + 44 * KB + i * 11 * KB) for i in range(2)]
    wd_ds = [S.dsem("wd%d" % i) for i in range(2)]
    ytok = [S.tile("ytok%d" % i, [128, D], F32, AR + i * 8 * KB) for i in range(2)]
    y_ds = [S.dsem("y%d" % i) for i in range(2)]
    nbias = [S.tile("nbias%d" % i, [128, 896], F32, HS + i * 3584) for i in range(2)]
    nb_ds = [S.dsem("nbias%d" % i) for i in range(2)]
    sbna = S.tile("sbna", [128, 896], F32, HS + 7168)
    pTna = [S.tile("pTna%d" % i, [128, 896], BF16, HS + 10752 + i * 1792) for i in range(2)]
    rlna = S.tile("rlna", [128, 128], F32, HS + 14336)
    NPT = 6
    pT = [S.tile("pT%d" % i, [128, 512], BF16, HS + i * KB) for i in range(NPT)]
    otok = S.tile("otok", [128, 4, 128], F32, HS + 6 * KB)
    rcol = S.tile("rcol", [128, 4], F32, HS + 8 * KB)
    tab2_ds, kw_ds, vw_ds = S.dsem("tab2"), S.dsem("kwin"), S.dsem("vwin")
    QS_NA = 128.0 ** -0.5
    QS_MLA = 192.0 ** -0.5

    def wr_next():
        i = wrk[0] % NWR
        wrk[0] += 1
        return i

    def ones_rstd(chunks, dim, rt):
        bk = nb()
        mm_group(bk, bk.t[:, :], [(ones.t[:, :], sq.t[:, c, :]) for c in chunks], [ones, sq])
        rstd_from_bank(bk, dim, rt)

    first_block = [True]
    own_blocks = [(0, bb) for bb in range(OWNP // 512)] + [(1, bb) for bb in range(OWNS // 512)]
    if stage == 3:
        own_blocks = own_blocks[:1]
    for (s, bb) in own_blocks:
        jrot = 1 + bb
        L = SP_ if s == 0 else SS_
        row0 = (0 if s == 0 else SP_) + jrot * 512
        nt_seg = (OWNP if s == 0 else OWNS) // 128
        S.mark('A')
        S.dma(SP, tab2_ds, tab2.t[:, :, :], tabs_d.t[(0 if s == 0 else NBP) + jrot], reads=[tabs_d], writes=[tab2])
        xts = [xr2.load(row0 + t * 128) for t in range(4)]
        for kc in range(16):
            bk = pbanks[pk[0] % 2]
            pk[0] += 1
            for t in range(4):
                S.op(PE, lambda t=t, kc=kc, bk=bk: T.transpose(bk.t[:, t * 128:(t + 1) * 128], xts[t].t[:, kc * 128:(kc + 1) * 128], ident.t[:, :]),
                     reads=[xts[t], ident], writes=[bk] if t == 0 else [], inc=(t == 3))
            evac_copy(kc, xT.t[:, kc, :], bk.t[:, :], [bk], [xT])
        for t in range(4):
            token_stats_scale(xts[t], junk2, ssb2, t)
        transpose_to_h(xts, h2, s, pbanks, pk)
        for m in range(12):
            if m % 2 == 0:
                wi = wr_next()
                S.dma(SP, wr_ds[wi], wr_pair[wi].t[:, :, :, :], wbf["WQ"].t[m:m + 2].rearrange("m p k n -> p m k n"), reads=[wbf["WQ"]], writes=[wr_pair[wi]])
            w = wr_pair[wi]
            bk = nb()
            mm_group(bk, bk.t[:, :], [(w.t[:, m % 2, kc, :], h2.t[:, kc, :]) for kc in range(16)], [w, h2])
            if m < 8:
                evac_copy(m, qna.t[:, m, :], bk.t[:, :], [bk], [qna], scale=QS_NA)
            else:
                S.op(DVE, lambda m=m, bk=bk: V.tensor_copy(out=qc32.t[:, m - 8, :], in_=bk.t[:, :]), reads=[bk], writes=[qc32])
                S.op(ACT, lambda m=m: A.activation(out=sq.t[:, m - 8, :], in_=qc32.t[:, m - 8, :], func=AF.Square), reads=[qc32], writes=[sq])
        ones_rstd(range(4), 512, rA)
        for m in range(4):
            S.op(DVE, lambda m=m: V.scalar_tensor_tensor(out=qn.t[:, m, :], in0=qc32.t[:, m, :], scalar=vecs.t[:, V_GQ + m:V_GQ + m + 1],
                                                         in1=rA.t[:, :], op0=ALU.mult, op1=ALU.mult), reads=[qc32, rA, vecs], writes=[qn])
        wu = []
        for i in range(2):
            wi = wr_next()
            S.dma(SP, wr_ds[wi], wr_uq[wi].t[:, :, :, :], wbf["WUQ"].t[8 * i:8 * i + 8].rearrange("m p k n -> p m k n"), reads=[wbf["WUQ"]], writes=[wr_uq[wi]])
            wu.append(wr_uq[wi])
        for hh in range(8):
            bk = nb()
            mm_group(bk, bk.t[:, :], [(wu[0].t[:, hh, m, :], qn.t[:, m, :]) for m in range(4)], [wu[0], qn])
            evac_copy(hh, qnope.t[:, hh, :], bk.t[:, :], [bk], [qnope], scale=QS_MLA)
        if first_block[0]:
            first_block[0] = False
            S.op(POOL, lambda: G.memset(qrope.t[:, :, :], 0.0), writes=[qrope])
        for hh in range(8):
            pr, e = hh // 2, hh % 2
            bkA = nb()
            mm_group(bkA, bkA.t[0:64, :], [(wu[1].t[:, pr, m, 64 * e:64 * e + 64], qn.t[:, m, :]) for m in range(4)], [wu[1], qn])
            bkB = nb()
            mm_group(bkB, bkB.t[0:64, :], [(wu[1].t[:, 4 + pr, m, 64 * e:64 * e + 64], qn.t[:, m, :]) for m in range(4)], [wu[1], qn])
            S.op(DVE, lambda bkA=bkA: V.scalar_tensor_tensor(out=tmp[0].t[0:64, :], in0=bkA.t[0:64, :], scalar=QS_MLA, in1=tab2.t[0:64, 0, :], op0=ALU.mult, op1=ALU.mult),
                 reads=[bkA, tab2], writes=[tmp[0]])
            S.op(DVE, lambda bkB=bkB: V.scalar_tensor_tensor(out=tmp[1].t[0:64, :], in0=bkB.t[0:64, :], scalar=QS_MLA, in1=tab2.t[0:64, 1, :], op0=ALU.mult, op1=ALU.mult),
                 reads=[bkB, tab2], writes=[tmp[1]])
            S.op(POOL, lambda hh=hh: G.tensor_tensor(out=qrope.t[0:64, hh, :], in0=tmp[0].t[0:64, :], in1=tmp[1].t[0:64, :], op=ALU.add),
                 reads=[tmp[0], tmp[1]], writes=[qrope])
        if KPART >= 2:
            S.mark('B')
            for qt in range(4):
                ti = bb * 4 + qt
                slot = _slot_of(ti, nt_seg)
                k0 = 128 + 128 * ti
                S.dma(SP, kw_ds, Kwin.t[:, :, :], KNA[s].t[:, :, k0:k0 + 896].rearrange("h d t -> d h t"), reads=[KNA[s]], writes=[Kwin])
                S.dma(SP, vw_ds, Vwin.t[:, :, :], VNA[s].t[k0:k0 + 896, :].rearrange("(c p) n -> p c n", p=128), reads=[VNA[s]], writes=[Vwin])
                for hh in range(8):
                    i2 = hh % 2
                    S.dma(SP, nb_ds[i2], nbias[i2].t[:, :], nab_d.t[s, slot, hh], reads=[nab_d], writes=[nbias[i2]])
                    bS = [bank[2 * i2], bank[2 * i2 + 1]]
                    ps2 = ps_all[:, 2 * i2:2 * i2 + 2, :].rearrange("p a b -> p (a b)")
                    for c in range(7):
                        S.op(PE, lambda c=c, hh=hh, ps2=ps2: T.matmul(ps2[:, c * 128:(c + 1) * 128], Kwin.t[:, hh, c * 128:(c + 1) * 128],
                                                                      qna.t[:, hh, qt * 128:(qt + 1) * 128], start=True, stop=True),
                             reads=[Kwin, qna] if c == 0 else [], writes=bS if c == 0 else [], inc=(c == 6))
                    S.op(DVE, lambda ps2=ps2, i2=i2: V.tensor_tensor(out=sbna.t[:, :], in0=ps2[:, 0:896], in1=nbias[i2].t[:, :], op=ALU.add),
                         reads=bS + [nbias[i2]], writes=[sbna])
                    S.op(ACT, lambda i2=i2: A.activation(out=pTna[i2].t[:, :], in_=sbna.t[:, :], func=AF.Exp), reads=[sbna], writes=[pTna[i2]])
                    bO = bank[4 + i2]
                    bL = bank[6 + i2]
                    mm_group(bO, bO.t[:, 0:128], [(Vwin.t[:, c, hh * 128:(hh + 1) * 128], pTna[i2].t[:, c * 128:(c + 1) * 128]) for c in range(7)], [Vwin, pTna[i2]])
                    mm_group(bL, bL.t[:, 0:128], [(ones.t[:, :], pTna[i2].t[:, c * 128:(c + 1) * 128]) for c in range(7)], [ones, pTna[i2]])
                    S.op(DVE, lambda bL=bL: V.reciprocal(out=rlna.t[:, :], in_=bL.t[:, 0:128]), reads=[bL], writes=[rlna])
                    S.op(DVE, lambda bO=bO, hh=hh, qt=qt: V.tensor_tensor(out=ona32.t[:, hh, qt * 128:(qt + 1) * 128], in0=bO.t[:, 0:128], in1=rlna.t[:, :], op=ALU.mult),
                         reads=[bO, rlna], writes=[ona32])
            for hh in range(8):
                S.op(ACT, lambda hh=hh: A.activation(out=sq.t[:, hh, :], in_=ona32.t[:, hh, :], func=AF.Square), reads=[ona32], writes=[sq])
        if KPART >= 3:
            S.mark('C')
            nch = L // 128
            kvk = [0]
            for i in range(NKV):
                S.op(DVE, lambda i=i: V.memset(Vsl[i].t[:, :, 128:129], 1.0), writes=[Vsl[i]])
            obanks = [bank[3], bank[4], bank[5], bank[6]]
            epi = [None]
            for hh in range(8):
                pend = []

                def emit_pv(p, hh=hh):
                    g, si, c = p
                    for t in range(4):
                        S.op(PE, lambda t=t: T.matmul(obanks[t].t[:, 0:129], pT[g % NPT].t[:, t * 128:(t + 1) * 128], Vsl[si].t[:, c, :],
                                                      start=(g == 0), stop=(g == nch - 1)),
                             reads=[Vsl[si], pT[g % NPT]] if t == 0 else [], writes=obanks if (t == 0 and g in (0, nch - 1)) else [], inc=(t == 3))

                for g in range(nch):
                    c = g % PER
                    if c == 0:
                        si = kvk[0] % NKV
                        kvk[0] += 1
                        sc_ = g // PER
                        S.dma(SP, kv_ds[si], Ksl[si].t[:, :], KT[s].t[hh, :, sc_ * SCK:(sc_ + 1) * SCK], reads=[KT[s]], writes=[Ksl[si]])
                        S.dma(SP, kv_ds[si], Pesl[si].t[:, :], KPE[s].t[:, sc_ * SCK:(sc_ + 1) * SCK], reads=[KPE[s]], writes=[Pesl[si]])
                        S.dma(SP, kv_ds[si], Vsl[si].t[:, :, 0:128], VV[s].t[hh, :, sc_ * PER:(sc_ + 1) * PER, :], reads=[VV[s]], writes=[Vsl[si]])
                    bS = bank[g % 3]
                    S.op(PE, lambda si=si, c=c, bS=bS: T.matmul(bS.t[:, :], Ksl[si].t[:, c * 128:(c + 1) * 128], qnope.t[:, hh, :], start=True, stop=False),
                         reads=[Ksl[si], qnope], writes=[bS], inc=False)
                    S.op(PE, lambda si=si, c=c, bS=bS: T.matmul(bS.t[:, :], Pesl[si].t[:, c * 128:(c + 1) * 128],
                                                                qrope.t[:, hh, :], start=False, stop=True),
                         reads=[Pesl[si], qrope], writes=[], inc=True)
                    if len(pend) >= 2:
                        emit_pv(pend.pop(0))
                    S.op(ACT, lambda g=g, bS=bS: A.activation(out=pT[g % NPT].t[:, :], in_=bS.t[:, :], func=AF.Exp), reads=[bS], writes=[pT[g % NPT]])
                    pend.append((g, si, c))
                    if g == 4 and epi[0] is not None:
                        epi[0]()
                        epi[0] = None
                for p in pend:
                    emit_pv(p)
                for t in range(4):
                    S.op(DVE, lambda t=t: V.reciprocal(out=rcol.t[:, t:t + 1], in_=obanks[t].t[:, 128:129]), reads=[obanks[t]], writes=[rcol])
                    S.op(DVE, lambda t=t: V.tensor_scalar(out=otok.t[:, t, :], in0=obanks[t].t[:, 0:128], scalar1=rcol.t[:, t:t + 1], scalar2=None, op0=ALU.mult),
                         reads=[obanks[t], rcol], writes=[otok])

                def epilogue(hh=hh):
                    for t in range(4):
                        S.op(PE, lambda t=t: T.transpose(bank[7].t[:, t * 128:(t + 1) * 128], otok.t[:, t, :], ident.t[:, :]),
                             reads=[otok, ident] if t == 0 else [], writes=[bank[7]] if t == 0 else [], inc=(t == 3))
                    S.op(DVE, lambda: V.tensor_copy(out=omla32.t[:, hh, :], in_=bank[7].t[:, :]), reads=[bank[7]], writes=[omla32])
                    S.op(POOL, lambda: G.tensor_tensor(out=sq.t[:, 8 + hh, :], in0=omla32.t[:, hh, :], in1=omla32.t[:, hh, :], op=ALU.mult),
                         reads=[omla32], writes=[sq])

                epi[0] = epilogue
            epi[0]()
        if KPART >= 4:
            S.mark('D')
            ones_rstd(range(0, 8), 1024, rA)
            ones_rstd(range(8, 16), 1024, rB)
            for hh in range(8):
                S.op(DVE, lambda hh=hh: V.scalar_tensor_tensor(out=merged.t[:, hh, :], in0=ona32.t[:, hh, :], scalar=vecs.t[:, V_GNA + hh:V_GNA + hh + 1],
                                                               in1=rA.t[:, :], op0=ALU.mult, op1=ALU.mult), reads=[ona32, rA, vecs], writes=[merged])
                S.op(DVE, lambda hh=hh: V.scalar_tensor_tensor(out=merged.t[:, 8 + hh, :], in0=omla32.t[:, hh, :], scalar=vecs.t[:, V_GMLA + hh:V_GMLA + hh + 1],
                                                               in1=rB.t[:, :], op0=ALU.mult, op1=ALU.mult), reads=[omla32, rB, vecs], writes=[merged])
            for n in range(16):
                if n % 2 == 0:
                    wi = wr_next()
                    S.dma(SP, wr_ds[wi], wr_pair[wi].t[:, :, :, :], wbf["WO"].t[n:n + 2].rearrange("m p k n -> p m k n"), reads=[wbf["WO"]], writes=[wr_pair[wi]])
                w = wr_pair[wi]
                bk = nb()
                mm_group(bk, bk.t[:, :], [(w.t[:, n % 2, kc, :], merged.t[:, kc, :]) for kc in range(16)], [w, merged])
                S.op(DVE, lambda n=n, bk=bk: V.scalar_tensor_tensor(out=xT.t[:, n, :], in0=bk.t[:, :], scalar=gt1(s, n), in1=xT.t[:, n, :], op0=ALU.mult, op1=ALU.add),
                     reads=[bk, xT, der], writes=[xT])
                S.op(ACT, lambda n=n: A.activation(out=sq.t[:, n, :], in_=xT.t[:, n, :], func=AF.Square), reads=[xT], writes=[sq])
            ones_rstd(range(16), 2048, rA)
            for kc in range(16):
                tt = tmp[kc % 2]
                S.op(DVE, lambda kc=kc, tt=tt: V.scalar_tensor_tensor(out=tt.t[:, :], in0=xT.t[:, kc, :], scalar=gs2(s, kc), in1=rA.t[:, :], op0=ALU.mult, op1=ALU.mult),
                     reads=[xT, rA, der], writes=[tt])
                S.op(ACT, lambda kc=kc, tt=tt: A.activation(out=h2.t[:, kc, :], in_=tt.t[:, :], func=AF.Identity, bias=sh2(s, kc), scale=1.0),
                     reads=[tt, der], writes=[h2])
        if KPART >= 5:
            S.mark('E')
            for c in range(NFF):
                wi = wr_next()
                w = wr_pair[wi]
                S.dma(SP, wr_ds[wi], w.t[:, :, :, :], wbf["WGU"].t[c], reads=[wbf["WGU"]], writes=[w])
                bG = nb()
                mm_group(bG, bG.t[:, :], [(w.t[:, 0, kc, :], h2.t[:, kc, :]) for kc in range(16)], [w, h2])
                bU = nb()
                mm_group(bU, bU.t[:, :], [(w.t[:, 1, kc, :], h2.t[:, kc, :]) for kc in range(16)], [w, h2])
                tt = tmp[c % 2]
                S.op(ACT, lambda bG=bG, tt=tt: A.activation(out=tt.t[:, :], in_=bG.t[:, :], func=AF.Silu), reads=[bG], writes=[tt])
                S.op(DVE, lambda c=c, bU=bU, tt=tt: V.tensor_tensor(out=aT.t[:, c, :], in0=bU.t[:, :], in1=tt.t[:, :], op=ALU.mult), reads=[bU, tt], writes=[aT])
            for n in range(16):
                w = wd[n % 2]
                S.dma(SP, wd_ds[n % 2], w.t[:, :, :], wbf["WD"].t[n], reads=[wbf["WD"]], writes=[w])
                bk = nb()
                mm_group(bk, bk.t[:, :], [(w.t[:, c, :], aT.t[:, c, :]) for c in range(NFF)], [w, aT])
                S.op(DVE, lambda n=n, bk=bk: V.scalar_tensor_tensor(out=xT.t[:, n, :], in0=bk.t[:, :], scalar=gt2(s, n), in1=xT.t[:, n, :], op0=ALU.mult, op1=ALU.add),
                     reads=[bk, xT, der], writes=[xT])
                S.op(ACT, lambda n=n: A.activation(out=sq.t[:, n, :], in_=xT.t[:, n, :], func=AF.Square), reads=[xT], writes=[sq])
        S.mark('F')
        ones_rstd(range(16), 2048, rA)
        for n in range(16):
            S.op(DVE, lambda n=n: V.scalar_tensor_tensor(out=xT.t[:, n, :], in0=xT.t[:, n, :], scalar=vecs.t[:, V_GFIN + n:V_GFIN + n + 1], in1=rA.t[:, :],
                                                         op0=ALU.mult, op1=ALU.mult), reads=[xT, rA, vecs], writes=[xT])
        for t in range(4):
            yt = ytok[t % 2]
            for g4 in range(4):
                bk = pbanks[pk[0] % 2]
                pk[0] += 1
                for i in range(4):
                    S.op(PE, lambda i=i, g4=g4, t=t, bk=bk: T.transpose(bk.t[:, i * 128:(i + 1) * 128], xT.t[:, g4 * 4 + i, t * 128:(t + 1) * 128], ident.t[:, :]),
                         reads=[xT, ident], writes=[bk] if i == 0 else [], inc=(i == 3))
                evac_copy(g4, yt.t[:, g4 * 512:(g4 + 1) * 512], bk.t[:, :], [bk], [yt])
            r0 = bb * 512 + t * 128
            S.dma(POOL, y_ds[t % 2], yout[s].t[r0:r0 + 128, :], yt.t[:, :], reads=[yt], writes=[yout[s]])
    S.barrier()
    S.mark('end')
    dbg_out['marks'] = S.marks
    return nc, dbg_out


def kernel(x_prompt, x_sample, c_prompt, c_sample, w_ada, b_ada, g_attn, w_in, rpb, g_q, w_uq, g_kv, w_ukv,
           g_out_na, g_out_mla, w_o, g_ffn, w_gate, w_up, w_down, g_final):
    inp = dict(x_prompt=x_prompt, x_sample=x_sample, c_prompt=c_prompt, c_sample=c_sample, w_ada=w_ada, b_ada=b_ada,
               g_attn=g_attn, w_in=w_in, rpb=rpb, g_q=g_q, w_uq=w_uq, g_kv=g_kv, w_ukv=w_ukv, g_out_na=g_out_na,
               g_out_mla=g_out_mla, w_o=w_o, g_ffn=g_ffn, w_gate=w_gate, w_up=w_up, w_down=w_down, g_final=g_final)
    maps = _prep(inp)
    nc, _ = build()
    res = run_bass_kernel_spmd(nc, maps, core_ids=list(range(NCORES)))
    yp = np.concatenate([np.asarray(r["y0"], np.float32) for r in res.results], axis=0)[None]
    ys = np.concatenate([np.asarray(r["y1"], np.float32) for r in res.results], axis=0)[None]
    return (yp, ys)
```

```python
import numpy as np
import concourse.bass as bass
import concourse.mybir as mybir
from concourse.bass_utils import run_bass_kernel_spmd

F32 = mybir.dt.float32
BF16 = mybir.dt.bfloat16
AF = mybir.ActivationFunctionType
ALU = mybir.AluOpType

NCORES = 8
D = 2048
KC = 16
SP_, SS_ = 8192, 16384
OWNP, OWNS = 1024, 2048
HALO = 512
NAP, NAS = OWNP + 2 * HALO, OWNS + 2 * HALO
NBP, NBS = SP_ // 512, SS_ // 512
DFF = 5632
NFF = DFF // 128
EPS = 1e-6
NEG = -30000.0
SCK = 1024
NKV = 5
import os
SKIP = int(os.environ.get('KSKIP', '0'))
KPART = int(os.environ.get('KPART', '9'))

V_BADA, V_GATTN, V_GFFN, V_GFIN, V_GQ, V_GKV, V_GNA, V_GMLA, V_CP, V_CS, NV = 0, 96, 112, 128, 144, 148, 152, 160, 168, 184, 200


class Eng:
    def __init__(self, name, h, sem, compute=True):
        self.name, self.h, self.sem, self.cnt, self.seen, self.compute = name, h, sem, 0, {}, compute


class DSem:
    def __init__(self, sem):
        self.sem, self.cnt = sem, 0


class Buf:
    def __init__(self, name, space, lo, hi):
        self.name, self.space, self.lo, self.hi = name, space, lo, hi
        self.w = None
        self.r = {}
        self.ov = [self]


class Tile:
    def __init__(self, t, buf):
        self.t, self.buf = t, buf


class Sched:
    def __init__(self, nc):
        self.nc = nc
        self.bufs = []
        self.dsems = []
        mk = lambda n, h, c=True: Eng(n, h, nc.alloc_semaphore("sem_" + n), c)
        self.PE = mk("pe", nc.tensor)
        self.ACT = mk("act", nc.scalar)
        self.DVE = mk("dve", nc.vector)
        self.POOL = mk("pool", nc.gpsimd)
        self.SP = mk("sp", nc.sync, False)
        self.engs = [self.PE, self.ACT, self.DVE, self.POOL, self.SP]
        self.nsem = 5
        self.npe = 0
        self.marks = []

    def dsem(self, name):
        d = DSem(self.nc.alloc_semaphore("ds_" + name))
        d.name = name
        self.dsems.append(d)
        self.nsem += 1
        return d

    def _reg(self, b):
        for o in self.bufs:
            if o.space == b.space and o.lo < b.hi and b.lo < o.hi:
                o.ov.append(b)
                b.ov.append(o)
        self.bufs.append(b)
        return b

    def tile(self, name, shape, dtype, off):
        esz = 2 if dtype == BF16 else 4
        n = 1
        for s in shape[1:]:
            n *= s
        t = self.nc.alloc_sbuf_tensor_at(name, list(shape), dtype, offset=int(off))
        return Tile(t, self._reg(Buf(name, "sb", int(off), int(off) + n * esz)))

    def dram(self, name, shape, dtype, kind):
        t = self.nc.dram_tensor(name, list(shape), dtype, kind=kind).ap()
        return Tile(t, self._reg(Buf(name, "dram_" + name, 0, 1)))

    def _wait(self, eng, ev):
        sem, val, _ = ev
        k = id(sem)
        if eng.seen.get(k, 0) >= val:
            return
        eng.seen[k] = val
        eng.h.wait_ge(sem, val)

    def _deps(self, eng, reads, writes, is_dma):
        for t in reads:
            for o in t.buf.ov:
                if o.w is not None:
                    if o.w[2] is eng and not is_dma and eng is self.PE:
                        continue
                    self._wait(eng, o.w)
                if o.space == "ps":
                    for e in o.r.values():
                        if e[2] is not eng:
                            self._wait(eng, e)
        for t in writes:
            for o in t.buf.ov:
                if o.w is not None and (is_dma or o.w[2] is not eng):
                    self._wait(eng, o.w)
                for e in o.r.values():
                    if is_dma or e[2] is not eng:
                        self._wait(eng, e)

    def mark(self, name):
        self.marks.append((name, self.npe))

    def op(self, eng, fn, reads=(), writes=(), inc=True):
        if eng is self.PE:
            self.npe += 1
        self._deps(eng, reads, writes, False)
        ins = fn()
        if inc:
            eng.cnt += 1
            ins.then_inc(eng.sem, 1)
            ev = (eng.sem, eng.cnt, eng)
        else:
            ev = (eng.sem, eng.cnt + 1, eng)
        for t in reads:
            t.buf.r[eng.name] = ev
        for t in writes:
            t.buf.w = ev
            t.buf.r = {}
        return ev

    def dma(self, q, ds, out, in_, reads=(), writes=()):
        self._deps(q, reads, writes, True)
        ds.cnt += 16
        q.h.dma_start(out=out, in_=in_).then_inc(ds.sem, 16)
        ev = (ds.sem, ds.cnt, None)
        for t in reads:
            t.buf.r[("d", id(ds))] = ev
        for t in writes:
            t.buf.w = ev
            t.buf.r = {}
        return ev

    def barrier(self):
        for e in self.engs:
            for o in self.engs:
                if o is not e and o.cnt > 0:
                    self._wait(e, (o.sem, o.cnt, o))
            for d in self.dsems:
                if d.cnt > 0:
                    self._wait(e, (d.sem, d.cnt, None))


def _SL(w):
    K, N = w.shape
    return np.ascontiguousarray(w.reshape(K // 128, 128, N // 128, 128).transpose(2, 1, 0, 3))


def _ML(w):
    K, N = w.shape
    return np.ascontiguousarray(w.reshape(K // 128, 128, N).transpose(1, 0, 2))


def _fm(v):
    return np.ascontiguousarray(v.reshape(-1, 128).T)


def _rope_tables(pos):
    inv = (np.float32(10000.0) ** (-(np.arange(0, 64, 2, dtype=np.float32)) / np.float32(64))).astype(np.float32)
    ang = pos.astype(np.float32)[:, None] * inv[None, :]
    ang = np.concatenate([ang, ang], axis=-1).astype(np.float32)
    c = np.cos(ang).astype(np.float32).T
    s = np.sin(ang).astype(np.float32).T
    s = np.concatenate([-s[:32], s[32:]], axis=0)
    return np.concatenate([c, c], 0), np.concatenate([s, s], 0)


def _na_bias_variant(rpb, r, rows):
    i = np.arange(14)[:, None, None, None]
    kc = np.arange(64)[None, :, None, None]
    qq = np.arange(2)[None, None, :, None]
    qc = np.arange(64)[None, None, None, :]
    kr = r - 6 + i
    qr = r + qq
    rs = np.clip(qr - 4, 0, rows - 8)
    cs = np.clip(qc - 8, 0, 64 - 16)
    valid = (kr >= 0) & (kr < rows) & (kr >= rs) & (kr < rs + 8) & (kc >= cs) & (kc < cs + 16)
    dr = np.clip(kr - qr + 7, 0, 14)
    dc = np.clip(kc - qc + 15, 0, 30)
    dr, dc, valid = np.broadcast_arrays(dr, dc, valid)
    g = rpb[:, dr, dc]
    g = np.where(valid[None], g, np.float32(NEG)).astype(np.float32)
    g = g.reshape(8, 7, 128, 128)
    return np.ascontiguousarray(g.transpose(0, 2, 1, 3).reshape(8, 128, 896))


def _slot_of(i, n):
    return 1 if i == 0 else 2 if i == 1 else 3 if i == n - 2 else 4 if i == n - 1 else 0


def _prep(inp, cores=None):
    f32 = np.float32
    w_in = np.asarray(inp["w_in"][0], f32)
    w_uq = np.asarray(inp["w_uq"][0], f32)
    w_ukv = np.asarray(inp["w_ukv"][0], f32)
    perm = (np.arange(64) + 32) % 64
    kpe = w_in[:, 4096:4160]
    WA = np.concatenate([w_in[:, 3584:4096], kpe, kpe, kpe[:, perm], kpe[:, perm]], axis=1)
    uk = w_ukv.reshape(512, 8, 256)[:, :, :128].reshape(512, 1024)
    uv = w_ukv.reshape(512, 8, 256)[:, :, 128:].reshape(512, 1024)
    uq = w_uq.reshape(512, 8, 192)
    uqn = uq[:, :, :128].reshape(512, 1024)
    uqr = uq[:, :, 128:].reshape(512, 512)
    uqp = uq[:, :, 128:][:, :, perm].reshape(512, 512)
    WQ = np.concatenate([w_in[:, 0:1024], w_in[:, 3072:3584]], axis=1)
    WUQ = np.concatenate([uqn, uqr, uqp], axis=1)
    wg = _SL(np.asarray(inp["w_gate"][0], f32))
    wu = _SL(np.asarray(inp["w_up"][0], f32))
    WGU = np.ascontiguousarray(np.stack([wg, wu], axis=2))
    vecs = np.zeros((128, NV), f32)
    vecs[:, V_BADA:V_BADA + 96] = _fm(np.asarray(inp["b_ada"][0], f32))
    vecs[:, V_GATTN:V_GATTN + 16] = _fm(np.asarray(inp["g_attn"][0], f32))
    vecs[:, V_GFFN:V_GFFN + 16] = _fm(np.asarray(inp["g_ffn"][0], f32))
    vecs[:, V_GFIN:V_GFIN + 16] = _fm(np.asarray(inp["g_final"], f32))
    vecs[:, V_GQ:V_GQ + 4] = _fm(np.asarray(inp["g_q"][0], f32))
    vecs[:, V_GKV:V_GKV + 4] = _fm(np.asarray(inp["g_kv"][0], f32))
    vecs[:, V_GNA:V_GNA + 8] = _fm(np.asarray(inp["g_out_na"][0], f32))
    vecs[:, V_GMLA:V_GMLA + 8] = _fm(np.asarray(inp["g_out_mla"][0], f32))
    vecs[:, V_CP:V_CP + 16] = _fm(np.asarray(inp["c_prompt"][0], f32))
    vecs[:, V_CS:V_CS + 16] = _fm(np.asarray(inp["c_sample"][0], f32))
    shared = {
        "vecs": vecs,
        "w_ada": np.ascontiguousarray(np.asarray(inp["w_ada"][0], f32)),
        "ident": np.eye(128, dtype=f32),
        "WA": _SL(WA), "WUK": _SL(uk), "WUV": _ML(uv),
        "WKNA": _SL(w_in[:, 1024:2048]), "WVNA": _ML(w_in[:, 2048:3072]),
        "WQ": _SL(WQ), "WUQ": _SL(WUQ), "WO": _SL(np.asarray(inp["w_o"][0], f32)),
        "WGU": WGU, "WD": _SL(np.asarray(inp["w_down"][0], f32)),
    }
    rpb = np.asarray(inp["rpb"][0], f32)
    xp = np.asarray(inp["x_prompt"][0], f32)
    xs = np.asarray(inp["x_sample"][0], f32)
    variants = {}

    def variant(r, rows):
        key = ("t", r) if r < 4 else ("b", rows - r) if r >= rows - 4 else ("i",)
        if key not in variants:
            variants[key] = _na_bias_variant(rpb, r, rows)
        return variants[key]

    maps = []
    for c in (range(NCORES) if cores is None else cores):
        sp0 = (c * OWNP - HALO) % SP_
        ss0 = (c * OWNS - HALO) % SS_
        xall = np.concatenate([np.roll(xp, -sp0, axis=0), np.roll(xs, -ss0, axis=0)], axis=0)
        pos = np.concatenate([(sp0 + np.arange(SP_)) % SP_, (ss0 + np.arange(SS_)) % SS_])
        c2, s2 = _rope_tables(pos)
        tab = np.stack([c2, s2], axis=1).reshape(128, 2, NBP + NBS, 512).transpose(2, 0, 1, 3)
        nab = np.zeros((2, 5, 8, 128, 896), f32)
        for sg, (n, rows, row0) in enumerate(((OWNP // 128, SP_ // 64, c * (OWNP // 64)), (OWNS // 128, SS_ // 64, c * (OWNS // 64)))):
            tiles = {0: 2, 1: 0, 2: 1, 3: n - 2, 4: n - 1}
            for slot, i in tiles.items():
                nab[sg, slot] = variant(row0 + 2 * i, rows)
        m = dict(shared)
        m["xall"] = xall
        m["tabs"] = np.ascontiguousarray(tab)
        m["nab"] = nab
        maps.append(m)
    return maps


def build(stage=99, dbg=False):
    nc = bass.Bass("TRN2", target_bir_lowering=False)
    S = Sched(nc)
    PE, ACT, DVE, POOL, SP = S.PE, S.ACT, S.DVE, S.POOL, S.SP
    T, V, G, A = nc.tensor, nc.vector, nc.gpsimd, nc.scalar
    BASE = 16512
    KB = 1024

    din = lambda n, sh: S.dram(n, sh, F32, "ExternalInput")
    xall = din("xall", [SP_ + SS_, D])
    vecs_d = din("vecs", [128, NV])
    wada_d = din("w_ada", [D, 6 * D])
    ident_d = din("ident", [128, 128])
    tabs_d = din("tabs", [NBP + NBS, 128, 2, 512])
    nab_d = din("nab", [2, 5, 8, 128, 896])
    wshapes = {"WA": [6, 128, 16, 128], "WUK": [8, 128, 4, 128], "WUV": [128, 4, 1024],
               "WKNA": [8, 128, 16, 128], "WVNA": [128, 16, 1024], "WQ": [12, 128, 16, 128],
               "WUQ": [16, 128, 4, 128], "WO": [16, 128, 16, 128], "WGU": [NFF, 128, 2, 16, 128],
               "WD": [16, 128, NFF, 128]}
    w32 = {k: din(k, sh) for k, sh in wshapes.items()}
    wbf = {k: S.dram(k + "_bf", sh, BF16, "Internal") for k, sh in wshapes.items()}
    KT = [S.dram("KT%d" % s, [8, 128, L], BF16, "Internal") for s, L in enumerate((SP_, SS_))]
    KPE = [S.dram("KPE%d" % s, [128, L], BF16, "Internal") for s, L in enumerate((SP_, SS_))]
    VV = [S.dram("VV%d" % s, [8, 128, L // 128, 128], BF16, "Internal") for s, L in enumerate((SP_, SS_))]
    KNA = [S.dram("KNA%d" % s, [8, 128, L], BF16, "Internal") for s, L in enumerate((NAP, NAS))]
    VNA = [S.dram("VNA%d" % s, [L, 1024], BF16, "Internal") for s, L in enumerate((NAP, NAS))]
    yout = [S.dram("y%d" % s, [L, D], F32, "ExternalOutput") for s, L in enumerate((OWNP, OWNS))]
    dbg_out = {}

    def dbg_dump(name, tile_, shape, dtype=F32):
        if not dbg:
            return
        o = S.dram("dbg_" + name, shape, dtype, "ExternalOutput")
        dbg_out[name] = o
        S.dma(SP, S.dsem("dbg_" + name), o.t, tile_.t.ap() if isinstance(tile_, Tile) else tile_[0], reads=[tile_ if isinstance(tile_, Tile) else tile_[1]], writes=[o])

    pending_casts = []

    def cast_weight(k, defer=False):
        ds = S.dsem("cast_" + k)
        n = 1
        for s_ in wshapes[k]:
            n *= s_
        rows = n // 2048
        src = w32[k].t
        dst = wbf[k].t
        names = "abcde"[:len(wshapes[k])]
        pat = " ".join(names)
        src2 = src.rearrange(f"{pat} -> ({pat})").rearrange("(r c) -> r c", c=2048)
        dst2 = dst.rearrange(f"{pat} -> ({pat})").rearrange("(r c) -> r c", c=2048)
        step = 2048
        pieces = list(range(0, rows, step))

        def piece(r0, last):
            r1 = min(rows, r0 + step)
            S.dma(POOL, ds, dst2[r0:r1, :], src2[r0:r1, :], reads=[w32[k]], writes=[])
            if last:
                wbf[k].buf.w = (ds.sem, ds.cnt, None)

        for r0 in pieces:
            fn = (lambda r0=r0: piece(r0, r0 == pieces[-1]))
            if defer:
                pending_casts.append(fn)
            else:
                fn()

    ps_all = nc.alloc_psum_tensor("ps_all", [128, 8, 512], F32)
    bank = [Tile(ps_all[:, i, :], S._reg(Buf("bank%d" % i, "ps", i, i + 1))) for i in range(8)]

    off = BASE
    ident = S.tile("ident", [128, 128], F32, off); off += 512
    ones = S.tile("ones", [128, 128], BF16, off); off += 256
    vecs = S.tile("vecs", [128, NV], F32, off); off += NV * 4
    mod = S.tile("mod", [128, 96, 2], F32, off); off += 768
    der = S.tile("der", [128, 2, 6, 16], F32, off); off += 768
    epsT = S.tile("epsT", [128, 1], F32, off); off += 32
    sc = S.tile("sc", [128, 16, 2], BF16, off); off += 64
    gq2 = S.tile("gq2", [128, 8], F32, off); off += 32
    CONST_END = BASE + 6 * KB
    assert off <= CONST_END
    cds = S.dsem("const")
    S.dma(SP, cds, ident.t[:, :], ident_d.t[:, :], reads=[ident_d], writes=[ident])
    S.dma(SP, cds, vecs.t[:, :], vecs_d.t[:, :], reads=[vecs_d], writes=[vecs])
    S.op(DVE, lambda: V.memset(ones.t[:, :], 1.0), writes=[ones])
    S.op(DVE, lambda: V.memset(epsT.t[:, :], EPS), writes=[epsT])
    for s in range(2):
        c0 = V_CP if s == 0 else V_CS
        S.op(ACT, lambda s=s, c0=c0: A.activation(out=sc.t[:, :, s], in_=vecs.t[:, c0:c0 + 16], func=AF.Silu),
             reads=[vecs], writes=[sc])

    P0 = CONST_END
    NWAD = 4
    wad = [S.tile("wad%d" % i, [128, 16, 512], BF16, P0 + i * 16 * KB) for i in range(NWAD)]
    wad_ds = [S.dsem("wad%d" % i) for i in range(NWAD)]
    psmod = bank[7]
    psmod_v = ps_all[:, 7, 0:192].rearrange("p (n s) -> p n s", s=2)
    mod_groups_done = [0]

    def mod_phase(groups):
        for g in groups:
            i = mod_groups_done[0] % NWAD
            mod_groups_done[0] += 1
            S.dma(POOL, wad_ds[i], wad[i].t[:, :, :],
                  wada_d.t[:, g * 512:(g + 1) * 512].rearrange("(kc p) n -> p kc n", p=128),
                  reads=[wada_d], writes=[wad[i]])
            for q in range(4):
                n = g * 4 + q
                for kc in range(16):
                    S.op(PE, lambda i=i, q=q, kc=kc, n=n: T.matmul(ps_all[:, 7, 2 * n:2 * n + 2], wad[i].t[:, kc, q * 128:(q + 1) * 128],
                                                                   sc.t[:, kc, :], start=(kc == 0), stop=(kc == 15)),
                         reads=[wad[i], sc] if kc == 0 else [], writes=[psmod] if kc == 0 else [], inc=(kc == 15))
                    S.npe -= 1
        for s in range(2):
            for g in groups:
                S.op(DVE, lambda s=s, g=g: V.tensor_tensor(out=mod.t[:, g * 4:(g + 1) * 4, s], in0=psmod_v[:, g * 4:(g + 1) * 4, s],
                                                           in1=vecs.t[:, V_BADA + g * 4:V_BADA + (g + 1) * 4], op=ALU.add),
                     reads=[psmod, vecs], writes=[mod])

    def derive(s, which):
        mv = lambda v: mod.t[:, v * 16:(v + 1) * 16, s]
        if which == 0:
            S.op(DVE, lambda: V.scalar_tensor_tensor(out=der.t[:, s, 0, :], in0=mv(1), scalar=1.0, in1=vecs.t[:, V_GATTN:V_GATTN + 16],
                                                     op0=ALU.add, op1=ALU.mult), reads=[mod, vecs], writes=[der])
            S.op(DVE, lambda: V.tensor_copy(out=der.t[:, s, 1, :], in_=mv(0)), reads=[mod], writes=[der])
        else:
            S.op(DVE, lambda: V.tensor_copy(out=der.t[:, s, 2, :], in_=mv(2)), reads=[mod], writes=[der])
            S.op(DVE, lambda: V.scalar_tensor_tensor(out=der.t[:, s, 3, :], in0=mv(4), scalar=1.0, in1=vecs.t[:, V_GFFN:V_GFFN + 16],
                                                     op0=ALU.add, op1=ALU.mult), reads=[mod, vecs], writes=[der])
            S.op(DVE, lambda: V.tensor_copy(out=der.t[:, s, 4, :], in_=mv(3)), reads=[mod], writes=[der])
            S.op(DVE, lambda: V.tensor_copy(out=der.t[:, s, 5, :], in_=mv(5)), reads=[mod], writes=[der])

    for k in ("WA", "WUK", "WUV"):
        cast_weight(k)
    mod_phase(list(range(0, 8)))
    for k in ("WKNA", "WVNA"):
        cast_weight(k)
    for k in ("WQ", "WUQ", "WO", "WGU", "WD"):
        cast_weight(k, defer=True)
    for s in range(2):
        derive(s, 0)
    gs1 = lambda s, kc: der.t[:, s, 0, kc:kc + 1]
    sh1 = lambda s, kc: der.t[:, s, 1, kc:kc + 1]
    gt1 = lambda s, kc: der.t[:, s, 2, kc:kc + 1]
    gs2 = lambda s, kc: der.t[:, s, 3, kc:kc + 1]
    sh2 = lambda s, kc: der.t[:, s, 4, kc:kc + 1]
    gt2 = lambda s, kc: der.t[:, s, 5, kc:kc + 1]
    if stage == 0:
        mod_phase(list(range(8, 24)))
        for s in range(2):
            derive(s, 1)
        dbg_dump("mod", mod, [128, 96, 2])
        dbg_dump("der", der, [128, 2, 6, 16])
        S.barrier()
        return nc, dbg_out

    def mm_group(bk, out_ap, pairs, reads):
        n = len(pairs)
        for i, (l, r) in enumerate(pairs):
            S.op(PE, lambda l=l, r=r, i=i: T.matmul(out_ap, l, r, start=(i == 0), stop=(i == n - 1)),
                 reads=reads if i == 0 else [], writes=[bk] if i == 0 else [], inc=(i == n - 1))

    class XRing:
        def __init__(self, base, nslots, tag):
            self.tiles = [S.tile("xt%s%d" % (tag, i), [128, D], F32, base + i * 8 * KB) for i in range(nslots)]
            self.ds = [S.dsem("xt%s%d" % (tag, i)) for i in range(nslots)]
            self.n = nslots
            self.k = 0

        def load(self, row0):
            i = self.k % self.n
            self.k += 1
            S.dma(SP, self.ds[i], self.tiles[i].t[:, :], xall.t[row0:row0 + 128, :], reads=[xall], writes=[self.tiles[i]])
            return self.tiles[i]

    def token_stats_scale(xt, junk, ssb, k):
        if isinstance(ssb, list):
            sb_ = ssb[k % len(ssb)]
            col = sb_.t[:, 0:1]
        else:
            sb_ = ssb
            col = ssb.t[:, k:k + 1]
        jk = junk[k % len(junk)] if isinstance(junk, list) else junk
        S.op(ACT, lambda: A.activation(out=jk.t[:, :], in_=xt.t[:, :], func=AF.Square, accum_out=col),
             reads=[xt], writes=[jk, sb_])
        S.op(ACT, lambda: A.activation(out=col, in_=col, func=AF.Sqrt, bias=epsT.t[:, 0:1], scale=1.0 / D),
             reads=[sb_, epsT], writes=[sb_])
        S.op(DVE, lambda: V.reciprocal(out=col, in_=col), reads=[sb_], writes=[sb_])
        S.op(DVE, lambda: V.tensor_scalar(out=xt.t[:, :], in0=xt.t[:, :], scalar1=col, scalar2=None, op0=ALU.mult),
             reads=[xt, sb_], writes=[xt])

    def transpose_to_h(xts, h, s, pbanks, pk):
        for kc in range(16):
            bk = pbanks[pk[0] % len(pbanks)]
            pk[0] += 1
            for t in range(4):
                S.op(PE, lambda t=t, kc=kc, bk=bk: T.transpose(bk.t[:, t * 128:(t + 1) * 128], xts[t].t[:, kc * 128:(kc + 1) * 128], ident.t[:, :]),
                     reads=[xts[t], ident], writes=[bk] if t == 0 else [], inc=(t == 3))
            if kc % 2 == 0:
                S.op(ACT, lambda kc=kc, bk=bk: A.activation(out=h.t[:, kc, :], in_=bk.t[:, :], func=AF.Identity, scale=gs1(s, kc), bias=sh1(s, kc)),
                     reads=[bk, der], writes=[h])
            else:
                S.op(DVE, lambda kc=kc, bk=bk: V.tensor_scalar(out=h.t[:, kc, :], in0=bk.t[:, :], scalar1=gs1(s, kc), scalar2=sh1(s, kc),
                                                                op0=ALU.mult, op1=ALU.add),
                     reads=[bk, der], writes=[h])

    def evac_copy(i, out_ap, in_ap, reads, writes, scale=None):
        if i % 2 == 0:
            if scale is None:
                S.op(ACT, lambda: A.copy(out=out_ap, in_=in_ap), reads=reads, writes=writes)
            else:
                S.op(ACT, lambda: A.mul(out=out_ap, in_=in_ap, mul=scale) if False else A.activation(out=out_ap, in_=in_ap, func=AF.Copy, scale=scale),
                     reads=reads, writes=writes)
        else:
            if scale is None:
                S.op(DVE, lambda: V.tensor_copy(out=out_ap, in_=in_ap), reads=reads, writes=writes)
            else:
                S.op(DVE, lambda: V.tensor_scalar(out=out_ap, in0=in_ap, scalar1=scale, scalar2=None, op0=ALU.mult), reads=reads, writes=writes)

    def rstd_from_bank(bk, dim, rt):
        S.op(ACT, lambda: A.activation(out=rt.t[:, :], in_=bk.t[:, :], func=AF.Sqrt, bias=epsT.t[:, 0:1], scale=1.0 / dim),
             reads=[bk, epsT], writes=[rt])
        S.op(DVE, lambda: V.reciprocal(out=rt.t[:, :], in_=rt.t[:, :]), reads=[rt], writes=[rt])

    S.mark('p1a')
    P1 = CONST_END
    o = P1
    xr = XRing(o, 8, "a"); o += 64 * KB
    hA = [S.tile("hA%d" % i, [128, 16, 512], BF16, o + i * 16 * KB) for i in range(2)]; o += 32 * KB
    wa = S.tile("wa", [128, 6, 16, 128], BF16, o); o += 24 * KB
    wuk = S.tile("wuk", [128, 8, 4, 128], BF16, o); o += 8 * KB
    wuv = S.tile("wuv", [128, 4, 1024], BF16, o); o += 8 * KB
    kvc32 = S.tile("kvc32", [128, 4, 512], F32, o); o += 8 * KB
    sqkv = S.tile("sqkv", [128, 4, 512], BF16, o); o += 4 * KB
    kvn2 = [S.tile("kvn%d" % i, [128, 4, 512], BF16, o + i * 4 * KB) for i in range(2)]; o += 8 * KB
    kT_out = S.tile("kT_out", [128, 8, 512], BF16, o); o += 8 * KB
    v_out = S.tile("v_out", [128, 8, 4, 128], BF16, o); o += 8 * KB
    t1 = S.tile("t1", [128, 512], F32, o); o += 2 * KB
    t2 = S.tile("t2", [128, 512], F32, o); o += 2 * KB
    krd = S.tile("krd", [128, 512], BF16, o); o += 1 * KB
    rk32 = S.tile("rk32", [128, 512], F32, o); o += 2 * KB
    tabA = [S.tile("tabA%d" % i, [128, 2, 512], F32, o + i * 4 * KB) for i in range(4)]; o += 16 * KB
    junk = [S.tile("junk%d" % i, [128, D], BF16, o + i * 4 * KB) for i in range(1)]; o += 4 * KB
    ssb = [S.tile("ssb%d" % i, [128, 1], F32, o + i * 32) for i in range(8)]; o += 256
    assert o <= BASE + 207 * KB, o
    tab_ds = [S.dsem("tabA%d" % i) for i in range(4)]
    wds = S.dsem("w1a")
    S.dma(SP, wds, wa.t[:, :, :, :], wbf["WA"].t.rearrange("m p k n -> p m k n"), reads=[wbf["WA"]], writes=[wa])
    S.dma(SP, wds, wuk.t[:, :, :, :], wbf["WUK"].t.rearrange("m p k n -> p m k n"), reads=[wbf["WUK"]], writes=[wuk])
    S.dma(SP, wds, wuv.t[:, :, :], wbf["WUV"].t, reads=[wbf["WUV"]], writes=[wuv])
    st_k, st_v, st_pe = S.dsem("st_k"), S.dsem("st_v"), S.dsem("st_pe")

    blocks = [(0, j) for j in range(NBP)] + [(1, j) for j in range(NBS)]
    if stage == 1:
        blocks = blocks[:3]
    pbanks = [bank[0], bank[1]]
    mbanks = [bank[2], bank[3], bank[4], bank[5], bank[6]]
    pk = [0]
    mk = [0]

    def nb():
        b = mbanks[mk[0] % len(mbanks)]
        mk[0] += 1
        return b

    xtiles = {}

    def p1_load(bi):
        s, j = blocks[bi]
        row0 = (0 if s == 0 else SP_) + j * 512
        xtiles[bi] = [xr.load(row0 + t * 128) for t in range(4)]
        S.dma(SP, tab_ds[bi % 4], tabA[bi % 4].t[:, :, :], tabs_d.t[(0 if s == 0 else NBP) + j], reads=[tabs_d], writes=[tabA[bi % 4]])

    def p1_stats(bi):
        for t in range(4):
            token_stats_scale(xtiles[bi][t], junk, ssb, (bi * 4 + t) % 8)

    def p1_trans(bi):
        s, j = blocks[bi]
        transpose_to_h(xtiles[bi], hA[bi % 2], s, pbanks, pk)

    def p1_part1(bi):
        s, j = blocks[bi]
        h = hA[bi % 2]
        tab = tabA[bi % 4]
        for m in range(4):
            bk = nb()
            mm_group(bk, bk.t[:, :], [(wa.t[:, m, kc, :], h.t[:, kc, :]) for kc in range(16)], [wa, h])
            S.op(DVE, lambda m=m, bk=bk: V.tensor_copy(out=kvc32.t[:, m, :], in_=bk.t[:, :]), reads=[bk], writes=[kvc32])
            S.op(ACT, lambda m=m: A.activation(out=sqkv.t[:, m, :], in_=kvc32.t[:, m, :], func=AF.Square), reads=[kvc32], writes=[sqkv])
        bkA = nb()
        mm_group(bkA, bkA.t[:, :], [(wa.t[:, 4, kc, :], h.t[:, kc, :]) for kc in range(16)], [wa, h])
        bkB = nb()
        mm_group(bkB, bkB.t[:, :], [(wa.t[:, 5, kc, :], h.t[:, kc, :]) for kc in range(16)], [wa, h])
        S.op(DVE, lambda: V.tensor_tensor(out=t1.t[:, :], in0=bkA.t[:, :], in1=tab.t[:, 0, :], op=ALU.mult), reads=[bkA, tab], writes=[t1])
        S.op(DVE, lambda: V.tensor_tensor(out=t2.t[:, :], in0=bkB.t[:, :], in1=tab.t[:, 1, :], op=ALU.mult), reads=[bkB, tab], writes=[t2])
        S.op(POOL, lambda: G.tensor_tensor(out=krd.t[:, :], in0=t1.t[:, :], in1=t2.t[:, :], op=ALU.add), reads=[t1, t2], writes=[krd])
        S.dma(POOL, st_pe, KPE[s].t[:, j * 512:(j + 1) * 512], krd.t[:, :], reads=[krd], writes=[KPE[s]])

    def p1_norm(bi):
        kvn = kvn2[bi % 2]
        bk = nb()
        mm_group(bk, bk.t[:, :], [(ones.t[:, :], sqkv.t[:, m, :]) for m in range(4)], [ones, sqkv])
        rstd_from_bank(bk, 512, rk32)
        for m in range(4):
            S.op(DVE, lambda m=m: V.scalar_tensor_tensor(out=kvn.t[:, m, :], in0=kvc32.t[:, m, :], scalar=vecs.t[:, V_GKV + m:V_GKV + m + 1],
                                                         in1=rk32.t[:, :], op0=ALU.mult, op1=ALU.mult),
                 reads=[kvc32, rk32, vecs], writes=[kvn])

    def p1_knope(bi):
        s, j = blocks[bi]
        kvn = kvn2[bi % 2]
        for hh in range(8):
            bk = nb()
            mm_group(bk, bk.t[:, :], [(wuk.t[:, hh, m, :], kvn.t[:, m, :]) for m in range(4)], [wuk, kvn])
            evac_copy(hh, kT_out.t[:, hh, :], bk.t[:, :], [bk], [kT_out])
        S.dma(POOL, st_k, KT[s].t[:, :, j * 512:(j + 1) * 512].rearrange("h d t -> d h t"), kT_out.t[:, :, :], reads=[kT_out], writes=[KT[s]])

    def p1_v(bi):
        s, j = blocks[bi]
        kvn = kvn2[bi % 2]
        for t in range(4):
            for c in range(2):
                bk = nb()
                mm_group(bk, bk.t[:, :], [(kvn.t[:, m, t * 128:(t + 1) * 128], wuv.t[:, m, c * 512:(c + 1) * 512]) for m in range(4)], [wuv, kvn])
                evac_copy(t * 2 + c, v_out.t[:, 4 * c:4 * c + 4, t, :], bk.t[:, :].rearrange("p (h d) -> p h d", d=128), [bk], [v_out])
        S.dma(POOL, st_v, VV[s].t[:, :, 4 * j:4 * j + 4, :].rearrange("h p c d -> p h c d"), v_out.t[:, :, :, :], reads=[v_out], writes=[VV[s]])

    nblk = len(blocks)
    p1_load(0)
    p1_load(1)
    p1_stats(0)
    p1_stats(1)
    p1_trans(0)
    p1_load(2)
    for bi in range(nblk):
        if bi + 1 < nblk:
            p1_trans(bi + 1)
        if bi + 2 < nblk:
            p1_stats(bi + 2)
        if bi + 3 < nblk:
            p1_load(bi + 3)
        p1_part1(bi)
        if bi > 0:
            p1_knope(bi - 1)
        p1_norm(bi)
        if bi > 0:
            p1_v(bi - 1)
        if pending_casts:
            pending_casts.pop(0)()
    p1_knope(nblk - 1)
    p1_v(nblk - 1)
    while pending_casts:
        pending_casts.pop(0)()
    if stage == 1:
        S.barrier()
        dbg_dump("h0", hA[0], [128, 16, 512], BF16)
        dbg_dump("KT0", (KT[0].t[:, :, 0:1536], KT[0]), [8, 128, 1536], BF16)
        dbg_dump("KPE0", (KPE[0].t[:, 0:1536], KPE[0]), [128, 1536], BF16)
        dbg_dump("VV0", (VV[0].t[:, :, 0:12, :], VV[0]), [8, 128, 12, 128], BF16)
        S.barrier()
        return nc, dbg_out

    S.mark('p1b')
    o = P1 + 96 * KB
    wkna = S.tile("wkna", [128, 8, 16, 128], BF16, o); o += 32 * KB
    wvna = S.tile("wvna", [128, 16, 1024], BF16, o); o += 32 * KB
    kna_out = S.tile("kna_out", [128, 8, 512], BF16, o); o += 8 * KB
    vna_out = S.tile("vna_out", [128, 4, 1024], BF16, o); o += 8 * KB
    wds2 = S.dsem("w1b")
    S.dma(SP, wds2, wkna.t[:, :, :, :], wbf["WKNA"].t.rearrange("m p k n -> p m k n"), reads=[wbf["WKNA"]], writes=[wkna])
    S.dma(SP, wds2, wvna.t[:, :, :], wbf["WVNA"].t, reads=[wbf["WVNA"]], writes=[wvna])
    st_kn, st_vn = S.dsem("st_kn"), S.dsem("st_vn")
    nblocks = [(0, j) for j in range(NAP // 512)] + [(1, j) for j in range(NAS // 512)]
    if stage == 2:
        nblocks = nblocks[:5]
    xt2 = {}

    def p1b_load(bi):
        s, j = nblocks[bi]
        row0 = (0 if s == 0 else SP_) + j * 512
        xt2[bi] = [xr.load(row0 + t * 128) for t in range(4)]

    def p1b_stats(bi):
        for t in range(4):
            token_stats_scale(xt2[bi][t], junk, ssb, (bi * 4 + t) % 8)

    def p1b_trans(bi):
        s, j = nblocks[bi]
        transpose_to_h(xt2[bi], hA[bi % 2], s, pbanks, pk)

    def p1b_main(bi):
        s, j = nblocks[bi]
        h = hA[bi % 2]
        for hh in range(8):
            bk = nb()
            mm_group(bk, bk.t[:, :], [(wkna.t[:, hh, kc, :], h.t[:, kc, :]) for kc in range(16)], [wkna, h])
            evac_copy(hh, kna_out.t[:, hh, :], bk.t[:, :], [bk], [kna_out])
        for t in range(4):
            for c in range(2):
                bk = nb()
                mm_group(bk, bk.t[:, :], [(h.t[:, kc, t * 128:(t + 1) * 128], wvna.t[:, kc, c * 512:(c + 1) * 512]) for kc in range(16)], [wvna, h])
                evac_copy(t * 2 + c, vna_out.t[:, t, c * 512:(c + 1) * 512], bk.t[:, :], [bk], [vna_out])
        S.dma(POOL, st_kn, KNA[s].t[:, :, j * 512:(j + 1) * 512].rearrange("h d t -> d h t"), kna_out.t[:, :, :], reads=[kna_out], writes=[KNA[s]])
        S.dma(POOL, st_vn, VNA[s].t[j * 512:(j + 1) * 512, :].rearrange("(t p) n -> p t n", p=128), vna_out.t[:, :, :], reads=[vna_out], writes=[VNA[s]])

    nnb = len(nblocks)
    p1b_load(0)
    p1b_load(1)
    p1b_stats(0)
    p1b_stats(1)
    p1b_trans(0)
    p1b_load(2)
    for bi in range(nnb):
        if bi + 1 < nnb:
            p1b_trans(bi + 1)
        if bi + 2 < nnb:
            p1b_stats(bi + 2)
        if bi + 3 < nnb:
            p1b_load(bi + 3)
        p1b_main(bi)

    S.mark('modB')
    mod_phase(list(range(8, 24)))
    for s in range(2):
        derive(s, 1)
    S.barrier()

    Q = CONST_END
    xT = S.tile("xT", [128, 16, 512], F32, Q)
    h2 = S.tile("h2", [128, 16, 512], BF16, Q + 32 * KB)
    HS = Q + 32 * KB
    sq = S.tile("sq", [128, 16, 512], BF16, Q + 48 * KB)
    rA = S.tile("rA", [128, 512], F32, Q + 64 * KB)
    rB = S.tile("rB", [128, 512], F32, Q + 66 * KB)
    tab2 = S.tile("tab2", [128, 2, 512], F32, Q + 68 * KB)
    tmp = [S.tile("tmp%d" % i, [128, 512], F32, Q + (72 + 2 * i) * KB) for i in range(2)]
    junk2 = S.tile("junk2", [128, D], BF16, Q + 72 * KB)
    ssb2 = [S.tile("ssb2_%d" % i, [128, 1], F32, BASE + 6 * KB - 256 + i * 32) for i in range(4)]
    NWR = 3
    WR = Q + 76 * KB
    wr_pair = [S.tile("wrp%d" % i, [128, 2, 16, 128], BF16, WR + i * 8 * KB) for i in range(NWR)]
    wr_uq = [S.tile("wru%d" % i, [128, 8, 4, 128], BF16, WR + i * 8 * KB) for i in range(NWR)]
    wr_ds = [S.dsem("wr%d" % i) for i in range(NWR)]
    wrk = [0]
    AR = Q + 100 * KB
    assert AR + 101 * KB <= BASE + 207 * KB
    xr2 = XRing(AR, 4, "b")
    Kblk = S.tile("Kblk", [128, 8, 1280], BF16, AR)
    Vblk = S.tile("Vblk", [128, 10, 1024], BF16, AR + 20 * KB)
    nbint = S.tile("nbint", [128, 8, 896], BF16, AR + 56 * KB)
    nbint_ds = S.dsem("nbint")
    kv_ds = [S.dsem("kv%d" % i) for i in range(NKV)]
    PER = SCK // 128
    SLOT = 6 * KB + 64
    assert NKV * SLOT <= 32 * KB
    Ksl = [S.tile("Ksl%d" % i, [128, SCK], BF16, AR + i * SLOT) for i in range(NKV)]
    Pesl = [S.tile("Pesl%d" % i, [128, SCK], BF16, AR + i * SLOT + 2 * KB) for i in range(NKV)]
    Vsl = [S.tile("Vsl%d" % i, [128, PER, 129], BF16, AR + i * SLOT + 4 * KB) for i in range(NKV)]
    merged = S.tile("merged", [128, 16, 512], BF16, AR)
    aT = S.tile("aT", [128, NFF, 512], BF16, AR)
    qc32 = S.tile("qc32", [128, 4, 512], F32, Q + 52 * KB)
    qn = S.tile("qn", [128, 4, 512], BF16, Q + 60 * KB)
    ona32 = S.tile("ona32", [128, 8, 512], F32, AR + 40 * KB)
    omla32 = S.tile("omla32", [128, 8, 512], F32, AR + 56 * KB)
    qna = S.tile("qna", [128, 8, 512], BF16, AR + 72 * KB)
    qnope = S.tile("qnope", [128, 8, 512], BF16, AR + 80 * KB)
    qrope = S.tile("qrope", [128, 8, 512], BF16, AR + 88 * KB)
    NWD = 3
    wd = [S.tile("wd%d" % i, [128, NFF, 128], BF16, AR + 44 * KB + i * 11 * KB) for i in range(NWD)]
    wd_ds = [S.dsem("wd%d" % i) for i in range(NWD)]
    ytok = [S.tile("ytok%d" % i, [128, D], F32, AR + i * 8 * KB) for i in range(2)]
    y_ds = [S.dsem("y%d" % i) for i in range(2)]
    nbias = [S.tile("nbias%d" % i, [128, 896], F32, HS + i * 3584) for i in range(2)]
    nb_ds = [S.dsem("nbias%d" % i) for i in range(2)]
    sbna = S.tile("sbna", [128, 896], F32, HS + 7168)
    pTna = [S.tile("pTna%d" % i, [128, 896], BF16, HS + 10752 + i * 1792) for i in range(2)]
    rlna = S.tile("rlna", [128, 128], F32, HS + 14336)
    NPT = 6
    pT = [S.tile("pT%d" % i, [128, 512], BF16, HS + i * KB) for i in range(NPT)]
    otok = S.tile("otok", [128, 4, 128], F32, HS + 6 * KB)
    rcol = S.tile("rcol", [128, 4], F32, HS + 8 * KB)
    tab2_ds, kw_ds, vw_ds = S.dsem("tab2"), S.dsem("kwin"), S.dsem("vwin")
    QS_NA = 128.0 ** -0.5
    QS_MLA = 192.0 ** -0.5

    def wr_next():
        i = wrk[0] % NWR
        wrk[0] += 1
        return i

    def ones_rstd(chunks, dim, rt):
        bk = nb()
        mm_group(bk, bk.t[:, :], [(ones.t[:, :], sq.t[:, c, :]) for c in chunks], [ones, sq])
        rstd_from_bank(bk, dim, rt)

    first_block = [True]
    own_blocks = [(0, bb) for bb in range(OWNP // 512)] + [(1, bb) for bb in range(OWNS // 512)]
    if stage == 3:
        own_blocks = own_blocks[:1]
    for (s, bb) in own_blocks:
        jrot = 1 + bb
        L = SP_ if s == 0 else SS_
        row0 = (0 if s == 0 else SP_) + jrot * 512
        nt_seg = (OWNP if s == 0 else OWNS) // 128
        S.mark('A')
        S.dma(SP, tab2_ds, tab2.t[:, :, :], tabs_d.t[(0 if s == 0 else NBP) + jrot], reads=[tabs_d], writes=[tab2])
        xts = [xr2.load(row0 + t * 128) for t in range(4)]
        for kc in range(16):
            bk = pbanks[pk[0] % 2]
            pk[0] += 1
            for t in range(4):
                S.op(PE, lambda t=t, kc=kc, bk=bk: T.transpose(bk.t[:, t * 128:(t + 1) * 128], xts[t].t[:, kc * 128:(kc + 1) * 128], ident.t[:, :]),
                     reads=[xts[t], ident], writes=[bk] if t == 0 else [], inc=(t == 3))
            evac_copy(kc, xT.t[:, kc, :], bk.t[:, :], [bk], [xT])
        for t in range(4):
            token_stats_scale(xts[t], junk2, ssb2, t)
        transpose_to_h(xts, h2, s, pbanks, pk)
        kb0 = 128 + 128 * (bb * 4)
        S.dma(SP, kw_ds, Kblk.t[:, :, :], KNA[s].t[:, :, kb0:kb0 + 1280].rearrange("h d t -> d h t"), reads=[KNA[s]], writes=[Kblk])
        S.dma(SP, vw_ds, Vblk.t[:, :, :], VNA[s].t[kb0:kb0 + 1280, :].rearrange("(c p) n -> p c n", p=128), reads=[VNA[s]], writes=[Vblk])
        S.dma(POOL, nbint_ds, nbint.t[:, :, :], nab_d.t[s, 0].rearrange("h p k -> p h k"), reads=[nab_d], writes=[nbint])
        for m in range(12):
            if m % 2 == 0:
                wi = wr_next()
                S.dma(SP, wr_ds[wi], wr_pair[wi].t[:, :, :, :], wbf["WQ"].t[m:m + 2].rearrange("m p k n -> p m k n"), reads=[wbf["WQ"]], writes=[wr_pair[wi]])
            w = wr_pair[wi]
            bk = nb()
            mm_group(bk, bk.t[:, :], [(w.t[:, m % 2, kc, :], h2.t[:, kc, :]) for kc in range(16)], [w, h2])
            if m < 8:
                evac_copy(m, qna.t[:, m, :], bk.t[:, :], [bk], [qna], scale=QS_NA)
            else:
                S.op(DVE, lambda m=m, bk=bk: V.tensor_copy(out=qc32.t[:, m - 8, :], in_=bk.t[:, :]), reads=[bk], writes=[qc32])
                S.op(ACT, lambda m=m: A.activation(out=sq.t[:, m - 8, :], in_=qc32.t[:, m - 8, :], func=AF.Square), reads=[qc32], writes=[sq])
        ones_rstd(range(4), 512, rA)
        for m in range(4):
            S.op(DVE, lambda m=m: V.scalar_tensor_tensor(out=qn.t[:, m, :], in0=qc32.t[:, m, :], scalar=vecs.t[:, V_GQ + m:V_GQ + m + 1],
                                                         in1=rA.t[:, :], op0=ALU.mult, op1=ALU.mult), reads=[qc32, rA, vecs], writes=[qn])
        wu = []
        for i in range(2):
            wi = wr_next()
            S.dma(SP, wr_ds[wi], wr_uq[wi].t[:, :, :, :], wbf["WUQ"].t[8 * i:8 * i + 8].rearrange("m p k n -> p m k n"), reads=[wbf["WUQ"]], writes=[wr_uq[wi]])
            wu.append(wr_uq[wi])
        for hh in range(8):
            bk = nb()
            mm_group(bk, bk.t[:, :], [(wu[0].t[:, hh, m, :], qn.t[:, m, :]) for m in range(4)], [wu[0], qn])
            evac_copy(hh, qnope.t[:, hh, :], bk.t[:, :], [bk], [qnope], scale=QS_MLA)
        if first_block[0]:
            first_block[0] = False
            S.op(POOL, lambda: G.memset(qrope.t[:, :, :], 0.0), writes=[qrope])
        for hh in range(8):
            pr, e = hh // 2, hh % 2
            bkA = nb()
            mm_group(bkA, bkA.t[0:64, :], [(wu[1].t[:, pr, m, 64 * e:64 * e + 64], qn.t[:, m, :]) for m in range(4)], [wu[1], qn])
            bkB = nb()
            mm_group(bkB, bkB.t[0:64, :], [(wu[1].t[:, 4 + pr, m, 64 * e:64 * e + 64], qn.t[:, m, :]) for m in range(4)], [wu[1], qn])
            S.op(DVE, lambda bkA=bkA: V.scalar_tensor_tensor(out=tmp[0].t[0:64, :], in0=bkA.t[0:64, :], scalar=QS_MLA, in1=tab2.t[0:64, 0, :], op0=ALU.mult, op1=ALU.mult),
                 reads=[bkA, tab2], writes=[tmp[0]])
            S.op(DVE, lambda bkB=bkB: V.scalar_tensor_tensor(out=tmp[1].t[0:64, :], in0=bkB.t[0:64, :], scalar=QS_MLA, in1=tab2.t[0:64, 1, :], op0=ALU.mult, op1=ALU.mult),
                 reads=[bkB, tab2], writes=[tmp[1]])
            S.op(POOL, lambda hh=hh: G.tensor_tensor(out=qrope.t[0:64, hh, :], in0=tmp[0].t[0:64, :], in1=tmp[1].t[0:64, :], op=ALU.add),
                 reads=[tmp[0], tmp[1]], writes=[qrope])
        if KPART >= 2:
            S.mark('B')
            for qt in range(4):
                ti = bb * 4 + qt
                slot = _slot_of(ti, nt_seg)
                for hh in range(8):
                    i2 = hh % 2
                    if slot == 0:
                        bias_t, bias_ap = nbint, nbint.t[:, hh, :]
                    else:
                        S.dma(SP, nb_ds[i2], nbias[i2].t[:, :], nab_d.t[s, slot, hh], reads=[nab_d], writes=[nbias[i2]])
                        bias_t, bias_ap = nbias[i2], nbias[i2].t[:, :]
                    bS = [bank[2 * i2], bank[2 * i2 + 1]]
                    ps2 = ps_all[:, 2 * i2:2 * i2 + 2, :].rearrange("p a b -> p (a b)")
                    for c in range(7):
                        S.op(PE, lambda c=c, hh=hh, ps2=ps2: T.matmul(ps2[:, c * 128:(c + 1) * 128], Kblk.t[:, hh, (qt + c) * 128:(qt + c + 1) * 128],
                                                                      qna.t[:, hh, qt * 128:(qt + 1) * 128], start=True, stop=True),
                             reads=[Kblk, qna] if c == 0 else [], writes=bS if c == 0 else [], inc=(c == 6))
                    S.op(DVE, lambda ps2=ps2, bias_ap=bias_ap: V.tensor_tensor(out=sbna.t[:, :], in0=ps2[:, 0:896], in1=bias_ap, op=ALU.add),
                         reads=bS + [bias_t], writes=[sbna])
                    S.op(ACT, lambda i2=i2: A.activation(out=pTna[i2].t[:, :], in_=sbna.t[:, :], func=AF.Exp), reads=[sbna], writes=[pTna[i2]])
                    bO = bank[4 + i2]
                    bL = bank[6 + i2]
                    mm_group(bO, bO.t[:, 0:128], [(Vblk.t[:, qt + c, hh * 128:(hh + 1) * 128], pTna[i2].t[:, c * 128:(c + 1) * 128]) for c in range(7)], [Vblk, pTna[i2]])
                    mm_group(bL, bL.t[:, 0:128], [(ones.t[:, :], pTna[i2].t[:, c * 128:(c + 1) * 128]) for c in range(7)], [ones, pTna[i2]])
                    S.op(DVE, lambda bL=bL: V.reciprocal(out=rlna.t[:, :], in_=bL.t[:, 0:128]), reads=[bL], writes=[rlna])
                    S.op(DVE, lambda bO=bO, hh=hh, qt=qt: V.tensor_tensor(out=ona32.t[:, hh, qt * 128:(qt + 1) * 128], in0=bO.t[:, 0:128], in1=rlna.t[:, :], op=ALU.mult),
                         reads=[bO, rlna], writes=[ona32])
            for hh in range(8):
                S.op(ACT, lambda hh=hh: A.activation(out=sq.t[:, hh, :], in_=ona32.t[:, hh, :], func=AF.Square), reads=[ona32], writes=[sq])
        if KPART >= 3:
            S.mark('C')
            nch = L // 128
            kvk = [0]
            for i in range(NKV):
                S.op(DVE, lambda i=i: V.memset(Vsl[i].t[:, :, 128:129], 1.0), writes=[Vsl[i]])
            obanks = [bank[3], bank[4], bank[5], bank[6]]
            epi = [None]
            for hh in range(8):
                pend = []

                def emit_pv(p, hh=hh):
                    g, si, c = p
                    for t in range(4):
                        S.op(PE, lambda t=t: T.matmul(obanks[t].t[:, 0:129], pT[g % NPT].t[:, t * 128:(t + 1) * 128], Vsl[si].t[:, c, :],
                                                      start=(g == 0), stop=(g == nch - 1)),
                             reads=[Vsl[si], pT[g % NPT]] if t == 0 else [], writes=obanks if (t == 0 and g in (0, nch - 1)) else [], inc=(t == 3))

                for g in range(nch):
                    c = g % PER
                    if c == 0:
                        si = kvk[0] % NKV
                        kvk[0] += 1
                        sc_ = g // PER
                        S.dma(SP, kv_ds[si], Ksl[si].t[:, :], KT[s].t[hh, :, sc_ * SCK:(sc_ + 1) * SCK], reads=[KT[s]], writes=[Ksl[si]])
                        S.dma(SP, kv_ds[si], Pesl[si].t[:, :], KPE[s].t[:, sc_ * SCK:(sc_ + 1) * SCK], reads=[KPE[s]], writes=[Pesl[si]])
                        S.dma(SP, kv_ds[si], Vsl[si].t[:, :, 0:128], VV[s].t[hh, :, sc_ * PER:(sc_ + 1) * PER, :], reads=[VV[s]], writes=[Vsl[si]])
                    bS = bank[g % 3]
                    S.op(PE, lambda si=si, c=c, bS=bS: T.matmul(bS.t[:, :], Ksl[si].t[:, c * 128:(c + 1) * 128], qnope.t[:, hh, :], start=True, stop=False),
                         reads=[Ksl[si], qnope], writes=[bS], inc=False)
                    S.op(PE, lambda si=si, c=c, bS=bS: T.matmul(bS.t[:, :], Pesl[si].t[:, c * 128:(c + 1) * 128],
                                                                qrope.t[:, hh, :], start=False, stop=True),
                         reads=[Pesl[si], qrope], writes=[], inc=True)
                    if len(pend) >= 2:
                        emit_pv(pend.pop(0))
                    S.op(ACT, lambda g=g, bS=bS: A.activation(out=pT[g % NPT].t[:, :], in_=bS.t[:, :], func=AF.Exp), reads=[bS], writes=[pT[g % NPT]])
                    pend.append((g, si, c))
                    if g == 4 and epi[0] is not None:
                        epi[0]()
                        epi[0] = None
                for p in pend:
                    emit_pv(p)
                for t in range(4):
                    S.op(DVE, lambda t=t: V.reciprocal(out=rcol.t[:, t:t + 1], in_=obanks[t].t[:, 128:129]), reads=[obanks[t]], writes=[rcol])
                    S.op(DVE, lambda t=t: V.tensor_scalar(out=otok.t[:, t, :], in0=obanks[t].t[:, 0:128], scalar1=rcol.t[:, t:t + 1], scalar2=None, op0=ALU.mult),
                         reads=[obanks[t], rcol], writes=[otok])

                def epilogue(hh=hh):
                    for t in range(4):
                        S.op(PE, lambda t=t: T.transpose(bank[7].t[:, t * 128:(t + 1) * 128], otok.t[:, t, :], ident.t[:, :]),
                             reads=[otok, ident] if t == 0 else [], writes=[bank[7]] if t == 0 else [], inc=(t == 3))
                    S.op(DVE, lambda: V.tensor_copy(out=omla32.t[:, hh, :], in_=bank[7].t[:, :]), reads=[bank[7]], writes=[omla32])
                    S.op(POOL, lambda: G.tensor_tensor(out=sq.t[:, 8 + hh, :], in0=omla32.t[:, hh, :], in1=omla32.t[:, hh, :], op=ALU.mult),
                         reads=[omla32], writes=[sq])

                epi[0] = epilogue
            epi[0]()
        if KPART >= 4:
            S.mark('D')
            ones_rstd(range(0, 8), 1024, rA)
            ones_rstd(range(8, 16), 1024, rB)
            for hh in range(8):
                S.op(DVE, lambda hh=hh: V.scalar_tensor_tensor(out=merged.t[:, hh, :], in0=ona32.t[:, hh, :], scalar=vecs.t[:, V_GNA + hh:V_GNA + hh + 1],
                                                               in1=rA.t[:, :], op0=ALU.mult, op1=ALU.mult), reads=[ona32, rA, vecs], writes=[merged])
                S.op(DVE, lambda hh=hh: V.scalar_tensor_tensor(out=merged.t[:, 8 + hh, :], in0=omla32.t[:, hh, :], scalar=vecs.t[:, V_GMLA + hh:V_GMLA + hh + 1],
                                                               in1=rB.t[:, :], op0=ALU.mult, op1=ALU.mult), reads=[omla32, rB, vecs], writes=[merged])
            for n in range(16):
                if n % 2 == 0:
                    wi = wr_next()
                    S.dma(SP, wr_ds[wi], wr_pair[wi].t[:, :, :, :], wbf["WO"].t[n:n + 2].rearrange("m p k n -> p m k n"), reads=[wbf["WO"]], writes=[wr_pair[wi]])
                w = wr_pair[wi]
                bk = nb()
                mm_group(bk, bk.t[:, :], [(w.t[:, n % 2, kc, :], merged.t[:, kc, :]) for kc in range(16)], [w, merged])
                S.op(DVE, lambda n=n, bk=bk: V.scalar_tensor_tensor(out=xT.t[:, n, :], in0=bk.t[:, :], scalar=gt1(s, n), in1=xT.t[:, n, :], op0=ALU.mult, op1=ALU.add),
                     reads=[bk, xT, der], writes=[xT])
                S.op(ACT, lambda n=n: A.activation(out=sq.t[:, n, :], in_=xT.t[:, n, :], func=AF.Square), reads=[xT], writes=[sq])
            ones_rstd(range(16), 2048, rA)
            for kc in range(16):
                tt = tmp[kc % 2]
                S.op(DVE, lambda kc=kc, tt=tt: V.scalar_tensor_tensor(out=tt.t[:, :], in0=xT.t[:, kc, :], scalar=gs2(s, kc), in1=rA.t[:, :], op0=ALU.mult, op1=ALU.mult),
                     reads=[xT, rA, der], writes=[tt])
                S.op(ACT, lambda kc=kc, tt=tt: A.activation(out=h2.t[:, kc, :], in_=tt.t[:, :], func=AF.Identity, bias=sh2(s, kc), scale=1.0),
                     reads=[tt, der], writes=[h2])
        if KPART >= 5:
            S.mark('E')
            for c in range(NFF):
                wi = wr_next()
                w = wr_pair[wi]
                S.dma(SP, wr_ds[wi], w.t[:, :, :, :], wbf["WGU"].t[c], reads=[wbf["WGU"]], writes=[w])
                bG = nb()
                mm_group(bG, bG.t[:, :], [(w.t[:, 0, kc, :], h2.t[:, kc, :]) for kc in range(16)], [w, h2])
                bU = nb()
                mm_group(bU, bU.t[:, :], [(w.t[:, 1, kc, :], h2.t[:, kc, :]) for kc in range(16)], [w, h2])
                tt = tmp[c % 2]
                S.op(ACT, lambda bG=bG, tt=tt: A.activation(out=tt.t[:, :], in_=bG.t[:, :], func=AF.Silu), reads=[bG], writes=[tt])
                S.op(DVE, lambda c=c, bU=bU, tt=tt: V.tensor_tensor(out=aT.t[:, c, :], in0=bU.t[:, :], in1=tt.t[:, :], op=ALU.mult), reads=[bU, tt], writes=[aT])
            for n in range(16):
                w = wd[n % NWD]
                S.dma(SP, wd_ds[n % NWD], w.t[:, :, :], wbf["WD"].t[n], reads=[wbf["WD"]], writes=[w])
                bk = nb()
                mm_group(bk, bk.t[:, :], [(w.t[:, c, :], aT.t[:, c, :]) for c in range(NFF)], [w, aT])
                S.op(DVE, lambda n=n, bk=bk: V.scalar_tensor_tensor(out=xT.t[:, n, :], in0=bk.t[:, :], scalar=gt2(s, n), in1=xT.t[:, n, :], op0=ALU.mult, op1=ALU.add),
                     reads=[bk, xT, der], writes=[xT])
                S.op(ACT, lambda n=n: A.activation(out=sq.t[:, n, :], in_=xT.t[:, n, :], func=AF.Square), reads=[xT], writes=[sq])
        S.mark('F')
        ones_rstd(range(16), 2048, rA)
        for n in range(16):
            S.op(DVE, lambda n=n: V.scalar_tensor_tensor(out=xT.t[:, n, :], in0=xT.t[:, n, :], scalar=vecs.t[:, V_GFIN + n:V_GFIN + n + 1], in1=rA.t[:, :],
                                                         op0=ALU.mult, op1=ALU.mult), reads=[xT, rA, vecs], writes=[xT])
        for t in range(4):
            yt = ytok[t % 2]
            for g4 in range(4):
                bk = pbanks[pk[0] % 2]
                pk[0] += 1
                for i in range(4):
                    S.op(PE, lambda i=i, g4=g4, t=t, bk=bk: T.transpose(bk.t[:, i * 128:(i + 1) * 128], xT.t[:, g4 * 4 + i, t * 128:(t + 1) * 128], ident.t[:, :]),
                         reads=[xT, ident], writes=[bk] if i == 0 else [], inc=(i == 3))
                evac_copy(g4, yt.t[:, g4 * 512:(g4 + 1) * 512], bk.t[:, :], [bk], [yt])
            r0 = bb * 512 + t * 128
            S.dma(POOL, y_ds[t % 2], yout[s].t[r0:r0 + 128, :], yt.t[:, :], reads=[yt], writes=[yout[s]])
    S.barrier()
    S.mark('end')
    dbg_out['marks'] = S.marks
    return nc, dbg_out


def kernel(x_prompt, x_sample, c_prompt, c_sample, w_ada, b_ada, g_attn, w_in, rpb, g_q, w_uq, g_kv, w_ukv,
           g_out_na, g_out_mla, w_o, g_ffn, w_gate, w_up, w_down, g_final):
    inp = dict(x_prompt=x_prompt, x_sample=x_sample, c_prompt=c_prompt, c_sample=c_sample, w_ada=w_ada, b_ada=b_ada,
               g_attn=g_attn, w_in=w_in, rpb=rpb, g_q=g_q, w_uq=w_uq, g_kv=g_kv, w_ukv=w_ukv, g_out_na=g_out_na,
               g_out_mla=g_out_mla, w_o=w_o, g_ffn=g_ffn, w_gate=w_gate, w_up=w_up, w_down=w_down, g_final=g_final)
    maps = _prep(inp)
    nc, _ = build()
    res = run_bass_kernel_spmd(nc, maps, core_ids=list(range(NCORES)))
    yp = np.concatenate([np.asarray(r["y0"], np.float32) for r in res.results], axis=0)[None]
    ys = np.concatenate([np.asarray(r["y1"], np.float32) for r in res.results], axis=0)[None]
    return (yp, ys)
```
